# Optimizing a Trainium2 kernel written in Bass

```python
import jax, jax.numpy as jnp
from jax import lax
import numpy as np

D_MODEL = 1024
BATCH = 8
SEQ = 2048
DEPTH = 4

N_MIXERS = 3
N_SUB = 3
FFN_RES_WEIGHT = 0.5
D_FF = 2816
RMS_EPS = 1e-6
FOX_HEADS = 16
FOX_HEAD_DIM = D_MODEL // FOX_HEADS
FOX_BLOCK = 128
SCONV_WIDTH = 3
LRU_WIDTH = D_MODEL
LRU_BLOCKS = 16
LRU_BLOCK_DIM = LRU_WIDTH // LRU_BLOCKS
LRU_CONV_WIDTH = 4
LRU_C = 8.0
N_FOX = len(range(0, DEPTH, N_MIXERS))
N_SCONV = len(range(1, DEPTH, N_MIXERS))
N_LRU = len(range(2, DEPTH, N_MIXERS))

kernel_name = "hybrid_fox_shortconv_rglru_macaron"


def rmsnorm(x, g):
    x32 = x.astype(jnp.float32)
    y = x32 * lax.rsqrt(jnp.mean(x32 * x32, axis=-1, keepdims=True) + RMS_EPS)
    return y.astype(x.dtype) * g


def causal_depthwise_conv(u, w, b=None):
    k_w, ch = w.shape
    out = lax.conv_general_dilated(
        u, w[:, None, :].astype(u.dtype), window_strides=(1,),
        padding=[(k_w - 1, 0)], dimension_numbers=("NWC", "WIO", "NWC"),
        feature_group_count=ch)
    if b is not None:
        out = out + b
    return out


def swiglu(h, w_in, w_out):
    g, u = jnp.split(h @ w_in, 2, axis=-1)
    return (jax.nn.silu(g) * u) @ w_out


def fox_mixer(h, w_in, b_f, w_out):
    bsz, seq, _ = h.shape
    proj = h @ w_in
    q, k, v, f_logit = jnp.split(proj, [D_MODEL, 2 * D_MODEL, 3 * D_MODEL], axis=-1)
    q = q.reshape(bsz, seq, FOX_HEADS, FOX_HEAD_DIM)
    k = k.reshape(bsz, seq, FOX_HEADS, FOX_HEAD_DIM)
    v = v.reshape(bsz, seq, FOX_HEADS, FOX_HEAD_DIM)
    log_f = jax.nn.log_sigmoid((f_logit + b_f).astype(jnp.float32))
    cum = jnp.cumsum(log_f, axis=1).transpose(0, 2, 1)
    scale = FOX_HEAD_DIM ** -0.5
    outs = []
    for blk in range(seq // FOX_BLOCK):
        s0 = blk * FOX_BLOCK
        s1 = s0 + FOX_BLOCK
        logits = jnp.einsum("bqhd,bkhd->bhqk", q[:, s0:s1], k[:, :s1]).astype(jnp.float32) * scale
        logits = logits + cum[:, :, s0:s1, None] - cum[:, :, None, :s1]
        q_pos = jnp.arange(s0, s1)[:, None]
        k_pos = jnp.arange(s1)[None, :]
        logits = jnp.where(k_pos <= q_pos, logits, -jnp.inf)
        p = jax.nn.softmax(logits, axis=-1).astype(v.dtype)
        outs.append(jnp.einsum("bhqk,bkhd->bqhd", p, v[:, :s1]))
    o = jnp.concatenate(outs, axis=1).reshape(bsz, seq, D_MODEL)
    return o @ w_out


def sconv_mixer(h, w_in, conv_w, w_out):
    b_gate, c_gate, xv = jnp.split(h @ w_in, 3, axis=-1)
    y = b_gate * causal_depthwise_conv(c_gate * xv, conv_w)
    return y @ w_out


def lru_mixer(h, w_in, conv_w, conv_b, w_a, b_a, w_x, b_x, lam, w_out):
    bsz, seq, _ = h.shape
    gate, xb = jnp.split(h @ w_in, 2, axis=-1)
    xb = causal_depthwise_conv(xb, conv_w, conv_b)
    xh = xb.reshape(bsz, seq, LRU_BLOCKS, LRU_BLOCK_DIM)
    r = jax.nn.sigmoid(jnp.einsum("bsni,nij->bsnj", xh, w_a) + b_a).reshape(bsz, seq, LRU_WIDTH)
    i = jax.nn.sigmoid(jnp.einsum("bsni,nij->bsnj", xh, w_x) + b_x).reshape(bsz, seq, LRU_WIDTH)
    log_a = -LRU_C * r.astype(jnp.float32) * jax.nn.softplus(-lam.astype(jnp.float32))
    a = jnp.exp(log_a)
    mult = jnp.sqrt(-jnp.expm1(2.0 * log_a))
    b_term = mult * (i * xb).astype(jnp.float32)

    def combine(left, right):
        a1, b1 = left
        a2, b2 = right
        return a1 * a2, a2 * b1 + b2

    _, hs = lax.associative_scan(combine, (a, b_term), axis=1)
    y = hs.astype(h.dtype) * jax.nn.gelu(gate)
    return y @ w_out


def setup_inputs(seed: int = 0) -> dict:
    key = jax.random.key(seed)
    ks = jax.random.split(key, 24)
    f32 = jnp.float32

    def nrm(k, shape, fan_in):
        return jax.random.normal(k, shape, f32) * (fan_in ** -0.5)

    x = jax.random.normal(ks[0], (BATCH, SEQ, D_MODEL), f32)
    c = jax.random.normal(ks[1], (BATCH, D_MODEL), f32)
    w_cond = nrm(ks[2], (DEPTH, D_MODEL, N_SUB * 3 * D_MODEL), D_MODEL)
    b_cond = 0.02 * jax.random.normal(ks[3], (DEPTH, N_SUB * 3 * D_MODEL), f32)
    norm_pre = 1.0 + 0.05 * jax.random.normal(ks[4], (DEPTH, N_SUB, D_MODEL), f32)
    norm_post = 1.0 + 0.05 * jax.random.normal(ks[5], (DEPTH, N_SUB, D_MODEL), f32)
    w_ffn_in = nrm(ks[6], (DEPTH, 2, D_MODEL, 2 * D_FF), D_MODEL)
    w_ffn_out = nrm(ks[7], (DEPTH, 2, D_FF, D_MODEL), D_FF)
    fox_w_in = nrm(ks[8], (N_FOX, D_MODEL, 3 * D_MODEL + FOX_HEADS), D_MODEL)
    fox_b_f = jax.random.uniform(ks[9], (N_FOX, FOX_HEADS), f32, 1.0, 4.0)
    fox_w_out = nrm(ks[10], (N_FOX, D_MODEL, D_MODEL), D_MODEL)
    sconv_w_in = nrm(ks[11], (N_SCONV, D_MODEL, 3 * D_MODEL), D_MODEL)
    sconv_conv_w = nrm(ks[12], (N_SCONV, SCONV_WIDTH, D_MODEL), SCONV_WIDTH)
    sconv_w_out = nrm(ks[13], (N_SCONV, D_MODEL, D_MODEL), D_MODEL)
    lru_w_in = nrm(ks[14], (N_LRU, D_MODEL, 2 * LRU_WIDTH), D_MODEL)
    lru_conv_w = nrm(ks[15], (N_LRU, LRU_CONV_WIDTH, LRU_WIDTH), LRU_CONV_WIDTH)
    lru_conv_b = 0.02 * jax.random.normal(ks[16], (N_LRU, LRU_WIDTH), f32)
    lru_w_a = nrm(ks[17], (N_LRU, LRU_BLOCKS, LRU_BLOCK_DIM, LRU_BLOCK_DIM), LRU_BLOCK_DIM)
    lru_b_a = 0.02 * jax.random.normal(ks[18], (N_LRU, LRU_BLOCKS, LRU_BLOCK_DIM), f32)
    lru_w_x = nrm(ks[19], (N_LRU, LRU_BLOCKS, LRU_BLOCK_DIM, LRU_BLOCK_DIM), LRU_BLOCK_DIM)
    lru_b_x = 0.02 * jax.random.normal(ks[20], (N_LRU, LRU_BLOCKS, LRU_BLOCK_DIM), f32)
    a_c = jax.random.uniform(ks[21], (N_LRU, LRU_WIDTH), f32, 0.9, 0.999)
    s = a_c ** (1.0 / LRU_C)
    lru_lambda = jnp.log(s) - jnp.log1p(-s)
    lru_w_out = nrm(ks[22], (N_LRU, LRU_WIDTH, D_MODEL), LRU_WIDTH)
    return {
        "x": x, "c": c, "w_cond": w_cond, "b_cond": b_cond,
        "norm_pre": norm_pre, "norm_post": norm_post,
        "w_ffn_in": w_ffn_in, "w_ffn_out": w_ffn_out,
        "fox_w_in": fox_w_in, "fox_b_f": fox_b_f, "fox_w_out": fox_w_out,
        "sconv_w_in": sconv_w_in, "sconv_conv_w": sconv_conv_w, "sconv_w_out": sconv_w_out,
        "lru_w_in": lru_w_in, "lru_conv_w": lru_conv_w, "lru_conv_b": lru_conv_b,
        "lru_w_a": lru_w_a, "lru_b_a": lru_b_a, "lru_w_x": lru_w_x, "lru_b_x": lru_b_x,
        "lru_lambda": lru_lambda, "lru_w_out": lru_w_out,
    }


def reference(x, c, w_cond, b_cond, norm_pre, norm_post, w_ffn_in, w_ffn_out,
              fox_w_in, fox_b_f, fox_w_out, sconv_w_in, sconv_conv_w, sconv_w_out,
              lru_w_in, lru_conv_w, lru_conv_b, lru_w_a, lru_b_a, lru_w_x, lru_b_x,
              lru_lambda, lru_w_out):
    bsz = x.shape[0]
    c_act = jax.nn.silu(c)
    for i in range(DEPTH):
        mod = (c_act @ w_cond[i] + b_cond[i]).reshape(bsz, N_SUB, 3, D_MODEL)
        shift = mod[:, :, 0, None, :]
        scale = mod[:, :, 1, None, :]
        gate = mod[:, :, 2, None, :]

        def pre(h, s):
            return rmsnorm(h, norm_pre[i, s]) * (1.0 + scale[:, s]) + shift[:, s]

        y = swiglu(pre(x, 0), w_ffn_in[i, 0], w_ffn_out[i, 0])
        x = x + FFN_RES_WEIGHT * gate[:, 0] * rmsnorm(y, norm_post[i, 0])

        h = pre(x, 1)
        kind = i % N_MIXERS
        j = i // N_MIXERS
        if kind == 0:
            y = fox_mixer(h, fox_w_in[j], fox_b_f[j], fox_w_out[j])
        elif kind == 1:
            y = sconv_mixer(h, sconv_w_in[j], sconv_conv_w[j], sconv_w_out[j])
        else:
            y = lru_mixer(h, lru_w_in[j], lru_conv_w[j], lru_conv_b[j], lru_w_a[j], lru_b_a[j],
                          lru_w_x[j], lru_b_x[j], lru_lambda[j], lru_w_out[j])
        x = x + gate[:, 1] * rmsnorm(y, norm_post[i, 1])

        y = swiglu(pre(x, 2), w_ffn_in[i, 1], w_ffn_out[i, 1])
        x = x + FFN_RES_WEIGHT * gate[:, 2] * rmsnorm(y, norm_post[i, 2])
    return x
```

```python
import numpy as np
import concourse.bass as bass
import concourse.mybir as mybir
from concourse.bass_utils import run_bass_kernel_spmd

F32, BF16 = mybir.dt.float32, mybir.dt.bfloat16
AF = mybir.ActivationFunctionType
ALU = mybir.AluOpType

D = 1024; T = 2048; DFF = 2816; NL = 4; NCORES = 8
EPS = 1e-6
KIND = [0, 1, 2, 0]
MIXJ = [0, 0, 0, 1]
NS8 = 6
NS22 = 2

def _build_index():
    idx = {}; n = 0
    for i in range(NL):
        for j in range(72):
            idx[("cond", i, j)] = n; n += 1
    for i in range(NL):
        for f in range(2):
            for j in range(44):
                idx[("win", i, f, j)] = n; n += 1
    for i in range(NL):
        nch = {0: 24, 1: 24, 2: 16}[KIND[i]]
        for j in range(nch):
            idx[("mixin", i, j)] = n; n += 1
    for i in range(NL):
        for m in range(8):
            idx[("mixout", i, m)] = n; n += 1
    idx[("lrubd", 0)] = n; n += 1
    idx[("lrubd", 1)] = n; n += 1
    return idx, n
W8IDX, N8 = _build_index()

V_BCOND = 0; V_NPRE = V_BCOND + 288; V_NPOST = V_NPRE + 96; V_NBF = V_NPOST + 96
V_SCW = V_NBF + 2; V_LCW = V_SCW + 24; V_LCB = V_LCW + 32; V_LBA = V_LCB + 8; V_LBX = V_LBA + 8
V_LAM = V_LBX + 8; V_C = V_LAM + 8; V_ID = V_C + 8; V_MASK = V_ID + 16; NV = V_MASK + 128


class Sched:
    ENG = ("pe", "act", "dve", "pool", "sp")

    def __init__(self, dry=False):
        self.dry = dry
        self.ops = {e: [] for e in self.ENG}
        self.lastw = {}; self.rd = {}; self.rd_dma = {}
        self.dma_cnt = {}

    def op(self, eng, fn, reads=(), writes=(), dma=None):
        if self.dry:
            return None
        deps = set()
        for r in reads:
            w = self.lastw.get(r)
            if w is not None: deps.add(w)
        for r in writes:
            w = self.lastw.get(r)
            if w is not None: deps.add(w)
            for e, i in self.rd.get(r, {}).items(): deps.add((e, i))
            for d in self.rd_dma.get(r, ()): deps.add(d)
        idx = len(self.ops[eng])
        node = {"fn": fn, "deps": deps, "dma": dma, "sig": False, "val": None}
        if dma is not None:
            c = self.dma_cnt.get(dma, 0) + 1; self.dma_cnt[dma] = c; node["val"] = 16 * c
        self.ops[eng].append(node)
        me = (eng, idx)
        for r in reads:
            if dma is not None: self.rd_dma.setdefault(r, []).append(me)
            else: self.rd.setdefault(r, {})[eng] = idx
        for r in writes:
            self.lastw[r] = me; self.rd[r] = {}; self.rd_dma[r] = []
        return me

    def emit(self, nc, stack):
        ops = self.ops
        NEAR = 2
        def near(eng, idx, e, i):
            return e == eng and eng in ("act", "dve") and idx - i <= NEAR
        for eng in self.ENG:
            for idx, node in enumerate(ops[eng]):
                for (e, i) in node["deps"]:
                    d = ops[e][i]
                    if d["dma"] is None and (e != eng or near(eng, idx, e, i)): d["sig"] = True
        esem = {e: stack.enter_context(nc.semaphore("s_" + e)) for e in self.ENG}
        dsem = {}
        for k in self.dma_cnt:
            dsem[k] = stack.enter_context(nc.semaphore("d%d" % len(dsem)))
        for eng in self.ENG:
            c = 0
            for node in ops[eng]:
                if node["dma"] is None and node["sig"]:
                    c += 1; node["val"] = c
        final_waits = [(dsem[k], 16 * c) for k, c in self.dma_cnt.items() if isinstance(k, tuple) and k[0] == "out"]

        def run(eng, e):
            known = {}
            for idx, node in enumerate(ops[eng]):
                waits = {}
                for (de, di) in node["deps"]:
                    d = ops[de][di]
                    if d["dma"] is not None: sem, val = dsem[d["dma"]], d["val"]
                    elif de == eng and not near(eng, idx, de, di): continue
                    else: sem, val = esem[de], d["val"]
                    key = id(sem)
                    if known.get(key, 0) >= val: continue
                    if key not in waits or waits[key][1] < val: waits[key] = (sem, val)
                for key, (sem, val) in waits.items():
                    e.wait_ge(sem, val); known[key] = val
                ins = node["fn"](e)
                if node["dma"] is not None: ins.then_inc(dsem[node["dma"]], 16)
                elif node["sig"]: ins.then_inc(esem[eng], 1)
            if eng == "sp":
                for sem, val in final_waits: e.wait_ge(sem, val)

        block = stack.enter_context(nc.Block())
        block.tensor(lambda e: run("pe", e))
        block.scalar(lambda e: run("act", e))
        block.vector(lambda e: run("dve", e))
        block.gpsimd(lambda e: run("pool", e))
        block.sync(lambda e: run("sp", e))


class WRing:
    def __init__(self, S, name, views, fetch, seq, rec, hold=1):
        self.S, self.name, self.views, self.fetch, self.seq, self.rec = S, name, views, fetch, seq, rec
        self.pos = 0; self.issued = 0; self.hold = hold

    def get(self, key):
        if self.S.dry:
            self.rec.append(key)
            return self.views[0], (self.name, 0)
        assert self.seq[self.pos] == key, (self.seq[self.pos], key)
        ns = len(self.views)
        while self.issued < min(len(self.seq), self.pos + ns - (self.hold - 1)):
            k = self.issued; sl = k % ns
            src = self.fetch(self.seq[k]); dst = self.views[sl]
            self.S.op("pool", lambda e, dst=dst, src=src: e.dma_start(out=dst, in_=src),
                      writes=[(self.name, sl)], dma=(self.name, sl))
            self.issued += 1
        sl = self.pos % ns; self.pos += 1
        return self.views[sl], (self.name, sl)


def mmg(out, pairs):
    def fn(e):
        n = len(pairs); ins = None
        for k, (l, r) in enumerate(pairs):
            ins = e.matmul(out, lhsT=l, rhs=r, start=(k == 0), stop=(k == n - 1))
        return ins
    return fn


def build_program(n_sub=12, debug=False):
    import contextlib
    dram = {}
    row8 = {}; row22 = {}
    nc = bass.Bass("TRN2", target_bir_lowering=False)
    xin = nc.dram_tensor("xin", [128, 8, T], F32, kind="ExternalInput").ap()
    vecs_d = nc.dram_tensor("vecs", [128, NV], F32, kind="ExternalInput").ap()
    wf_d = nc.dram_tensor("wf", [2, 128, 128], F32, kind="ExternalInput").ap()
    sel_d = nc.dram_tensor("sel", [16, 2048], F32, kind="ExternalInput").ap()
    outT = nc.dram_tensor("outT", [128, 8, T], F32, kind="ExternalOutput").ap()

    stack = contextlib.ExitStack()
    with stack:
        def sb(name, shape, dt):
            return stack.enter_context(nc.sbuf_tensor(name, shape, dt))
        xT = sb("xT", [128, 8, T], F32)
        arena = sb("arena", [128, 24064], F32)
        w8buf = sb("w8buf", [128, NS8 * 1024], BF16)
        w22buf = sb("w22buf", [128, NS22 * DFF], BF16)
        sqb = sb("sqb", [128, 2048], F32)
        rsb = sb("rsb", [128, 512], F32)
        tmpb = [sb("tmp%d" % k, [128, 512], F32) for k in range(2)]
        sgb = [sb("sg%d" % k, [128, 512], F32) for k in range(2)]
        vecs = sb("vecs_sb", [128, NV], F32)
        modsb = sb("modsb", [128, 72], F32)
        Asb = sb("Asb", [128, 24], F32)
        GGsb = sb("GGsb", [128, 24], F32)
        cactb = sb("cactb", [128, 8], BF16)
        onesd = sb("onesd", [128, 128], BF16)
        cst = sb("cst", [128, 4], F32)
        wfb = sb("wfb", [128, 128], BF16)
        sp8 = sb("sp8", [128, 8], F32)
        hl = sb("hl", [128, 1], F32)
        nbf = sb("nbf", [128, 2], F32)
        psA = [stack.enter_context(nc.psum_tensor("psA%d" % k, [128, 512], F32)) for k in range(2)]
        psB = [stack.enter_context(nc.psum_tensor("psB%d" % k, [128, 512], F32)) for k in range(2)]
        psY = [stack.enter_context(nc.psum_tensor("psY%d" % k, [128, 512], F32)) for k in range(2)]
        psS = stack.enter_context(nc.psum_tensor("psS", [128, 512], F32))
        psM = stack.enter_context(nc.psum_tensor("psM", [128, 512], F32))

        def abf(off_b, n_el):
            return arena[:, off_b // 4: off_b // 4 + n_el // 2].bitcast(BF16)
        def af32(off_b, n_el):
            return arena[:, off_b // 4: off_b // 4 + n_el]
        hTh = abf(0, 8 * 1024).rearrange("p (c t) -> p c t", t=1024)
        actT = abf(16384, 22 * 1024).rearrange("p (c t) -> p c t", t=1024)
        yTf = af32(61440, 8 * 1024).rearrange("p (c t) -> p c t", t=1024)
        hT = abf(0, 8 * T).rearrange("p (c t) -> p c t", t=T)
        yTm = af32(0, 8 * 1024).rearrange("p (c t) -> p c t", t=1024)
        ymix = abf(32768, 8 * T).rearrange("p (c t) -> p c t", t=T)
        WK = 65536
        sq = sqb[:, :].bitcast(BF16).rearrange("p (c t) -> p c t", t=512)

        w8views = [w8buf[:, k * 1024:(k + 1) * 1024] for k in range(NS8)]
        w22views = [w22buf[:, k * DFF:(k + 1) * DFF] for k in range(NS22)]

        def norm8(key):
            return key[:3] if key[0] == "mixout" else key
        def fetch8(key):
            return dram["w8"][row8[norm8(key)]]
        def fetch22(key):
            return dram["w22"][row22[key]]

        def v3(view, kc=8):
            return view.rearrange("p (k c) -> p k c", c=128)

        H_ALL = [("h", b) for b in range(4)]

        def gen(S, W8, W22):
            state = {"phase_first": False, "bctr": 0}

            def AR(reads=(), writes=()):
                reads = list(reads); writes = list(writes)
                if state["phase_first"]:
                    writes.append("arena"); state["phase_first"] = False
                else:
                    reads.append("arena")
                return dict(reads=reads, writes=writes)

            def nb():
                state["bctr"] += 1
                return state["bctr"] % 2

            S.op("sp", lambda e: e.dma_start(out=vecs[:], in_=vecs_d), writes=["vecs"], dma=("vl", 0))
            for b in range(4):
                S.op("sp", lambda e, b=b: e.dma_start(out=xT[:, :, b * 512:(b + 1) * 512], in_=xin[:, :, b * 512:(b + 1) * 512]),
                     writes=[("x", b)], dma=("xl", b))
            S.op("dve", lambda e: e.memset(onesd[:], 1.0 / 1024.0), writes=["onesd"])
            S.op("dve", lambda e: e.memset(cst[:, 0:1], EPS), writes=["cst"])
            S.op("dve", lambda e: e.memset(cst[:, 1:2], 1.0), writes=["cst"])
            S.op("dve", lambda e: e.memset(cst[:, 2:3], 0.0), writes=["cst"])
            S.op("act", lambda e: e.activation(out=cactb[:], in_=vecs[:, V_C:V_C + 8], func=AF.Silu),
                 reads=["vecs"], writes=["cact"])
            S.op("dve", lambda e: e.tensor_scalar(out=nbf[0:16, :], in0=vecs[0:16, V_NBF:V_NBF + 2], scalar1=-1.0, scalar2=None, op0=ALU.mult),
                 reads=["vecs"], writes=["nbf"])

            def Acol(s, c): return Asb[:, s * 8 + c: s * 8 + c + 1]
            def SHcol(s, c): return modsb[:, s * 24 + c: s * 24 + c + 1]
            def GGcol(s, c): return GGsb[:, s * 8 + c: s * 8 + c + 1]

            def mod_phase(i):
                for j in range(72):
                    wv, wr = W8.get(("cond", i, j))
                    w3 = v3(wv)
                    S.op("pe", mmg(psM[:, j:j + 1], [(w3[:, kc, :], cactb[:, kc:kc + 1]) for kc in range(8)]),
                         reads=[wr, "cact"], writes=["psM"])
                S.op("dve", lambda e: e.tensor_tensor(out=modsb[:], in0=psM[:, 0:72], in1=vecs[:, V_BCOND + i * 72: V_BCOND + (i + 1) * 72], op=ALU.add),
                     reads=["psM", "vecs"], writes=["modA"])
                for s in range(3):
                    wsub = 1.0 if s == 1 else 0.5
                    o = (i * 3 + s) * 8
                    S.op("dve", lambda e, s=s, o=o: e.scalar_tensor_tensor(
                        out=Asb[:, s * 8:(s + 1) * 8], in0=modsb[:, s * 24 + 8: s * 24 + 16], scalar=1.0,
                        in1=vecs[:, V_NPRE + o: V_NPRE + o + 8], op0=ALU.add, op1=ALU.mult),
                        reads=["vecs"], writes=["modA"])
                    S.op("dve", lambda e, s=s, o=o, wsub=wsub: e.scalar_tensor_tensor(
                        out=GGsb[:, s * 8:(s + 1) * 8], in0=modsb[:, s * 24 + 16: s * 24 + 24], scalar=wsub,
                        in1=vecs[:, V_NPOST + o: V_NPOST + o + 8], op0=ALU.mult, op1=ALU.mult),
                        reads=["vecs"], writes=["modA"])

            def rstd_from_sq(src_reads):
                S.op("pe", mmg(psS[:], [(onesd[:], sq[:, c, :]) for c in range(8)]), reads=["sq", "onesd"], writes=["psS"])
                S.op("act", lambda e: e.activation(out=rsb[:], in_=psS[:], func=AF.Sqrt, bias=cst[:, 0:1], scale=1.0),
                     reads=["psS", "cst"], writes=["rs"])
                S.op("dve", lambda e: e.reciprocal(out=rsb[:], in_=rsb[:]), reads=["rs"], writes=["rs"])

            def prenorm(s, t0, ntok, dst, hres):
                for bi in range(ntok // 512):
                    tb = t0 + bi * 512; xb = tb // 512
                    S.op("act", lambda e, tb=tb: e.activation(out=sq, in_=xT[:, :, tb:tb + 512], func=AF.Square),
                         reads=[("x", xb)], writes=["sq"])
                    rstd_from_sq(None)
                    for c in range(8):
                        k = c % 2
                        S.op("dve", lambda e, c=c, k=k, tb=tb: e.scalar_tensor_tensor(
                            out=tmpb[k][:], in0=xT[:, c, tb:tb + 512], scalar=Acol(s, c), in1=rsb[:], op0=ALU.mult, op1=ALU.mult),
                            reads=[("x", xb), "rs", "modA"], writes=[("tmp", k)])
                        S.op("act", lambda e, c=c, k=k, bi=bi: e.activation(
                            out=dst[:, c, bi * 512:(bi + 1) * 512], in_=tmpb[k][:], func=AF.Identity, bias=SHcol(s, c), scale=1.0),
                            **AR(reads=[("tmp", k), "modA"], writes=[hres(bi)]))

            def postnorm(s, t0, ysrc):
                for tt in range(2):
                    tb = t0 + tt * 512; xb = tb // 512
                    S.op("act", lambda e, tt=tt: e.activation(out=sq, in_=ysrc[:, :, tt * 512:(tt + 1) * 512], func=AF.Square),
                         **AR(reads=[("y", tt)], writes=["sq"]))
                    rstd_from_sq(None)
                    for m in range(8):
                        k = m % 2
                        S.op("dve", lambda e, m=m, k=k, tt=tt: e.scalar_tensor_tensor(
                            out=tmpb[k][:], in0=ysrc[:, m, tt * 512:(tt + 1) * 512], scalar=GGcol(s, m), in1=rsb[:], op0=ALU.mult, op1=ALU.mult),
                            **AR(reads=[("y", tt), "rs", "modA"], writes=[("tmp", k)]))
                        S.op("dve", lambda e, m=m, k=k, tb=tb: e.tensor_tensor(
                            out=xT[:, m, tb:tb + 512], in0=xT[:, m, tb:tb + 512], in1=tmpb[k][:], op=ALU.add),
                            reads=[("tmp", k), ("x", xb)], writes=[("x", xb)])

            def ffn(i, f, s):
                state["phase_first"] = True
                for half in range(2):
                    t0 = half * 1024
                    prenorm(s, t0, 1024, hTh, lambda bi: ("ah", bi))
                    for n in range(22):
                        wg, rg = W8.get(("win", i, f, n)); wu, ru = W8.get(("win", i, f, 22 + n))
                        wg3, wu3 = v3(wg), v3(wu)
                        for tt in range(2):
                            b = nb(); ts = slice(tt * 512, (tt + 1) * 512)
                            S.op("pe", mmg(psA[b][:], [(wg3[:, kc, :], hTh[:, kc, ts]) for kc in range(8)]),
                                 **AR(reads=[rg, ("ah", tt)], writes=[("psA", b)]))
                            S.op("pe", mmg(psB[b][:], [(wu3[:, kc, :], hTh[:, kc, ts]) for kc in range(8)]),
                                 **AR(reads=[ru, ("ah", tt)], writes=[("psB", b)]))
                            S.op("act", lambda e, b=b: e.activation(out=sgb[b][:], in_=psA[b][:], func=AF.Silu),
                                 reads=[("psA", b)], writes=[("sg", b)])
                            S.op("dve", lambda e, b=b, n=n, ts=ts: e.tensor_tensor(out=actT[:, n, ts], in0=sgb[b][:], in1=psB[b][:], op=ALU.mult),
                                 **AR(reads=[("sg", b), ("psB", b)], writes=[("act", tt)]))
                    for m in range(8):
                        wo, ro = W22.get(("wout", i, f, m)); wo3 = v3(wo)
                        for tt in range(2):
                            b = nb(); ts = slice(tt * 512, (tt + 1) * 512)
                            S.op("pe", mmg(psY[b][:], [(wo3[:, kc, :], actT[:, kc, ts]) for kc in range(22)]),
                                 **AR(reads=[ro, ("act", tt)], writes=[("psY", b)]))
                            S.op("act", lambda e, b=b, m=m, ts=ts: e.activation(out=yTf[:, m, ts], in_=psY[b][:], func=AF.Copy),
                                 **AR(reads=[("psY", b)], writes=[("y", tt)]))
                    postnorm(s, t0, yTf)

            def mix_out(i):
                for half in range(2):
                    for m in range(8):
                        wo, ro = W8.get(("mixout", i, m, half)); wo3 = v3(wo)
                        for tt in range(2):
                            b = nb(); tok = half * 1024 + tt * 512
                            S.op("pe", mmg(psY[b][:], [(wo3[:, kc, :], ymix[:, kc, tok:tok + 512]) for kc in range(8)]),
                                 **AR(reads=[ro, ("ymix", tok // 512)], writes=[("psY", b)]))
                            S.op("act", lambda e, b=b, m=m, tt=tt: e.activation(out=yTm[:, m, tt * 512:(tt + 1) * 512], in_=psY[b][:], func=AF.Copy),
                                 **AR(reads=[("psY", b)], writes=[("y", tt)] + H_ALL))
                    postnorm(1, half * 1024, yTm)

            def proj512(w3, tt, ps, wr, pres):
                S.op("pe", mmg(ps[:], [(w3[:, kc, :], hT[:, kc, tt * 512:(tt + 1) * 512]) for kc in range(8)]),
                     **AR(reads=[wr, ("h", tt)], writes=[pres]))

            def sconv(i):
                state["phase_first"] = True
                prenorm(1, 0, T, hT, lambda bi: ("h", bi))
                cx = af32(WK, 2064)[:, 0:2050]
                cv = af32(WK + 8256, 2048)
                S.op("dve", lambda e: e.memset(cx[:, 0:2], 0.0), **AR(writes=["cx"]))
                for m in range(8):
                    wB, rB = W8.get(("mixin", i, m)); wC, rC = W8.get(("mixin", i, 8 + m)); wX, rX = W8.get(("mixin", i, 16 + m))
                    for tt in range(4):
                        b = nb()
                        proj512(v3(wC), tt, psA[b], rC, ("psA", b))
                        proj512(v3(wX), tt, psB[b], rX, ("psB", b))
                        S.op("act", lambda e, b=b: e.activation(out=sgb[b][:], in_=psA[b][:], func=AF.Copy),
                             reads=[("psA", b)], writes=[("sg", b)])
                        S.op("dve", lambda e, b=b, tt=tt: e.tensor_tensor(out=cx[:, 2 + tt * 512: 2 + (tt + 1) * 512], in0=sgb[b][:], in1=psB[b][:], op=ALU.mult),
                             **AR(reads=[("sg", b), ("psB", b)], writes=["cx"]))
                    wc = lambda k, m=m: vecs[:, V_SCW + m * 3 + k: V_SCW + m * 3 + k + 1]
                    S.op("dve", lambda e, wc=wc: e.tensor_scalar(out=cv[:], in0=cx[:, 2:2050], scalar1=wc(2), scalar2=None, op0=ALU.mult),
                         **AR(reads=["cx", "vecs"], writes=["cv"]))
                    for k in (1, 0):
                        S.op("dve", lambda e, wc=wc, k=k: e.scalar_tensor_tensor(out=cv[:], in0=cx[:, k:k + 2048], scalar=wc(k), in1=cv[:], op0=ALU.mult, op1=ALU.add),
                             **AR(reads=["cx", "vecs"], writes=["cv"]))
                    for tt in range(4):
                        b = nb()
                        proj512(v3(wB), tt, psA[b], rB, ("psA", b))
                        S.op("dve", lambda e, b=b, tt=tt, m=m: e.tensor_tensor(out=ymix[:, m, tt * 512:(tt + 1) * 512], in0=psA[b][:], in1=cv[:, tt * 512:(tt + 1) * 512], op=ALU.mult),
                             **AR(reads=[("psA", b), "cv"], writes=[("ymix", tt)]))
                mix_out(i)

            def lru(i):
                state["phase_first"] = True
                prenorm(1, 0, T, hT, lambda bi: ("h", bi))
                if S.dry: extra8.extend([("lrubd", 0), ("lrubd", 1)])
                bd = sqb[:, 0:1024].bitcast(BF16).rearrange("p (w m c) -> p w m c", w=2, m=8)
                bdf = sqb[:, 0:1024].bitcast(BF16)
                for w in range(2):
                    S.op("pool", lambda e, w=w: e.dma_start(out=bdf[:, w * 1024:(w + 1) * 1024], in_=dram["w8"][row8[("lrubd", w)]]),
                         writes=["sq"], dma=("bd", w))
                S.op("act", lambda e: e.activation(out=sp8[:], in_=vecs[:, V_LAM:V_LAM + 8], func=AF.Exp, scale=-1.0), reads=["vecs"], writes=["sp8"])
                S.op("act", lambda e: e.activation(out=sp8[:], in_=sp8[:], func=AF.Ln, bias=cst[:, 1:2], scale=1.0), reads=["cst"], writes=["sp8"])
                S.op("dve", lambda e: e.tensor_scalar(out=sp8[:], in0=sp8[:], scalar1=-8.0, scalar2=None, op0=ALU.mult), reads=["sp8"], writes=["sp8"])
                gl = af32(WK, 1024); xraw = af32(WK + 4096, 1028); xbb = af32(WK + 8208, 1024)
                xbf = abf(WK + 12304, 1024); ab = af32(WK + 14352, 1024); ig = af32(WK + 18448, 1024); tp = af32(WK + 22544, 1024)
                for m in range(8):
                    wG, rG = W8.get(("mixin", i, m)); wXb, rXb = W8.get(("mixin", i, 8 + m))
                    cwc = lambda k, m=m: vecs[:, V_LCW + m * 4 + k: V_LCW + m * 4 + k + 1]
                    for seg in range(2):
                        if seg == 0:
                            S.op("dve", lambda e: e.memset(xraw[:, 0:3], 0.0), **AR(writes=["xraw"]))
                        else:
                            S.op("dve", lambda e: e.tensor_copy(out=xraw[:, 0:3], in_=xraw[:, 1024:1027]), **AR(reads=["xraw"], writes=["xraw"]))
                        for tt in range(2):
                            b = nb(); blk = seg * 2 + tt; ts = slice(tt * 512, (tt + 1) * 512)
                            proj512(v3(wG), blk, psA[b], rG, ("psA", b))
                            S.op("act", lambda e, b=b: e.activation(out=sgb[b][:], in_=psA[b][:], func=AF.Square), reads=[("psA", b)], writes=[("sg", b)])
                            S.op("dve", lambda e, b=b: e.tensor_scalar(out=sgb[b][:], in0=sgb[b][:], scalar1=0.044715, scalar2=1.0, op0=ALU.mult, op1=ALU.add),
                                 reads=[("sg", b)], writes=[("sg", b)])
                            S.op("dve", lambda e, b=b: e.tensor_tensor(out=sgb[b][:], in0=sgb[b][:], in1=psA[b][:], op=ALU.mult),
                                 reads=[("sg", b), ("psA", b)], writes=[("sg", b)])
                            S.op("act", lambda e, b=b: e.activation(out=sgb[b][:], in_=sgb[b][:], func=AF.Sigmoid, scale=1.5957691216057308),
                                 reads=[("sg", b)], writes=[("sg", b)])
                            S.op("dve", lambda e, b=b, ts=ts: e.tensor_tensor(out=gl[:, ts], in0=sgb[b][:], in1=psA[b][:], op=ALU.mult),
                                 **AR(reads=[("sg", b), ("psA", b)], writes=["gl"]))
                            proj512(v3(wXb), blk, psB[b], rXb, ("psB", b))
                            S.op("act", lambda e, b=b, tt=tt: e.activation(out=xraw[:, 3 + tt * 512: 3 + (tt + 1) * 512], in_=psB[b][:], func=AF.Copy),
                                 **AR(reads=[("psB", b)], writes=["xraw"]))
                        S.op("dve", lambda e, cwc=cwc, m=m: e.tensor_scalar(out=xbb[:], in0=xraw[:, 3:1027], scalar1=cwc(3), scalar2=vecs[:, V_LCB + m: V_LCB + m + 1], op0=ALU.mult, op1=ALU.add),
                             **AR(reads=["xraw", "vecs"], writes=["xb"]))
                        for k in range(3):
                            S.op("dve", lambda e, cwc=cwc, k=k: e.scalar_tensor_tensor(out=xbb[:], in0=xraw[:, k:k + 1024], scalar=cwc(k), in1=xbb[:], op0=ALU.mult, op1=ALU.add),
                                 **AR(reads=["xraw", "vecs"], writes=["xb"]))
                        S.op("act", lambda e: e.activation(out=xbf[:], in_=xbb[:], func=AF.Copy), **AR(reads=["xb"], writes=["xbf"]))
                        for tt in range(2):
                            b = nb(); ts = slice(tt * 512, (tt + 1) * 512)
                            S.op("pe", mmg(psA[b][:], [(bd[:, 0, m, :], xbf[:, ts])]), **AR(reads=["sq", "xbf"], writes=[("psA", b)]))
                            S.op("act", lambda e, b=b, m=m: e.activation(out=sgb[b][:], in_=psA[b][:], func=AF.Sigmoid, bias=vecs[:, V_LBA + m: V_LBA + m + 1], scale=1.0),
                                 reads=[("psA", b), "vecs"], writes=[("sg", b)])
                            S.op("dve", lambda e, b=b, m=m: e.tensor_scalar(out=sgb[b][:], in0=sgb[b][:], scalar1=sp8[:, m:m + 1], scalar2=None, op0=ALU.mult),
                                 reads=[("sg", b), "sp8"], writes=[("sg", b)])
                            S.op("act", lambda e, b=b, m=m, ts=ts: e.activation(out=ab[:, ts], in_=sgb[b][:], func=AF.Exp),
                                 **AR(reads=[("sg", b)], writes=["ab"]))
                            S.op("pe", mmg(psB[b][:], [(bd[:, 1, m, :], xbf[:, ts])]), **AR(reads=["sq", "xbf"], writes=[("psB", b)]))
                            S.op("act", lambda e, b=b, m=m, ts=ts: e.activation(out=ig[:, ts], in_=psB[b][:], func=AF.Sigmoid, bias=vecs[:, V_LBX + m: V_LBX + m + 1], scale=1.0),
                                 **AR(reads=[("psB", b), "vecs"], writes=["ig"]))
                        S.op("dve", lambda e: e.tensor_tensor(out=tp[:], in0=ab[:], in1=ab[:], op=ALU.mult), **AR(reads=["ab"], writes=["tp"]))
                        S.op("dve", lambda e: e.tensor_scalar(out=tp[:], in0=tp[:], scalar1=-1.0, scalar2=1.0, op0=ALU.mult, op1=ALU.add), **AR(writes=["tp"]))
                        S.op("act", lambda e: e.activation(out=tp[:], in_=tp[:], func=AF.Sqrt), **AR(reads=["tp"], writes=["tp"]))
                        S.op("dve", lambda e: e.tensor_tensor(out=ig[:], in0=ig[:], in1=xbb[:], op=ALU.mult), **AR(reads=["ig", "xb"], writes=["ig"]))
                        S.op("dve", lambda e: e.tensor_tensor(out=ig[:], in0=ig[:], in1=tp[:], op=ALU.mult), **AR(reads=["tp"], writes=["ig"]))
                        init = 0.0 if seg == 0 else hl[:, 0:1]
                        S.op("dve", lambda e, init=init: e.tensor_tensor_scan(out=tp[:], data0=ab[:], data1=ig[:], initial=init, op0=ALU.mult, op1=ALU.add),
                             **AR(reads=["ab", "ig", "hl"], writes=["tp"]))
                        S.op("act", lambda e: e.activation(out=hl[:, 0:1], in_=tp[:, 1023:1024], func=AF.Copy), **AR(reads=["tp"], writes=["hl"]))
                        S.op("dve", lambda e, m=m, seg=seg: e.tensor_tensor(out=ymix[:, m, seg * 1024:(seg + 1) * 1024], in0=tp[:], in1=gl[:], op=ALU.mult),
                             **AR(reads=["tp", "gl"], writes=[("ymix", seg * 2), ("ymix", seg * 2 + 1)]))
                mix_out(i)

            def fox(i):
                j = MIXJ[i]
                state["phase_first"] = True
                prenorm(1, 0, T, hT, lambda bi: ("h", bi))
                selv = sqb[0:16, :]
                S.op("sp", lambda e: e.dma_start(out=selv, in_=sel_d), writes=["sq"], dma=("sel", 0))
                S.op("pool", lambda e: e.dma_start(out=wfb[:], in_=wf_d[j]), writes=["wfb"], dma=("wfb", 0))
                wfb3 = wfb[:, :].rearrange("p (k c) -> p k c", c=16)
                cum8 = af32(WK, 2048)[0:16, :]
                ncum = af32(WK + 8192, 256)
                QT = abf(WK + 9216, 2048); KT = abf(WK + 13312, 2048)
                Vx = abf(WK + 17408, 4096).rearrange("p (t h c) -> p t h c", t=16, h=2)
                Lb = af32(WK + 9216, 2048)[0:16, :]; Zb = af32(WK + 17408, 2048)[0:16, :]
                S.op("dve", lambda e: e.memset(Zb, 0.0), **AR(writes=["Vx"]))
                for tt in range(4):
                    ts = slice(tt * 512, (tt + 1) * 512)
                    S.op("pe", mmg(psM[0:16, :], [(wfb3[:, kc, :], hT[:, kc, ts]) for kc in range(8)]), **AR(reads=["wfb", ("h", tt)], writes=["psM"]))
                    S.op("act", lambda e, ts=ts: e.activation(out=Lb[:, ts], in_=psM[0:16, :], func=AF.Exp, bias=nbf[0:16, j:j + 1], scale=-1.0),
                         **AR(reads=["psM", "nbf"], writes=["QT", "KT"]))
                    S.op("act", lambda e, ts=ts: e.activation(out=Lb[:, ts], in_=Lb[:, ts], func=AF.Ln, bias=cst[0:16, 1:2], scale=1.0),
                         **AR(reads=["cst"], writes=["QT", "KT"]))
                S.op("dve", lambda e: e.tensor_scalar(out=Lb, in0=Lb, scalar1=-8.0, scalar2=None, op0=ALU.mult), **AR(reads=["QT", "KT"], writes=["QT", "KT"]))
                S.op("dve", lambda e: e.tensor_tensor_scan(out=cum8, data0=Lb, data1=Zb, initial=0.0, op0=ALU.add, op1=ALU.add),
                     **AR(reads=["QT", "KT", "Vx"], writes=["cum8"]))
                for tile in range(16):
                    S.op("pe", mmg(psM[:, tile * 16:(tile + 1) * 16], [(cum8[:, tile * 128:(tile + 1) * 128], vecs[0:16, V_ID:V_ID + 16])]),
                         **AR(reads=["cum8", "vecs"], writes=["psM"]))
                S.op("dve", lambda e: e.tensor_scalar(out=ncum[:], in0=psM[:, 0:256], scalar1=-0.125, scalar2=None, op0=ALU.mult), **AR(reads=["psM"], writes=["ncum"]))
                S.op("dve", lambda e: e.memset(Vx[:, :, :, 64:128], 1.0), **AR(writes=["Vx"]))
                mask = vecs[:, V_MASK:V_MASK + 128]
                PT = [sgb[k][:, 0:256].bitcast(BF16) for k in range(2)]
                for m in range(8):
                    wq, rq = W8.get(("mixin", i, m)); wk, rk = W8.get(("mixin", i, 8 + m)); wv, rv = W8.get(("mixin", i, 16 + m))
                    wv3 = v3(wv)
                    for tt in range(4):
                        b = nb(); ts = slice(tt * 512, (tt + 1) * 512)
                        proj512(v3(wq), tt, psA[b], rq, ("psA", b))
                        S.op("act", lambda e, b=b, ts=ts: e.activation(out=QT[:, ts], in_=psA[b][:], func=AF.Copy), **AR(reads=[("psA", b)], writes=["QT"]))
                        proj512(v3(wk), tt, psB[b], rk, ("psB", b))
                        S.op("act", lambda e, b=b, ts=ts: e.activation(out=KT[:, ts], in_=psB[b][:], func=AF.Copy), **AR(reads=[("psB", b)], writes=["KT"]))
                    for g in range(4):
                        b = nb()
                        for q in range(4):
                            tile = g * 4 + q
                            S.op("pe", mmg(psB[b][:, q * 128:(q + 1) * 128], [(hT[:, kc, tile * 128:(tile + 1) * 128], wv3[:, kc, :]) for kc in range(8)]),
                                 **AR(reads=[rv, ("h", g)], writes=[("psB", b)]))
                        pv = psB[b][:, :].rearrange("p (q c) -> p q c", c=128)
                        for hh in range(2):
                            S.op("dve", lambda e, g=g, hh=hh, pv=pv: e.tensor_copy(out=Vx[:, g * 4:(g + 1) * 4, hh, 0:64], in_=pv[:, :, hh * 64:(hh + 1) * 64]),
                                 **AR(reads=[("psB", b)], writes=["Vx"]))
                    for hh in range(2):
                        h = 2 * m + hh; hs = slice(hh * 64, (hh + 1) * 64)
                        for c in range(4):
                            kcb = c % 2; yb = c % 2
                            S.op("pe", mmg(psS[:], [(selv[:, h * 128:(h + 1) * 128], cum8[:, c * 512:(c + 1) * 512])]), **AR(reads=["sq", "cum8"], writes=["psS"]))
                            S.op("act", lambda e, kcb=kcb: e.activation(out=tmpb[kcb][:], in_=psS[:], func=AF.Copy), reads=["psS"], writes=[("tmp", kcb)])
                            nj = 4 * (c + 1)
                            for jt in range(nj):
                                b = nb(); n0 = max(0, jt * 128 - c * 512)
                                S.op("pe", mmg(psA[b][:, n0:512], [(KT[hs, jt * 128:(jt + 1) * 128], QT[hs, c * 512 + n0:(c + 1) * 512])]),
                                     **AR(reads=["QT", "KT"], writes=[("psA", b)]))
                                S.op("dve", lambda e, b=b, n0=n0, kcb=kcb: e.tensor_tensor(out=psA[b][:, n0:512], in0=psA[b][:, n0:512], in1=tmpb[kcb][:, n0:512], op=ALU.add),
                                     reads=[("psA", b), ("tmp", kcb)], writes=[("psA", b)])
                                if jt >= 4 * c:
                                    S.op("dve", lambda e, b=b, n0=n0: e.tensor_tensor(out=psA[b][:, n0:n0 + 128], in0=psA[b][:, n0:n0 + 128], in1=mask, op=ALU.add),
                                         reads=[("psA", b), "vecs"], writes=[("psA", b)])
                                S.op("act", lambda e, b=b, n0=n0, jt=jt, h=h: e.activation(out=PT[b][:, n0:512], in_=psA[b][:, n0:512], func=AF.Exp,
                                                                                           bias=ncum[:, jt * 16 + h: jt * 16 + h + 1], scale=0.125),
                                     **AR(reads=[("psA", b), "ncum"], writes=[("sg", b)]))
                                S.op("pe", (lambda b=b, n0=n0, jt=jt, hh=hh, yb=yb, nj=nj: (lambda e: e.matmul(
                                    psY[yb][:, n0:512], lhsT=Vx[:, jt, hh, :], rhs=PT[b][:, n0:512], start=(jt == 0), stop=(jt == nj - 1))))(),
                                     **AR(reads=[("sg", b), "Vx"], writes=[("psY", yb)]))
                            S.op("dve", lambda e, yb=yb: e.reciprocal(out=rsb[64:128, :], in_=psY[yb][64:128, :]), reads=[("psY", yb)], writes=["rs"])
                            S.op("dve", lambda e, yb=yb, hs=hs, m=m, c=c: e.tensor_tensor(out=ymix[hs, m, c * 512:(c + 1) * 512], in0=psY[yb][0:64, :], in1=rsb[64:128, :], op=ALU.mult),
                                 **AR(reads=[("psY", yb), "rs"], writes=[("ymix", c)]))
                mix_out(i)

            nsub = 0
            for i in range(NL):
                if nsub >= n_sub: break
                mod_phase(i)
                for s in range(3):
                    if nsub >= n_sub: break
                    if s == 0: ffn(i, 0, 0)
                    elif s == 2: ffn(i, 1, 2)
                    else: (fox, sconv, lru)[KIND[i]](i)
                    nsub += 1
            for b in range(4):
                S.op("sp", lambda e, b=b: e.dma_start(out=outT[:, :, b * 512:(b + 1) * 512], in_=xT[:, :, b * 512:(b + 1) * 512]),
                     reads=[("x", b)], dma=("out", b))
            if getattr(S, "debug", False):
                dg = af32(WK, 4096)
                allres = list(S.lastw.keys())
                S.op("dve", lambda e: e.memset(dg[:], 0.0), reads=allres, writes=["dbg"])
                S.op("dve", lambda e: e.tensor_copy(out=dg[:, 0:72], in_=modsb[:]), writes=["dbg"])
                S.op("dve", lambda e: e.tensor_copy(out=dg[:, 72:96], in_=Asb[:]), writes=["dbg"])
                S.op("dve", lambda e: e.tensor_copy(out=dg[:, 96:120], in_=GGsb[:]), writes=["dbg"])
                S.op("dve", lambda e: e.tensor_copy(out=dg[:, 120:632], in_=rsb[:]), writes=["dbg"])
                S.op("dve", lambda e: e.tensor_copy(out=dg[:, 632:1144], in_=yTf[:, 0, 0:512]), writes=["dbg"])
                S.op("dve", lambda e: e.tensor_copy(out=dg[:, 1144:1656], in_=hTh[:, 0, 0:512]), writes=["dbg"])
                S.op("dve", lambda e: e.tensor_copy(out=dg[:, 1656:2168], in_=actT[:, 0, 0:512]), writes=["dbg"])
                S.op("dve", lambda e: e.tensor_copy(out=dg[:, 2168:2176], in_=cactb[:]), writes=["dbg"])
                S.op("dve", lambda e: e.tensor_copy(out=dg[:, 2176:2688], in_=sq[:, 0, :]), writes=["dbg"])
                S.op("sp", lambda e: e.dma_start(out=dram["dbg"], in_=dg[:]), reads=["dbg"], dma=("out", 9))

        rec8, rec22, extra8 = [], [], []
        Sd = Sched(dry=True)
        gen(Sd, WRing(Sd, "w8", w8views, fetch8, None, rec8, 3), WRing(Sd, "w22", w22views, fetch22, None, rec22, 1))
        for k in [norm8(k) for k in rec8] + extra8:
            if k not in row8: row8[k] = len(row8)
        for k in rec22:
            if k not in row22: row22[k] = len(row22)
        dram["w8"] = nc.dram_tensor("w8all", [max(1, len(row8)), 128, 1024], F32, kind="ExternalInput").ap()
        dram["w22"] = nc.dram_tensor("w22all", [max(1, len(row22)), 128, DFF], F32, kind="ExternalInput").ap()
        if debug:
            dram["dbg"] = nc.dram_tensor("dbg", [128, 4096], F32, kind="ExternalOutput").ap()
        S = Sched()
        S.debug = debug
        gen(S, WRing(S, "w8", w8views, fetch8, rec8, None, 3), WRing(S, "w22", w22views, fetch22, rec22, None, 1))
        with nc.allow_low_precision("bf16 matmul operands, fp32 accumulate"):
            S.emit(nc, stack)
    nc.row8 = row8; nc.row22 = row22
    return nc


def _chunk8(w):
    n = w.shape[1] // 128
    return np.ascontiguousarray(w.reshape(8, 128, n, 128).transpose(2, 1, 0, 3)).reshape(n, 128, 1024)


def prep_inputs(inp, row8, row22):
    f32 = np.float32
    g = {k: np.asarray(v, dtype=f32) for k, v in inp.items()}
    w8 = np.zeros((max(1, len(row8)), 128, 1024), f32)
    src = {}
    for i in range(NL):
        src[("cond", i)] = _chunk8(g["w_cond"][i])
        for f in range(2):
            src[("win", i, f)] = _chunk8(g["w_ffn_in"][i, f])
        j = MIXJ[i]
        if KIND[i] == 0:
            src[("mixin", i)] = _chunk8(g["fox_w_in"][j][:, 0:3072]); wo = g["fox_w_out"][j]
        elif KIND[i] == 1:
            src[("mixin", i)] = _chunk8(g["sconv_w_in"][j]); wo = g["sconv_w_out"][j]
        else:
            src[("mixin", i)] = _chunk8(g["lru_w_in"][j]); wo = g["lru_w_out"][j]
        src[("mixout", i)] = _chunk8(wo)
    for w, nm in enumerate(("lru_w_a", "lru_w_x")):
        bd = np.zeros((128, 8, 128), f32)
        for m in range(8):
            for hh in range(2):
                bd[hh * 64:(hh + 1) * 64, m, hh * 64:(hh + 1) * 64] = g[nm][0, 2 * m + hh]
        src[("lrubd", w)] = bd.reshape(128, 1024)
    for key, r in row8.items():
        if key[0] == "lrubd": w8[r] = src[key]
        else: w8[r] = src[key[:-1]][key[-1]]
    w22all = np.ascontiguousarray(g["w_ffn_out"].reshape(NL, 2, 22, 128, 8, 128).transpose(0, 1, 4, 3, 2, 5))
    w22 = np.zeros((max(1, len(row22)), 128, DFF), f32)
    for key, r in row22.items():
        _, i, f, m = key
        w22[r] = w22all[i, f, m].reshape(128, DFF)
    wf = np.ascontiguousarray(g["fox_w_in"][:, :, 3072:3088].reshape(2, 8, 128, 16).transpose(0, 2, 1, 3)).reshape(2, 128, 128)
    sel = np.zeros((16, 16, 128), f32)
    for h in range(16): sel[h, h, :] = 1.0
    sel = sel.reshape(16, 2048)
    def pc(v):
        return v.reshape(v.shape[:-1] + (8, 128))
    base = np.zeros((128, NV), f32)
    base[:, V_BCOND:V_BCOND + 288] = g["b_cond"].reshape(NL, 72, 128).transpose(2, 0, 1).reshape(128, 288)
    base[:, V_NPRE:V_NPRE + 96] = g["norm_pre"].reshape(NL, 3, 8, 128).transpose(3, 0, 1, 2).reshape(128, 96)
    base[:, V_NPOST:V_NPOST + 96] = g["norm_post"].reshape(NL, 3, 8, 128).transpose(3, 0, 1, 2).reshape(128, 96)
    base[0:16, V_NBF:V_NBF + 2] = g["fox_b_f"].T
    base[:, V_SCW:V_SCW + 24] = g["sconv_conv_w"][0].reshape(3, 8, 128).transpose(2, 1, 0).reshape(128, 24)
    base[:, V_LCW:V_LCW + 32] = g["lru_conv_w"][0].reshape(4, 8, 128).transpose(2, 1, 0).reshape(128, 32)
    base[:, V_LCB:V_LCB + 8] = g["lru_conv_b"][0].reshape(8, 128).T
    base[:, V_LBA:V_LBA + 8] = g["lru_b_a"][0].reshape(8, 128).T
    base[:, V_LBX:V_LBX + 8] = g["lru_b_x"][0].reshape(8, 128).T
    base[:, V_LAM:V_LAM + 8] = g["lru_lambda"][0].reshape(8, 128).T
    base[0:16, V_ID:V_ID + 16] = np.eye(16, dtype=f32)
    s_idx = np.arange(128)[:, None]; t_idx = np.arange(128)[None, :]
    base[:, V_MASK:V_MASK + 128] = np.where(s_idx <= t_idx, 0.0, -240000.0).astype(f32)
    in_maps = []
    for b in range(NCORES):
        v = base.copy()
        v[:, V_C:V_C + 8] = g["c"][b].reshape(8, 128).T
        xin = np.ascontiguousarray(g["x"][b].T.reshape(8, 128, T).transpose(1, 0, 2))
        in_maps.append({"xin": xin, "vecs": v, "w8all": w8, "w22all": w22, "wf": wf, "sel": sel})
    return in_maps


def kernel(**inputs):
    nc = build_program()
    in_maps = prep_inputs(inputs, nc.row8, nc.row22)
    res = run_bass_kernel_spmd(nc, in_maps, core_ids=list(range(NCORES)))
    out = np.empty((NCORES, T, D), np.float32)
    for b in range(NCORES):
        o = np.asarray(res.results[b]["outT"])
        out[b] = o.transpose(2, 1, 0).reshape(T, D)
    return out
```

```python
import numpy as np
import concourse.bass as bass
import concourse.mybir as mybir
from concourse.bass_utils import run_bass_kernel_spmd

F32, BF16 = mybir.dt.float32, mybir.dt.bfloat16
AF = mybir.ActivationFunctionType
ALU = mybir.AluOpType

D = 1024; T = 2048; DFF = 2816; NL = 4; NCORES = 8
EPS = 1e-6
KIND = [0, 1, 2, 0]
MIXJ = [0, 0, 0, 1]
NS8 = 6
NS22 = 2

def _build_index():
    idx = {}; n = 0
    for i in range(NL):
        for j in range(72):
            idx[("cond", i, j)] = n; n += 1
    for i in range(NL):
        for f in range(2):
            for j in range(44):
                idx[("win", i, f, j)] = n; n += 1
    for i in range(NL):
        nch = {0: 24, 1: 24, 2: 16}[KIND[i]]
        for j in range(nch):
            idx[("mixin", i, j)] = n; n += 1
    for i in range(NL):
        for m in range(8):
            idx[("mixout", i, m)] = n; n += 1
    idx[("lrubd", 0)] = n; n += 1
    idx[("lrubd", 1)] = n; n += 1
    return idx, n
W8IDX, N8 = _build_index()

V_BCOND = 0; V_NPRE = V_BCOND + 288; V_NPOST = V_NPRE + 96; V_NBF = V_NPOST + 96
V_SCW = V_NBF + 2; V_LCW = V_SCW + 24; V_LCB = V_LCW + 32; V_LBA = V_LCB + 8; V_LBX = V_LBA + 8
V_LAM = V_LBX + 8; V_C = V_LAM + 8; V_ID = V_C + 8; V_MASK = V_ID + 16; NV = V_MASK + 128


class Sched:
    ENG = ("pe", "act", "dve", "pool", "sp")

    def __init__(self, dry=False):
        self.dry = dry
        self.ops = {e: [] for e in self.ENG}
        self.lastw = {}; self.rd = {}; self.rd_dma = {}
        self.dma_cnt = {}

    def op(self, eng, fn, reads=(), writes=(), dma=None):
        if self.dry:
            return None
        deps = set()
        for r in reads:
            w = self.lastw.get(r)
            if w is not None: deps.add(w)
        for r in writes:
            w = self.lastw.get(r)
            if w is not None: deps.add(w)
            for e, i in self.rd.get(r, {}).items(): deps.add((e, i))
            for d in self.rd_dma.get(r, ()): deps.add(d)
        idx = len(self.ops[eng])
        node = {"fn": fn, "deps": deps, "dma": dma, "sig": False, "val": None}
        if dma is not None:
            c = self.dma_cnt.get(dma, 0) + 1; self.dma_cnt[dma] = c; node["val"] = 16 * c
        self.ops[eng].append(node)
        me = (eng, idx)
        for r in reads:
            if dma is not None: self.rd_dma.setdefault(r, []).append(me)
            else: self.rd.setdefault(r, {})[eng] = idx
        for r in writes:
            self.lastw[r] = me; self.rd[r] = {}; self.rd_dma[r] = []
        return me

    def emit(self, nc, stack):
        ops = self.ops
        NEAR = 2
        def near(eng, idx, e, i):
            return e == eng and eng in ("act", "dve") and idx - i <= NEAR
        for eng in self.ENG:
            for idx, node in enumerate(ops[eng]):
                for (e, i) in node["deps"]:
                    d = ops[e][i]
                    if d["dma"] is None and (e != eng or near(eng, idx, e, i)): d["sig"] = True
        esem = {e: stack.enter_context(nc.semaphore("s_" + e)) for e in self.ENG}
        dsem = {}
        for k in self.dma_cnt:
            dsem[k] = stack.enter_context(nc.semaphore("d%d" % len(dsem)))
        for eng in self.ENG:
            c = 0
            for node in ops[eng]:
                if node["dma"] is None and node["sig"]:
                    c += 1; node["val"] = c
        final_waits = [(dsem[k], 16 * c) for k, c in self.dma_cnt.items() if isinstance(k, tuple) and k[0] == "out"]

        def run(eng, e):
            known = {}
            for idx, node in enumerate(ops[eng]):
                waits = {}
                for (de, di) in node["deps"]:
                    d = ops[de][di]
                    if d["dma"] is not None: sem, val = dsem[d["dma"]], d["val"]
                    elif de == eng and not near(eng, idx, de, di): continue
                    else: sem, val = esem[de], d["val"]
                    key = id(sem)
                    if known.get(key, 0) >= val: continue
                    if key not in waits or waits[key][1] < val: waits[key] = (sem, val)
                for key, (sem, val) in waits.items():
                    e.wait_ge(sem, val); known[key] = val
                ins = node["fn"](e)
                if node["dma"] is not None: ins.then_inc(dsem[node["dma"]], 16)
                elif node["sig"]: ins.then_inc(esem[eng], 1)
            if eng == "sp":
                for sem, val in final_waits: e.wait_ge(sem, val)

        block = stack.enter_context(nc.Block())
        block.tensor(lambda e: run("pe", e))
        block.scalar(lambda e: run("act", e))
        block.vector(lambda e: run("dve", e))
        block.gpsimd(lambda e: run("pool", e))
        block.sync(lambda e: run("sp", e))


class WRing:
    def __init__(self, S, name, views, fetch, seq, rec, hold=1):
        self.S, self.name, self.views, self.fetch, self.seq, self.rec = S, name, views, fetch, seq, rec
        self.pos = 0; self.issued = 0; self.hold = hold

    def get(self, key):
        if self.S.dry:
            self.rec.append(key)
            return self.views[0], (self.name, 0)
        assert self.seq[self.pos] == key, (self.seq[self.pos], key)
        ns = len(self.views)
        while self.issued < min(len(self.seq), self.pos + ns - (self.hold - 1)):
            k = self.issued; sl = k % ns
            src = self.fetch(self.seq[k]); dst = self.views[sl]
            self.S.op("pool", lambda e, dst=dst, src=src: e.dma_start(out=dst, in_=src),
                      writes=[(self.name, sl)], dma=(self.name, sl))
            self.issued += 1
        sl = self.pos % ns; self.pos += 1
        return self.views[sl], (self.name, sl)


def mmg(out, pairs):
    def fn(e):
        n = len(pairs); ins = None
        for k, (l, r) in enumerate(pairs):
            ins = e.matmul(out, lhsT=l, rhs=r, start=(k == 0), stop=(k == n - 1))
        return ins
    return fn


def build_program(n_sub=12, debug=False):
    import contextlib
    dram = {}
    row8 = {}; row22 = {}
    nc = bass.Bass("TRN2", target_bir_lowering=False)
    xin = nc.dram_tensor("xin", [128, 8, T], F32, kind="ExternalInput").ap()
    vecs_d = nc.dram_tensor("vecs", [128, NV], F32, kind="ExternalInput").ap()
    wf_d = nc.dram_tensor("wf", [2, 128, 128], F32, kind="ExternalInput").ap()
    sel_d = nc.dram_tensor("sel", [16, 2048], F32, kind="ExternalInput").ap()
    outT = nc.dram_tensor("outT", [128, 8, T], F32, kind="ExternalOutput").ap()

    stack = contextlib.ExitStack()
    with stack:
        def sb(name, shape, dt):
            return stack.enter_context(nc.sbuf_tensor(name, shape, dt))
        xT = sb("xT", [128, 8, T], F32)
        arena = sb("arena", [128, 24064], F32)
        w8buf = sb("w8buf", [128, NS8 * 1024], BF16)
        w22buf = sb("w22buf", [128, NS22 * DFF], BF16)
        sqb = sb("sqb", [128, 2048], F32)
        rsb = sb("rsb", [128, 512], F32)
        tmpb = [sb("tmp%d" % k, [128, 512], F32) for k in range(2)]
        sgb = [sb("sg%d" % k, [128, 512], F32) for k in range(2)]
        vecs = sb("vecs_sb", [128, NV], F32)
        modsb2 = [sb("modsb%d" % k, [128, 72], F32) for k in range(2)]
        Asb2 = [sb("Asb%d" % k, [128, 24], F32) for k in range(2)]
        GGsb2 = [sb("GGsb%d" % k, [128, 24], F32) for k in range(2)]
        cactb = sb("cactb", [128, 8], BF16)
        onesd = sb("onesd", [128, 128], BF16)
        cst = sb("cst", [128, 4], F32)
        wfb = sb("wfb", [128, 128], BF16)
        sp8 = sb("sp8", [128, 8], F32)
        hl = sb("hl", [128, 1], F32)
        nbf = sb("nbf", [128, 2], F32)
        idb = sb("idb", [128, 16], BF16)
        psA = [stack.enter_context(nc.psum_tensor("psA%d" % k, [128, 512], F32)) for k in range(2)]
        psB = [stack.enter_context(nc.psum_tensor("psB%d" % k, [128, 512], F32)) for k in range(2)]
        psY = [stack.enter_context(nc.psum_tensor("psY%d" % k, [128, 512], F32)) for k in range(2)]
        psS = stack.enter_context(nc.psum_tensor("psS", [128, 512], F32))
        psM = stack.enter_context(nc.psum_tensor("psM", [128, 512], F32))

        def abf(off_b, n_el):
            return arena[:, off_b // 4: off_b // 4 + n_el // 2].bitcast(BF16)
        def af32(off_b, n_el):
            return arena[:, off_b // 4: off_b // 4 + n_el]
        hTh = abf(0, 8 * 1024).rearrange("p (c t) -> p c t", t=1024)
        actT = abf(16384, 22 * 1024).rearrange("p (c t) -> p c t", t=1024)
        yTf = af32(61440, 8 * 1024).rearrange("p (c t) -> p c t", t=1024)
        hT = abf(0, 8 * T).rearrange("p (c t) -> p c t", t=T)
        yTm = af32(0, 8 * 1024).rearrange("p (c t) -> p c t", t=1024)
        ymix = abf(32768, 8 * T).rearrange("p (c t) -> p c t", t=T)
        WK = 65536
        sq = sqb[:, :].bitcast(BF16).rearrange("p (c t) -> p c t", t=512)

        w8views = [w8buf[:, k * 1024:(k + 1) * 1024] for k in range(NS8)]
        w22views = [w22buf[:, k * DFF:(k + 1) * DFF] for k in range(NS22)]

        def norm8(key):
            return key[:3] if key[0] == "mixout" else key
        def fetch8(key):
            return dram["w8"][row8[norm8(key)]]
        def fetch22(key):
            return dram["w22"][row22[key]]

        def v3(view, kc=8):
            return view.rearrange("p (k c) -> p k c", c=128)

        H_ALL = [("h", b) for b in range(4)]

        def gen(S, W8, W22):
            state = {"phase_first": False, "bctr": 0, "par": 0}

            def AR(reads=(), writes=()):
                reads = list(reads); writes = list(writes)
                if state["phase_first"]:
                    writes.append("arena"); state["phase_first"] = False
                else:
                    reads.append("arena")
                return dict(reads=reads, writes=writes)

            def nb():
                state["bctr"] += 1
                return state["bctr"] % 2

            S.op("sp", lambda e: e.dma_start(out=vecs[:], in_=vecs_d), writes=["vecs"], dma=("vl", 0))
            for b in range(4):
                S.op("sp", lambda e, b=b: e.dma_start(out=xT[:, :, b * 512:(b + 1) * 512], in_=xin[:, :, b * 512:(b + 1) * 512]),
                     writes=[("x", b)], dma=("xl", b))
            S.op("dve", lambda e: e.memset(onesd[:], 1.0 / 1024.0), writes=["onesd"])
            S.op("dve", lambda e: e.memset(cst[:, 0:1], EPS), writes=["cst"])
            S.op("dve", lambda e: e.memset(cst[:, 1:2], 1.0), writes=["cst"])
            S.op("dve", lambda e: e.memset(cst[:, 2:3], 0.0), writes=["cst"])
            S.op("act", lambda e: e.activation(out=cactb[:], in_=vecs[:, V_C:V_C + 8], func=AF.Silu),
                 reads=["vecs"], writes=["cact"])
            S.op("dve", lambda e: e.tensor_copy(out=idb[0:16, :], in_=vecs[0:16, V_ID:V_ID + 16]), reads=["vecs"], writes=["idb"])
            S.op("dve", lambda e: e.tensor_scalar(out=nbf[0:16, :], in0=vecs[0:16, V_NBF:V_NBF + 2], scalar1=-1.0, scalar2=None, op0=ALU.mult),
                 reads=["vecs"], writes=["nbf"])

            def Acol(s, c): return Asb2[state["par"]][:, s * 8 + c: s * 8 + c + 1]
            def SHcol(s, c): return modsb2[state["par"]][:, s * 24 + c: s * 24 + c + 1]
            def GGcol(s, c): return GGsb2[state["par"]][:, s * 8 + c: s * 8 + c + 1]
            def MODA(): return ("modA", state["par"])

            def mod_step(i, j):
                wv, wr = W8.get(("cond", i, j))
                w3 = v3(wv)
                S.op("pe", mmg(psM[:, j:j + 1], [(w3[:, kc, :], cactb[:, kc:kc + 1]) for kc in range(8)]),
                     reads=[wr, "cact"], writes=["psM"])

            def mod_finish(i):
                pp = i % 2; modsb = modsb2[pp]; Asb = Asb2[pp]; GGsb = GGsb2[pp]; res = ("modA", pp)
                S.op("dve", lambda e: e.tensor_tensor(out=modsb[:], in0=psM[:, 0:72], in1=vecs[:, V_BCOND + i * 72: V_BCOND + (i + 1) * 72], op=ALU.add),
                     reads=["psM", "vecs"], writes=[res])
                for s in range(3):
                    wsub = 1.0 if s == 1 else 0.5
                    o = (i * 3 + s) * 8
                    S.op("dve", lambda e, s=s, o=o: e.scalar_tensor_tensor(
                        out=Asb[:, s * 8:(s + 1) * 8], in0=modsb[:, s * 24 + 8: s * 24 + 16], scalar=1.0,
                        in1=vecs[:, V_NPRE + o: V_NPRE + o + 8], op0=ALU.add, op1=ALU.mult),
                        reads=["vecs"], writes=[res])
                    S.op("dve", lambda e, s=s, o=o, wsub=wsub: e.scalar_tensor_tensor(
                        out=GGsb[:, s * 8:(s + 1) * 8], in0=modsb[:, s * 24 + 16: s * 24 + 24], scalar=wsub,
                        in1=vecs[:, V_NPOST + o: V_NPOST + o + 8], op0=ALU.mult, op1=ALU.mult),
                        reads=["vecs"], writes=[res])

            def mod_phase(i):
                for j in range(72): mod_step(i, j)
                mod_finish(i)

            def rstd_from_sq(src_reads):
                S.op("pe", mmg(psS[:], [(onesd[:], sq[:, c, :]) for c in range(8)]), reads=["sq", "onesd"], writes=["psS"])
                S.op("act", lambda e: e.activation(out=rsb[:], in_=psS[:], func=AF.Sqrt, bias=cst[:, 0:1], scale=1.0),
                     reads=["psS", "cst"], writes=["rs"])
                S.op("dve", lambda e: e.reciprocal(out=rsb[:], in_=rsb[:]), reads=["rs"], writes=["rs"])

            def prenorm(s, t0, ntok, dst, hres):
                for bi in range(ntok // 512):
                    tb = t0 + bi * 512; xb = tb // 512
                    S.op("act", lambda e, tb=tb: e.activation(out=sq, in_=xT[:, :, tb:tb + 512], func=AF.Square),
                         reads=[("x", xb)], writes=["sq"])
                    rstd_from_sq(None)
                    for c in range(8):
                        k = c % 2
                        S.op("dve", lambda e, c=c, k=k, tb=tb, ac=Acol(s, c): e.scalar_tensor_tensor(
                            out=tmpb[k][:], in0=xT[:, c, tb:tb + 512], scalar=ac, in1=rsb[:], op0=ALU.mult, op1=ALU.mult),
                            reads=[("x", xb), "rs", MODA()], writes=[("tmp", k)])
                        S.op("act", lambda e, c=c, k=k, bi=bi, sh=SHcol(s, c): e.activation(
                            out=dst[:, c, bi * 512:(bi + 1) * 512], in_=tmpb[k][:], func=AF.Identity, bias=sh, scale=1.0),
                            **AR(reads=[("tmp", k), MODA()], writes=[hres(bi)]))

            def postnorm(s, t0, ysrc):
                for tt in range(2):
                    tb = t0 + tt * 512; xb = tb // 512
                    S.op("act", lambda e, tt=tt: e.activation(out=sq, in_=ysrc[:, :, tt * 512:(tt + 1) * 512], func=AF.Square),
                         **AR(reads=[("y", tt)], writes=["sq"]))
                    rstd_from_sq(None)
                    for m in range(8):
                        k = m % 2
                        S.op("dve", lambda e, m=m, k=k, tt=tt, gc=GGcol(s, m): e.scalar_tensor_tensor(
                            out=tmpb[k][:], in0=ysrc[:, m, tt * 512:(tt + 1) * 512], scalar=gc, in1=rsb[:], op0=ALU.mult, op1=ALU.mult),
                            **AR(reads=[("y", tt), "rs", MODA()], writes=[("tmp", k)]))
                        S.op("dve", lambda e, m=m, k=k, tb=tb: e.tensor_tensor(
                            out=xT[:, m, tb:tb + 512], in0=xT[:, m, tb:tb + 512], in1=tmpb[k][:], op=ALU.add),
                            reads=[("tmp", k), ("x", xb)], writes=[("x", xb)])

            def ffn(i, f, s):
                state["phase_first"] = True
                for half in range(2):
                    t0 = half * 1024
                    prenorm(s, t0, 1024, hTh, lambda bi: ("ah", bi))
                    for n in range(22):
                        wg, rg = W8.get(("win", i, f, n)); wu, ru = W8.get(("win", i, f, 22 + n))
                        wg3, wu3 = v3(wg), v3(wu)
                        for tt in range(2):
                            b = nb(); ts = slice(tt * 512, (tt + 1) * 512)
                            S.op("pe", mmg(psA[b][:], [(wg3[:, kc, :], hTh[:, kc, ts]) for kc in range(8)]),
                                 **AR(reads=[rg, ("ah", tt)], writes=[("psA", b)]))
                            S.op("pe", mmg(psB[b][:], [(wu3[:, kc, :], hTh[:, kc, ts]) for kc in range(8)]),
                                 **AR(reads=[ru, ("ah", tt)], writes=[("psB", b)]))
                            S.op("act", lambda e, b=b: e.activation(out=sgb[b][:], in_=psA[b][:], func=AF.Silu),
                                 reads=[("psA", b)], writes=[("sg", b)])
                            S.op("dve", lambda e, b=b, n=n, ts=ts: e.tensor_tensor(out=actT[:, n, ts], in0=sgb[b][:], in1=psB[b][:], op=ALU.mult),
                                 **AR(reads=[("sg", b), ("psB", b)], writes=[("act", tt)]))
                    for m in range(8):
                        wo, ro = W22.get(("wout", i, f, m)); wo3 = v3(wo)
                        for tt in range(2):
                            b = nb(); ts = slice(tt * 512, (tt + 1) * 512)
                            S.op("pe", mmg(psY[b][:], [(wo3[:, kc, :], actT[:, kc, ts]) for kc in range(22)]),
                                 **AR(reads=[ro, ("act", tt)], writes=[("psY", b)]))
                            S.op("act", lambda e, b=b, m=m, ts=ts: e.activation(out=yTf[:, m, ts], in_=psY[b][:], func=AF.Copy),
                                 **AR(reads=[("psY", b)], writes=[("y", tt)]))
                    postnorm(s, t0, yTf)

            def mix_out(i):
                for half in range(2):
                    for m in range(8):
                        wo, ro = W8.get(("mixout", i, m, half)); wo3 = v3(wo)
                        for tt in range(2):
                            b = nb(); tok = half * 1024 + tt * 512
                            S.op("pe", mmg(psY[b][:], [(wo3[:, kc, :], ymix[:, kc, tok:tok + 512]) for kc in range(8)]),
                                 **AR(reads=[ro, ("ymix", tok // 512)], writes=[("psY", b)]))
                            S.op("act", lambda e, b=b, m=m, tt=tt: e.activation(out=yTm[:, m, tt * 512:(tt + 1) * 512], in_=psY[b][:], func=AF.Copy),
                                 **AR(reads=[("psY", b)], writes=[("y", tt)] + H_ALL))
                    postnorm(1, half * 1024, yTm)

            def proj512(w3, tt, ps, wr, pres):
                S.op("pe", mmg(ps[:], [(w3[:, kc, :], hT[:, kc, tt * 512:(tt + 1) * 512]) for kc in range(8)]),
                     **AR(reads=[wr, ("h", tt)], writes=[pres]))

            def sconv(i, between=None):
                state["phase_first"] = True
                prenorm(1, 0, T, hT, lambda bi: ("h", bi))
                cx = af32(WK, 2064)[:, 0:2050]
                cv = af32(WK + 8256, 2048)
                S.op("dve", lambda e: e.memset(cx[:, 0:2], 0.0), **AR(writes=["cx"]))
                for m in range(8):
                    wB, rB = W8.get(("mixin", i, m)); wC, rC = W8.get(("mixin", i, 8 + m)); wX, rX = W8.get(("mixin", i, 16 + m))
                    for tt in range(4):
                        b = nb()
                        proj512(v3(wC), tt, psA[b], rC, ("psA", b))
                        proj512(v3(wX), tt, psB[b], rX, ("psB", b))
                        S.op("act", lambda e, b=b: e.activation(out=sgb[b][:], in_=psA[b][:], func=AF.Copy),
                             reads=[("psA", b)], writes=[("sg", b)])
                        S.op("dve", lambda e, b=b, tt=tt: e.tensor_tensor(out=cx[:, 2 + tt * 512: 2 + (tt + 1) * 512], in0=sgb[b][:], in1=psB[b][:], op=ALU.mult),
                             **AR(reads=[("sg", b), ("psB", b)], writes=["cx"]))
                    wc = lambda k, m=m: vecs[:, V_SCW + m * 3 + k: V_SCW + m * 3 + k + 1]
                    S.op("dve", lambda e, wc=wc: e.tensor_scalar(out=cv[:], in0=cx[:, 2:2050], scalar1=wc(2), scalar2=None, op0=ALU.mult),
                         **AR(reads=["cx", "vecs"], writes=["cv"]))
                    for k in (1, 0):
                        S.op("dve", lambda e, wc=wc, k=k: e.scalar_tensor_tensor(out=cv[:], in0=cx[:, k:k + 2048], scalar=wc(k), in1=cv[:], op0=ALU.mult, op1=ALU.add),
                             **AR(reads=["cx", "vecs"], writes=["cv"]))
                    for tt in range(4):
                        b = nb()
                        proj512(v3(wB), tt, psA[b], rB, ("psA", b))
                        S.op("dve", lambda e, b=b, tt=tt, m=m: e.tensor_tensor(out=ymix[:, m, tt * 512:(tt + 1) * 512], in0=psA[b][:], in1=cv[:, tt * 512:(tt + 1) * 512], op=ALU.mult),
                             **AR(reads=[("psA", b), "cv"], writes=[("ymix", tt)]))
                    if between is not None: between(m)
                mix_out(i)

            def lru(i, between=None):
                state["phase_first"] = True
                prenorm(1, 0, T, hT, lambda bi: ("h", bi))
                if S.dry: extra8.extend([("lrubd", 0), ("lrubd", 1)])
                bd = sqb[:, 0:1024].bitcast(BF16).rearrange("p (w m c) -> p w m c", w=2, m=8)
                bdf = sqb[:, 0:1024].bitcast(BF16)
                for w in range(2):
                    S.op("pool", lambda e, w=w: e.dma_start(out=bdf[:, w * 1024:(w + 1) * 1024], in_=dram["w8"][row8[("lrubd", w)]]),
                         writes=["sq"], dma=("bd", w))
                S.op("act", lambda e: e.activation(out=sp8[:], in_=vecs[:, V_LAM:V_LAM + 8], func=AF.Exp, scale=-1.0), reads=["vecs"], writes=["sp8"])
                S.op("act", lambda e: e.activation(out=sp8[:], in_=sp8[:], func=AF.Ln, bias=cst[:, 1:2], scale=1.0), reads=["cst"], writes=["sp8"])
                S.op("dve", lambda e: e.tensor_scalar(out=sp8[:], in0=sp8[:], scalar1=-8.0, scalar2=None, op0=ALU.mult), reads=["sp8"], writes=["sp8"])
                gl = af32(WK, 1024); xraw = af32(WK + 4096, 1028); xbb = af32(WK + 8208, 1024)
                xbf = abf(WK + 12304, 1024); ab = af32(WK + 14352, 1024); ig = af32(WK + 18448, 1024); tp = af32(WK + 22544, 1024)
                for m in range(8):
                    wG, rG = W8.get(("mixin", i, m)); wXb, rXb = W8.get(("mixin", i, 8 + m))
                    cwc = lambda k, m=m: vecs[:, V_LCW + m * 4 + k: V_LCW + m * 4 + k + 1]
                    for seg in range(2):
                        if seg == 0:
                            S.op("dve", lambda e: e.memset(xraw[:, 0:3], 0.0), **AR(writes=["xraw"]))
                        else:
                            S.op("dve", lambda e: e.tensor_copy(out=xraw[:, 0:3], in_=xraw[:, 1024:1027]), **AR(reads=["xraw"], writes=["xraw"]))
                        for tt in range(2):
                            b = nb(); blk = seg * 2 + tt; ts = slice(tt * 512, (tt + 1) * 512)
                            proj512(v3(wG), blk, psA[b], rG, ("psA", b))
                            S.op("act", lambda e, b=b: e.activation(out=sgb[b][:], in_=psA[b][:], func=AF.Square), reads=[("psA", b)], writes=[("sg", b)])
                            S.op("dve", lambda e, b=b: e.tensor_scalar(out=sgb[b][:], in0=sgb[b][:], scalar1=0.044715, scalar2=1.0, op0=ALU.mult, op1=ALU.add),
                                 reads=[("sg", b)], writes=[("sg", b)])
                            S.op("dve", lambda e, b=b: e.tensor_tensor(out=sgb[b][:], in0=sgb[b][:], in1=psA[b][:], op=ALU.mult),
                                 reads=[("sg", b), ("psA", b)], writes=[("sg", b)])
                            S.op("act", lambda e, b=b: e.activation(out=sgb[b][:], in_=sgb[b][:], func=AF.Sigmoid, scale=1.5957691216057308),
                                 reads=[("sg", b)], writes=[("sg", b)])
                            S.op("dve", lambda e, b=b, ts=ts: e.tensor_tensor(out=gl[:, ts], in0=sgb[b][:], in1=psA[b][:], op=ALU.mult),
                                 **AR(reads=[("sg", b), ("psA", b)], writes=["gl"]))
                            proj512(v3(wXb), blk, psB[b], rXb, ("psB", b))
                            S.op("act", lambda e, b=b, tt=tt: e.activation(out=xraw[:, 3 + tt * 512: 3 + (tt + 1) * 512], in_=psB[b][:], func=AF.Copy),
                                 **AR(reads=[("psB", b)], writes=["xraw"]))
                        S.op("dve", lambda e, cwc=cwc, m=m: e.tensor_scalar(out=xbb[:], in0=xraw[:, 3:1027], scalar1=cwc(3), scalar2=vecs[:, V_LCB + m: V_LCB + m + 1], op0=ALU.mult, op1=ALU.add),
                             **AR(reads=["xraw", "vecs"], writes=["xb"]))
                        for k in range(3):
                            S.op("dve", lambda e, cwc=cwc, k=k: e.scalar_tensor_tensor(out=xbb[:], in0=xraw[:, k:k + 1024], scalar=cwc(k), in1=xbb[:], op0=ALU.mult, op1=ALU.add),
                                 **AR(reads=["xraw", "vecs"], writes=["xb"]))
                        S.op("act", lambda e: e.activation(out=xbf[:], in_=xbb[:], func=AF.Copy), **AR(reads=["xb"], writes=["xbf"]))
                        for tt in range(2):
                            b = nb(); ts = slice(tt * 512, (tt + 1) * 512)
                            S.op("pe", mmg(psA[b][:], [(bd[:, 0, m, :], xbf[:, ts])]), **AR(reads=["sq", "xbf"], writes=[("psA", b)]))
                            S.op("act", lambda e, b=b, m=m: e.activation(out=sgb[b][:], in_=psA[b][:], func=AF.Sigmoid, bias=vecs[:, V_LBA + m: V_LBA + m + 1], scale=1.0),
                                 reads=[("psA", b), "vecs"], writes=[("sg", b)])
                            S.op("dve", lambda e, b=b, m=m: e.tensor_scalar(out=sgb[b][:], in0=sgb[b][:], scalar1=sp8[:, m:m + 1], scalar2=None, op0=ALU.mult),
                                 reads=[("sg", b), "sp8"], writes=[("sg", b)])
                            S.op("act", lambda e, b=b, m=m, ts=ts: e.activation(out=ab[:, ts], in_=sgb[b][:], func=AF.Exp),
                                 **AR(reads=[("sg", b)], writes=["ab"]))
                            S.op("pe", mmg(psB[b][:], [(bd[:, 1, m, :], xbf[:, ts])]), **AR(reads=["sq", "xbf"], writes=[("psB", b)]))
                            S.op("act", lambda e, b=b, m=m, ts=ts: e.activation(out=ig[:, ts], in_=psB[b][:], func=AF.Sigmoid, bias=vecs[:, V_LBX + m: V_LBX + m + 1], scale=1.0),
                                 **AR(reads=[("psB", b), "vecs"], writes=["ig"]))
                        S.op("dve", lambda e: e.tensor_tensor(out=tp[:], in0=ab[:], in1=ab[:], op=ALU.mult), **AR(reads=["ab"], writes=["tp"]))
                        S.op("dve", lambda e: e.tensor_scalar(out=tp[:], in0=tp[:], scalar1=-1.0, scalar2=1.0, op0=ALU.mult, op1=ALU.add), **AR(writes=["tp"]))
                        S.op("act", lambda e: e.activation(out=tp[:], in_=tp[:], func=AF.Sqrt), **AR(reads=["tp"], writes=["tp"]))
                        S.op("dve", lambda e: e.tensor_tensor(out=ig[:], in0=ig[:], in1=xbb[:], op=ALU.mult), **AR(reads=["ig", "xb"], writes=["ig"]))
                        S.op("dve", lambda e: e.tensor_tensor(out=ig[:], in0=ig[:], in1=tp[:], op=ALU.mult), **AR(reads=["tp"], writes=["ig"]))
                        init = 0.0 if seg == 0 else hl[:, 0:1]
                        S.op("dve", lambda e, init=init: e.tensor_tensor_scan(out=tp[:], data0=ab[:], data1=ig[:], initial=init, op0=ALU.mult, op1=ALU.add),
                             **AR(reads=["ab", "ig", "hl"], writes=["tp"]))
                        S.op("act", lambda e: e.activation(out=hl[:, 0:1], in_=tp[:, 1023:1024], func=AF.Copy), **AR(reads=["tp"], writes=["hl"]))
                        S.op("dve", lambda e, m=m, seg=seg: e.tensor_tensor(out=ymix[:, m, seg * 1024:(seg + 1) * 1024], in0=tp[:], in1=gl[:], op=ALU.mult),
                             **AR(reads=["tp", "gl"], writes=[("ymix", seg * 2), ("ymix", seg * 2 + 1)]))
                    if between is not None: between(m)
                mix_out(i)

            def fox(i, between=None):
                j = MIXJ[i]
                state["phase_first"] = True
                prenorm(1, 0, T, hT, lambda bi: ("h", bi))
                selv = sqb[0:16, 0:1024].bitcast(BF16)
                S.op("pool", lambda e: e.dma_start(out=selv, in_=sel_d), writes=["sq"], dma=("sel", 0))
                S.op("pool", lambda e: e.dma_start(out=wfb[:], in_=wf_d[j]), writes=["wfb"], dma=("wfb", 0))
                wfb3 = wfb[:, :].rearrange("p (k c) -> p k c", c=16)
                cum8 = af32(WK, 2048)[0:16, :]
                ncum = af32(WK + 8192, 256)
                QT = abf(WK + 9216, 2048); KT = abf(WK + 13312, 2048)
                Vx = abf(WK + 17408, 4096).rearrange("p (t h c) -> p t h c", t=16, h=2)
                Lb = af32(WK + 9216, 2048)[0:16, :]; Zb = af32(WK + 17408, 2048)[0:16, :]
                S.op("dve", lambda e: e.memset(Zb, 0.0), **AR(writes=["Vx"]))
                for tt in range(4):
                    ts = slice(tt * 512, (tt + 1) * 512)
                    S.op("pe", mmg(psM[0:16, :], [(wfb3[:, kc, :], hT[:, kc, ts]) for kc in range(8)]), **AR(reads=["wfb", ("h", tt)], writes=["psM"]))
                    S.op("act", lambda e, ts=ts: e.activation(out=Lb[:, ts], in_=psM[0:16, :], func=AF.Exp, bias=nbf[0:16, j:j + 1], scale=-1.0),
                         **AR(reads=["psM", "nbf"], writes=["QT", "KT"]))
                    S.op("act", lambda e, ts=ts: e.activation(out=Lb[:, ts], in_=Lb[:, ts], func=AF.Ln, bias=cst[0:16, 1:2], scale=1.0),
                         **AR(reads=["cst"], writes=["QT", "KT"]))
                S.op("dve", lambda e: e.tensor_scalar(out=Lb, in0=Lb, scalar1=-8.0, scalar2=None, op0=ALU.mult), **AR(reads=["QT", "KT"], writes=["QT", "KT"]))
                S.op("dve", lambda e: e.tensor_tensor_scan(out=cum8, data0=Lb, data1=Zb, initial=0.0, op0=ALU.add, op1=ALU.add),
                     **AR(reads=["QT", "KT", "Vx"], writes=["cum8"]))
                hiT = abf(WK + 25600, 2048)[0:16, :]; midT = abf(WK, 2048)[0:16, :]; loT = abf(WK + 4096, 2048)[0:16, :]
                S.op("dve", lambda e: e.tensor_copy(out=hiT, in_=cum8), **AR(reads=["cum8"], writes=["cum8"]))
                S.op("dve", lambda e: e.tensor_tensor(out=Lb, in0=cum8, in1=hiT, op=ALU.subtract), **AR(reads=["cum8"], writes=["QT", "KT"]))
                S.op("dve", lambda e: e.tensor_copy(out=midT, in_=Lb), **AR(reads=["QT", "KT"], writes=["cum8"]))
                S.op("dve", lambda e: e.tensor_tensor(out=Lb, in0=Lb, in1=midT, op=ALU.subtract), **AR(reads=["cum8"], writes=["QT", "KT"]))
                S.op("dve", lambda e: e.tensor_copy(out=loT, in_=Lb), **AR(reads=["QT", "KT"], writes=["cum8"]))
                c3 = (hiT, midT, loT)
                for tile in range(16):
                    S.op("pe", mmg(psM[:, tile * 16:(tile + 1) * 16], [(X[:, tile * 128:(tile + 1) * 128], idb[0:16, :]) for X in c3]),
                         **AR(reads=["cum8", "idb"], writes=["psM"]))
                S.op("dve", lambda e: e.tensor_scalar(out=ncum[:], in0=psM[:, 0:256], scalar1=-0.125, scalar2=None, op0=ALU.mult), **AR(reads=["psM"], writes=["ncum"]))
                S.op("dve", lambda e: e.memset(Vx[:, :, :, 64:128], 1.0), **AR(writes=["Vx"]))
                mask = vecs[:, V_MASK:V_MASK + 128]
                PTs = [sgb[k // 2][:, (k % 2) * 256:(k % 2) * 256 + 256].bitcast(BF16) for k in range(4)]
                SB = [psA[0], psA[1], psB[0], psB[1]]; SR = [("psA", 0), ("psA", 1), ("psB", 0), ("psB", 1)]
                for m in range(8):
                    wq, rq = W8.get(("mixin", i, m)); wk, rk = W8.get(("mixin", i, 8 + m)); wv, rv = W8.get(("mixin", i, 16 + m))
                    wv3 = v3(wv)
                    for tt in range(4):
                        b = nb(); ts = slice(tt * 512, (tt + 1) * 512)
                        proj512(v3(wq), tt, psA[b], rq, ("psA", b))
                        S.op("act", lambda e, b=b, ts=ts: e.activation(out=QT[:, ts], in_=psA[b][:], func=AF.Copy), **AR(reads=[("psA", b)], writes=["QT"]))
                        proj512(v3(wk), tt, psB[b], rk, ("psB", b))
                        S.op("act", lambda e, b=b, ts=ts: e.activation(out=KT[:, ts], in_=psB[b][:], func=AF.Copy), **AR(reads=[("psB", b)], writes=["KT"]))
                    for g in range(4):
                        b = nb()
                        for q in range(4):
                            tile = g * 4 + q
                            S.op("pe", mmg(psB[b][:, q * 128:(q + 1) * 128], [(hT[:, kc, tile * 128:(tile + 1) * 128], wv3[:, kc, :]) for kc in range(8)]),
                                 **AR(reads=[rv, ("h", g)], writes=[("psB", b)]))
                        pv = psB[b][:, :].rearrange("p (q c) -> p q c", c=128)
                        for hh in range(2):
                            S.op("dve", lambda e, g=g, hh=hh, pv=pv: e.tensor_copy(out=Vx[:, g * 4:(g + 1) * 4, hh, 0:64], in_=pv[:, :, hh * 64:(hh + 1) * 64]),
                                 **AR(reads=[("psB", b)], writes=["Vx"]))
                    groups = [(hh, c) for hh in range(2) for c in range(4)]
                    tiles = []
                    for gi, (hh, c) in enumerate(groups):
                        nj = 4 * (c + 1)
                        for jt in range(nj):
                            tiles.append((gi, hh, c, jt, max(0, jt * 128 - c * 512), nj))
                    LA = 3; NT = len(tiles)

                    def rec_S(t, m=m):
                        gi, hh, c, jt, n0, nj = tiles[t]; k = t % 4; h = 2 * m + hh; hs = slice(hh * 64, (hh + 1) * 64)
                        if jt == 0:
                            kcb = gi % 2
                            S.op("pe", mmg(psS[:], [(selv[:, h * 128:(h + 1) * 128], X[:, c * 512:(c + 1) * 512]) for X in c3]), **AR(reads=["sq", "cum8"], writes=["psS"]))
                            S.op("act", lambda e, kcb=kcb: e.activation(out=tmpb[kcb][:], in_=psS[:], func=AF.Copy), reads=["psS"], writes=[("tmp", kcb)])
                        S.op("pe", mmg(SB[k][:, n0:512], [(KT[hs, jt * 128:(jt + 1) * 128], QT[hs, c * 512 + n0:(c + 1) * 512])]),
                             **AR(reads=["QT", "KT"], writes=[SR[k]]))

                    def rec_rest(t, m=m):
                        gi, hh, c, jt, n0, nj = tiles[t]; k = t % 4; h = 2 * m + hh; hs = slice(hh * 64, (hh + 1) * 64)
                        kcb = gi % 2; yb = gi % 2
                        S.op("dve", lambda e, k=k, n0=n0, kcb=kcb: e.tensor_tensor(out=SB[k][:, n0:512], in0=SB[k][:, n0:512], in1=tmpb[kcb][:, n0:512], op=ALU.add),
                             reads=[SR[k], ("tmp", kcb)], writes=[SR[k]])
                        if jt >= 4 * c:
                            S.op("dve", lambda e, k=k, n0=n0: e.tensor_tensor(out=SB[k][:, n0:n0 + 128], in0=SB[k][:, n0:n0 + 128], in1=mask, op=ALU.add),
                                 reads=[SR[k], "vecs"], writes=[SR[k]])
                        S.op("act", lambda e, k=k, n0=n0, jt=jt, h=h: e.activation(out=PTs[k][:, n0:512], in_=SB[k][:, n0:512], func=AF.Exp,
                                                                                   bias=ncum[:, jt * 16 + h: jt * 16 + h + 1], scale=0.125),
                             **AR(reads=[SR[k], "ncum", ("sg", k // 2)], writes=[("pt", k)]))
                        S.op("pe", (lambda k=k, n0=n0, jt=jt, hh=hh, yb=yb, nj=nj: (lambda e: e.matmul(
                            psY[yb][:, n0:512], lhsT=Vx[:, jt, hh, :], rhs=PTs[k][:, n0:512], start=(jt == 0), stop=(jt == nj - 1))))(),
                             **AR(reads=[("pt", k), ("sg", k // 2), "Vx"], writes=[("psY", yb)]))
                        if jt == nj - 1:
                            S.op("dve", lambda e, yb=yb: e.reciprocal(out=rsb[64:128, :], in_=psY[yb][64:128, :]), reads=[("psY", yb)], writes=["rs"])
                            S.op("dve", lambda e, yb=yb, hs=hs, m=m, c=c: e.tensor_tensor(out=ymix[hs, m, c * 512:(c + 1) * 512], in0=psY[yb][0:64, :], in1=rsb[64:128, :], op=ALU.mult),
                                 **AR(reads=[("psY", yb), "rs"], writes=[("ymix", c)]))

                    for t in range(NT + LA):
                        if t < NT: rec_S(t)
                        if t >= LA: rec_rest(t - LA)
                    if between is not None: between(m)
                mix_out(i)

            nsub = 0
            PREFETCH_MOD = False
            mod_phase(0)
            for i in range(NL):
                if nsub >= n_sub: break
                state["par"] = i % 2
                if i > 0 and not PREFETCH_MOD: mod_phase(i)
                for s in range(3):
                    if nsub >= n_sub: break
                    if s == 0: ffn(i, 0, 0)
                    elif s == 2: ffn(i, 1, 2)
                    else:
                        btw = None
                        if PREFETCH_MOD and i + 1 < NL and nsub + 2 < n_sub:
                            def btw(m, i=i):
                                for j in range(m * 9, (m + 1) * 9): mod_step(i + 1, j)
                                if m == 7: mod_finish(i + 1)
                        (fox, sconv, lru)[KIND[i]](i, btw)
                    nsub += 1
            for b in range(4):
                S.op("sp", lambda e, b=b: e.dma_start(out=outT[:, :, b * 512:(b + 1) * 512], in_=xT[:, :, b * 512:(b + 1) * 512]),
                     reads=[("x", b)], dma=("out", b))
            if getattr(S, "debug", False):
                dg = af32(WK, 4096)
                allres = list(S.lastw.keys())
                S.op("dve", lambda e: e.memset(dg[:], 0.0), reads=allres, writes=["dbg"])
                S.op("dve", lambda e: e.tensor_copy(out=dg[:, 0:72], in_=modsb[:]), writes=["dbg"])
                S.op("dve", lambda e: e.tensor_copy(out=dg[:, 72:96], in_=Asb[:]), writes=["dbg"])
                S.op("dve", lambda e: e.tensor_copy(out=dg[:, 96:120], in_=GGsb[:]), writes=["dbg"])
                S.op("dve", lambda e: e.tensor_copy(out=dg[:, 120:632], in_=rsb[:]), writes=["dbg"])
                S.op("dve", lambda e: e.tensor_copy(out=dg[:, 632:1144], in_=yTf[:, 0, 0:512]), writes=["dbg"])
                S.op("dve", lambda e: e.tensor_copy(out=dg[:, 1144:1656], in_=hTh[:, 0, 0:512]), writes=["dbg"])
                S.op("dve", lambda e: e.tensor_copy(out=dg[:, 1656:2168], in_=actT[:, 0, 0:512]), writes=["dbg"])
                S.op("dve", lambda e: e.tensor_copy(out=dg[:, 2168:2176], in_=cactb[:]), writes=["dbg"])
                S.op("dve", lambda e: e.tensor_copy(out=dg[:, 2176:2688], in_=sq[:, 0, :]), writes=["dbg"])
                S.op("sp", lambda e: e.dma_start(out=dram["dbg"], in_=dg[:]), reads=["dbg"], dma=("out", 9))

        rec8, rec22, extra8 = [], [], []
        Sd = Sched(dry=True)
        gen(Sd, WRing(Sd, "w8", w8views, fetch8, None, rec8, 3), WRing(Sd, "w22", w22views, fetch22, None, rec22, 1))
        for k in [norm8(k) for k in rec8] + extra8:
            if k not in row8: row8[k] = len(row8)
        for k in rec22:
            if k not in row22: row22[k] = len(row22)
        dram["w8"] = nc.dram_tensor("w8all", [max(1, len(row8)), 128, 1024], F32, kind="ExternalInput").ap()
        dram["w22"] = nc.dram_tensor("w22all", [max(1, len(row22)), 128, DFF], F32, kind="ExternalInput").ap()
        if debug:
            dram["dbg"] = nc.dram_tensor("dbg", [128, 4096], F32, kind="ExternalOutput").ap()
        S = Sched()
        S.debug = debug
        gen(S, WRing(S, "w8", w8views, fetch8, rec8, None, 3), WRing(S, "w22", w22views, fetch22, rec22, None, 1))
        with nc.allow_low_precision("bf16 matmul operands, fp32 accumulate"):
            S.emit(nc, stack)
    nc.row8 = row8; nc.row22 = row22
    return nc


def _chunk8(w):
    n = w.shape[1] // 128
    return np.ascontiguousarray(w.reshape(8, 128, n, 128).transpose(2, 1, 0, 3)).reshape(n, 128, 1024)


def prep_inputs(inp, row8, row22):
    f32 = np.float32
    g = {k: np.asarray(v, dtype=f32) for k, v in inp.items()}
    w8 = np.zeros((max(1, len(row8)), 128, 1024), f32)
    src = {}
    for i in range(NL):
        src[("cond", i)] = _chunk8(g["w_cond"][i])
        for f in range(2):
            src[("win", i, f)] = _chunk8(g["w_ffn_in"][i, f])
        j = MIXJ[i]
        if KIND[i] == 0:
            src[("mixin", i)] = _chunk8(g["fox_w_in"][j][:, 0:3072]); wo = g["fox_w_out"][j]
        elif KIND[i] == 1:
            src[("mixin", i)] = _chunk8(g["sconv_w_in"][j]); wo = g["sconv_w_out"][j]
        else:
            src[("mixin", i)] = _chunk8(g["lru_w_in"][j]); wo = g["lru_w_out"][j]
        src[("mixout", i)] = _chunk8(wo)
    for w, nm in enumerate(("lru_w_a", "lru_w_x")):
        bd = np.zeros((128, 8, 128), f32)
        for m in range(8):
            for hh in range(2):
                bd[hh * 64:(hh + 1) * 64, m, hh * 64:(hh + 1) * 64] = g[nm][0, 2 * m + hh]
        src[("lrubd", w)] = bd.reshape(128, 1024)
    for key, r in row8.items():
        if key[0] == "lrubd": w8[r] = src[key]
        else: w8[r] = src[key[:-1]][key[-1]]
    w22all = np.ascontiguousarray(g["w_ffn_out"].reshape(NL, 2, 22, 128, 8, 128).transpose(0, 1, 4, 3, 2, 5))
    w22 = np.zeros((max(1, len(row22)), 128, DFF), f32)
    for key, r in row22.items():
        _, i, f, m = key
        w22[r] = w22all[i, f, m].reshape(128, DFF)
    wf = np.ascontiguousarray(g["fox_w_in"][:, :, 3072:3088].reshape(2, 8, 128, 16).transpose(0, 2, 1, 3)).reshape(2, 128, 128)
    sel = np.zeros((16, 16, 128), f32)
    for h in range(16): sel[h, h, :] = 1.0
    sel = sel.reshape(16, 2048)
    def pc(v):
        return v.reshape(v.shape[:-1] + (8, 128))
    base = np.zeros((128, NV), f32)
    base[:, V_BCOND:V_BCOND + 288] = g["b_cond"].reshape(NL, 72, 128).transpose(2, 0, 1).reshape(128, 288)
    base[:, V_NPRE:V_NPRE + 96] = g["norm_pre"].reshape(NL, 3, 8, 128).transpose(3, 0, 1, 2).reshape(128, 96)
    base[:, V_NPOST:V_NPOST + 96] = g["norm_post"].reshape(NL, 3, 8, 128).transpose(3, 0, 1, 2).reshape(128, 96)
    base[0:16, V_NBF:V_NBF + 2] = g["fox_b_f"].T
    base[:, V_SCW:V_SCW + 24] = g["sconv_conv_w"][0].reshape(3, 8, 128).transpose(2, 1, 0).reshape(128, 24)
    base[:, V_LCW:V_LCW + 32] = g["lru_conv_w"][0].reshape(4, 8, 128).transpose(2, 1, 0).reshape(128, 32)
    base[:, V_LCB:V_LCB + 8] = g["lru_conv_b"][0].reshape(8, 128).T
    base[:, V_LBA:V_LBA + 8] = g["lru_b_a"][0].reshape(8, 128).T
    base[:, V_LBX:V_LBX + 8] = g["lru_b_x"][0].reshape(8, 128).T
    base[:, V_LAM:V_LAM + 8] = g["lru_lambda"][0].reshape(8, 128).T
    base[0:16, V_ID:V_ID + 16] = np.eye(16, dtype=f32)
    s_idx = np.arange(128)[:, None]; t_idx = np.arange(128)[None, :]
    base[:, V_MASK:V_MASK + 128] = np.where(s_idx <= t_idx, 0.0, -240000.0).astype(f32)
    in_maps = []
    for b in range(NCORES):
        v = base.copy()
        v[:, V_C:V_C + 8] = g["c"][b].reshape(8, 128).T
        xin = np.ascontiguousarray(g["x"][b].T.reshape(8, 128, T).transpose(1, 0, 2))
        in_maps.append({"xin": xin, "vecs": v, "w8all": w8, "w22all": w22, "wf": wf, "sel": sel})
    return in_maps


def kernel(**inputs):
    nc = build_program()
    in_maps = prep_inputs(inputs, nc.row8, nc.row22)
    res = run_bass_kernel_spmd(nc, in_maps, core_ids=list(range(NCORES)))
    out = np.empty((NCORES, T, D), np.float32)
    for b in range(NCORES):
        o = np.asarray(res.results[b]["outT"])
        out[b] = o.transpose(2, 1, 0).reshape(T, D)
    return out
```

```python
import numpy as np
import concourse.bass as bass
import concourse.mybir as mybir
from concourse.bass_utils import run_bass_kernel_spmd

F32, BF16 = mybir.dt.float32, mybir.dt.bfloat16
AF = mybir.ActivationFunctionType
ALU = mybir.AluOpType

D = 1024; T = 2048; DFF = 2816; NL = 4; NCORES = 8
EPS = 1e-6
KIND = [0, 1, 2, 0]
MIXJ = [0, 0, 0, 1]
NS8 = 6
NS22 = 2

def _build_index():
    idx = {}; n = 0
    for i in range(NL):
        for j in range(72):
            idx[("cond", i, j)] = n; n += 1
    for i in range(NL):
        for f in range(2):
            for j in range(44):
                idx[("win", i, f, j)] = n; n += 1
    for i in range(NL):
        nch = {0: 24, 1: 24, 2: 16}[KIND[i]]
        for j in range(nch):
            idx[("mixin", i, j)] = n; n += 1
    for i in range(NL):
        for m in range(8):
            idx[("mixout", i, m)] = n; n += 1
    idx[("lrubd", 0)] = n; n += 1
    idx[("lrubd", 1)] = n; n += 1
    return idx, n
W8IDX, N8 = _build_index()

V_BCOND = 0; V_NPRE = V_BCOND + 288; V_NPOST = V_NPRE + 96; V_NBF = V_NPOST + 96
V_SCW = V_NBF + 2; V_LCW = V_SCW + 24; V_LCB = V_LCW + 32; V_LBA = V_LCB + 8; V_LBX = V_LBA + 8
V_LAM = V_LBX + 8; V_C = V_LAM + 8; V_ID = V_C + 8; V_MASK = V_ID + 16; NV = V_MASK + 128


class Sched:
    ENG = ("pe", "act", "dve", "pool", "sp")

    def __init__(self, dry=False):
        self.dry = dry
        self.ops = {e: [] for e in self.ENG}
        self.lastw = {}; self.rd = {}; self.rd_dma = {}
        self.dma_cnt = {}

    def op(self, eng, fn, reads=(), writes=(), dma=None):
        if self.dry:
            return None
        deps = set()
        for r in reads:
            w = self.lastw.get(r)
            if w is not None: deps.add(w)
        for r in writes:
            w = self.lastw.get(r)
            if w is not None: deps.add(w)
            for e, i in self.rd.get(r, {}).items(): deps.add((e, i))
            for d in self.rd_dma.get(r, ()): deps.add(d)
        idx = len(self.ops[eng])
        node = {"fn": fn, "deps": deps, "dma": dma, "sig": False, "val": None}
        if dma is not None:
            c = self.dma_cnt.get(dma, 0) + 1; self.dma_cnt[dma] = c; node["val"] = 16 * c
        self.ops[eng].append(node)
        me = (eng, idx)
        for r in reads:
            if dma is not None: self.rd_dma.setdefault(r, []).append(me)
            else: self.rd.setdefault(r, {})[eng] = idx
        for r in writes:
            self.lastw[r] = me; self.rd[r] = {}; self.rd_dma[r] = []
        return me

    def emit(self, nc, stack):
        ops = self.ops
        NEAR = 2
        def near(eng, idx, e, i):
            return e == eng and eng in ("act", "dve") and idx - i <= NEAR
        for eng in self.ENG:
            for idx, node in enumerate(ops[eng]):
                for (e, i) in node["deps"]:
                    d = ops[e][i]
                    if d["dma"] is None and (e != eng or near(eng, idx, e, i)): d["sig"] = True
        esem = {e: stack.enter_context(nc.semaphore("s_" + e)) for e in self.ENG}
        dsem = {}
        for k in self.dma_cnt:
            dsem[k] = stack.enter_context(nc.semaphore("d%d" % len(dsem)))
        for eng in self.ENG:
            c = 0
            for node in ops[eng]:
                if node["dma"] is None and node["sig"]:
                    c += 1; node["val"] = c
        final_waits = [(dsem[k], 16 * c) for k, c in self.dma_cnt.items() if isinstance(k, tuple) and k[0] == "out"]

        def run(eng, e):
            known = {}
            for idx, node in enumerate(ops[eng]):
                waits = {}
                for (de, di) in node["deps"]:
                    d = ops[de][di]
                    if d["dma"] is not None: sem, val = dsem[d["dma"]], d["val"]
                    elif de == eng and not near(eng, idx, de, di): continue
                    else: sem, val = esem[de], d["val"]
                    key = id(sem)
                    if known.get(key, 0) >= val: continue
                    if key not in waits or waits[key][1] < val: waits[key] = (sem, val)
                for key, (sem, val) in waits.items():
                    e.wait_ge(sem, val); known[key] = val
                ins = node["fn"](e)
                if node["dma"] is not None: ins.then_inc(dsem[node["dma"]], 16)
                elif node["sig"]: ins.then_inc(esem[eng], 1)
            if eng == "sp":
                for sem, val in final_waits: e.wait_ge(sem, val)

        block = stack.enter_context(nc.Block())
        block.tensor(lambda e: run("pe", e))
        block.scalar(lambda e: run("act", e))
        block.vector(lambda e: run("dve", e))
        block.gpsimd(lambda e: run("pool", e))
        block.sync(lambda e: run("sp", e))


class WRing:
    def __init__(self, S, name, views, fetch, seq, rec, hold=1):
        self.S, self.name, self.views, self.fetch, self.seq, self.rec = S, name, views, fetch, seq, rec
        self.pos = 0; self.issued = 0; self.hold = hold

    def get(self, key):
        if self.S.dry:
            self.rec.append(key)
            return self.views[0], (self.name, 0)
        assert self.seq[self.pos] == key, (self.seq[self.pos], key)
        ns = len(self.views)
        while self.issued < min(len(self.seq), self.pos + ns - (self.hold - 1)):
            k = self.issued; sl = k % ns
            src = self.fetch(self.seq[k]); dst = self.views[sl]
            self.S.op("pool", lambda e, dst=dst, src=src: e.dma_start(out=dst, in_=src),
                      writes=[(self.name, sl)], dma=(self.name, sl))
            self.issued += 1
        sl = self.pos % ns; self.pos += 1
        return self.views[sl], (self.name, sl)


def mmg(out, pairs):
    def fn(e):
        n = len(pairs); ins = None
        for k, (l, r) in enumerate(pairs):
            ins = e.matmul(out, lhsT=l, rhs=r, start=(k == 0), stop=(k == n - 1))
        return ins
    return fn


def build_program(n_sub=12, debug=False):
    import contextlib
    dram = {}
    row8 = {}; row22 = {}
    nc = bass.Bass("TRN2", target_bir_lowering=False)
    xin = nc.dram_tensor("xin", [128, 8, T], F32, kind="ExternalInput").ap()
    vecs_d = nc.dram_tensor("vecs", [128, NV], F32, kind="ExternalInput").ap()
    wf_d = nc.dram_tensor("wf", [2, 128, 128], F32, kind="ExternalInput").ap()
    sel_d = nc.dram_tensor("sel", [80, 2048], F32, kind="ExternalInput").ap()
    outT = nc.dram_tensor("outT", [128, 8, T], F32, kind="ExternalOutput").ap()

    stack = contextlib.ExitStack()
    with stack:
        def sb(name, shape, dt):
            return stack.enter_context(nc.sbuf_tensor(name, shape, dt))
        xT = sb("xT", [128, 8, T], F32)
        arena = sb("arena", [128, 24064], F32)
        w8buf = sb("w8buf", [128, NS8 * 1024], BF16)
        w22buf = sb("w22buf", [128, NS22 * DFF], BF16)
        sqb = sb("sqb", [128, 2048], F32)
        rsb = sb("rsb", [128, 512], F32)
        tmpb = [sb("tmp%d" % k, [128, 512], F32) for k in range(2)]
        sgb = [sb("sg%d" % k, [128, 512], F32) for k in range(2)]
        vecs = sb("vecs_sb", [128, NV], F32)
        modsb2 = [sb("modsb%d" % k, [128, 72], F32) for k in range(2)]
        Asb2 = [sb("Asb%d" % k, [128, 24], F32) for k in range(2)]
        GGsb2 = [sb("GGsb%d" % k, [128, 24], F32) for k in range(2)]
        cactb = sb("cactb", [128, 8], BF16)
        onesd = sb("onesd", [128, 128], BF16)
        cst = sb("cst", [128, 4], F32)
        wfb = sb("wfb", [128, 128], BF16)
        sp8 = sb("sp8", [128, 8], F32)
        hl = sb("hl", [128, 1], F32)
        nbf = sb("nbf", [128, 2], F32)
        idb = sb("idb", [128, 16], BF16)
        psA = [stack.enter_context(nc.psum_tensor("psA%d" % k, [128, 512], F32)) for k in range(2)]
        psB = [stack.enter_context(nc.psum_tensor("psB%d" % k, [128, 512], F32)) for k in range(2)]
        psY = [stack.enter_context(nc.psum_tensor("psY%d" % k, [128, 512], F32)) for k in range(2)]
        psS = stack.enter_context(nc.psum_tensor("psS", [128, 512], F32))
        psM = stack.enter_context(nc.psum_tensor("psM", [128, 512], F32))

        def abf(off_b, n_el):
            return arena[:, off_b // 4: off_b // 4 + n_el // 2].bitcast(BF16)
        def af32(off_b, n_el):
            return arena[:, off_b // 4: off_b // 4 + n_el]
        hTh = abf(0, 8 * 1024).rearrange("p (c t) -> p c t", t=1024)
        actT = abf(16384, 22 * 1024).rearrange("p (c t) -> p c t", t=1024)
        yTf = af32(61440, 8 * 1024).rearrange("p (c t) -> p c t", t=1024)
        hT = abf(0, 8 * T).rearrange("p (c t) -> p c t", t=T)
        yTm = af32(0, 8 * 1024).rearrange("p (c t) -> p c t", t=1024)
        ymix = abf(32768, 8 * T).rearrange("p (c t) -> p c t", t=T)
        WK = 65536
        sq = sqb[:, :].bitcast(BF16).rearrange("p (c t) -> p c t", t=512)

        w8views = [w8buf[:, k * 1024:(k + 1) * 1024] for k in range(NS8)]
        w22views = [w22buf[:, k * DFF:(k + 1) * DFF] for k in range(NS22)]

        def norm8(key):
            return key[:3] if key[0] == "mixout" else key
        def fetch8(key):
            return dram["w8"][row8[norm8(key)]]
        def fetch22(key):
            return dram["w22"][row22[key]]

        def v3(view, kc=8):
            return view.rearrange("p (k c) -> p k c", c=128)

        H_ALL = [("h", b) for b in range(4)]

        def gen(S, W8, W22):
            state = {"phase_first": False, "bctr": 0, "par": 0}

            def AR(reads=(), writes=()):
                reads = list(reads); writes = list(writes)
                if state["phase_first"]:
                    writes.append("arena"); state["phase_first"] = False
                else:
                    reads.append("arena")
                return dict(reads=reads, writes=writes)

            def nb():
                state["bctr"] += 1
                return state["bctr"] % 2

            S.op("sp", lambda e: e.dma_start(out=vecs[:], in_=vecs_d), writes=["vecs"], dma=("vl", 0))
            for b in range(4):
                S.op("sp", lambda e, b=b: e.dma_start(out=xT[:, :, b * 512:(b + 1) * 512], in_=xin[:, :, b * 512:(b + 1) * 512]),
                     writes=[("x", b)], dma=("xl", b))
            S.op("dve", lambda e: e.memset(onesd[:], 1.0 / 1024.0), writes=["onesd"])
            S.op("dve", lambda e: e.memset(cst[:, 0:1], EPS), writes=["cst"])
            S.op("dve", lambda e: e.memset(cst[:, 1:2], 1.0), writes=["cst"])
            S.op("dve", lambda e: e.memset(cst[:, 2:3], 0.0), writes=["cst"])
            S.op("act", lambda e: e.activation(out=cactb[:], in_=vecs[:, V_C:V_C + 8], func=AF.Silu),
                 reads=["vecs"], writes=["cact"])
            S.op("dve", lambda e: e.tensor_copy(out=idb[0:80, :], in_=vecs[0:80, V_ID:V_ID + 16]), reads=["vecs"], writes=["idb"])
            S.op("dve", lambda e: e.tensor_scalar(out=nbf[0:16, :], in0=vecs[0:16, V_NBF:V_NBF + 2], scalar1=-1.0, scalar2=None, op0=ALU.mult),
                 reads=["vecs"], writes=["nbf"])

            def Acol(s, c): return Asb2[state["par"]][:, s * 8 + c: s * 8 + c + 1]
            def SHcol(s, c): return modsb2[state["par"]][:, s * 24 + c: s * 24 + c + 1]
            def GGcol(s, c): return GGsb2[state["par"]][:, s * 8 + c: s * 8 + c + 1]
            def MODA(): return ("modA", state["par"])

            def mod_step(i, j):
                wv, wr = W8.get(("cond", i, j))
                w3 = v3(wv)
                S.op("pe", mmg(psM[:, j:j + 1], [(w3[:, kc, :], cactb[:, kc:kc + 1]) for kc in range(8)]),
                     reads=[wr, "cact"], writes=["psM"])

            def mod_finish(i):
                pp = i % 2; modsb = modsb2[pp]; Asb = Asb2[pp]; GGsb = GGsb2[pp]; res = ("modA", pp)
                S.op("dve", lambda e: e.tensor_tensor(out=modsb[:], in0=psM[:, 0:72], in1=vecs[:, V_BCOND + i * 72: V_BCOND + (i + 1) * 72], op=ALU.add),
                     reads=["psM", "vecs"], writes=[res])
                for s in range(3):
                    wsub = 1.0 if s == 1 else 0.5
                    o = (i * 3 + s) * 8
                    S.op("dve", lambda e, s=s, o=o: e.scalar_tensor_tensor(
                        out=Asb[:, s * 8:(s + 1) * 8], in0=modsb[:, s * 24 + 8: s * 24 + 16], scalar=1.0,
                        in1=vecs[:, V_NPRE + o: V_NPRE + o + 8], op0=ALU.add, op1=ALU.mult),
                        reads=["vecs"], writes=[res])
                    S.op("dve", lambda e, s=s, o=o, wsub=wsub: e.scalar_tensor_tensor(
                        out=GGsb[:, s * 8:(s + 1) * 8], in0=modsb[:, s * 24 + 16: s * 24 + 24], scalar=wsub,
                        in1=vecs[:, V_NPOST + o: V_NPOST + o + 8], op0=ALU.mult, op1=ALU.mult),
                        reads=["vecs"], writes=[res])

            def mod_phase(i):
                for j in range(72): mod_step(i, j)
                mod_finish(i)

            def stat_mm():
                S.op("pe", mmg(psS[:], [(onesd[:], sq[:, c, :]) for c in range(8)]), reads=["sq", "onesd"], writes=["psS"])

            def stat_sqrt():
                S.op("act", lambda e: e.activation(out=rsb[:], in_=psS[:], func=AF.Sqrt, bias=cst[:, 0:1], scale=1.0),
                     reads=["psS", "cst"], writes=["rs"])

            def stat_recip():
                S.op("dve", lambda e: e.reciprocal(out=rsb[:], in_=rsb[:]), reads=["rs"], writes=["rs"])

            def prenorm(s, t0, ntok, dst, hres):
                nblk = ntok // 512

                def square(bi):
                    tb = t0 + bi * 512
                    S.op("act", lambda e, tb=tb: e.activation(out=sq, in_=xT[:, :, tb:tb + 512], func=AF.Square),
                         reads=[("x", tb // 512)], writes=["sq"])
                    stat_mm()

                square(0); stat_sqrt()
                for bi in range(nblk):
                    tb = t0 + bi * 512; xb = tb // 512
                    if bi + 1 < nblk: square(bi + 1)
                    stat_recip()
                    for c in range(8):
                        k = c % 2
                        S.op("dve", lambda e, c=c, k=k, tb=tb, ac=Acol(s, c): e.scalar_tensor_tensor(
                            out=tmpb[k][:], in0=xT[:, c, tb:tb + 512], scalar=ac, in1=rsb[:], op0=ALU.mult, op1=ALU.mult),
                            reads=[("x", xb), "rs", MODA()], writes=[("tmp", k)])
                        S.op("act", lambda e, c=c, k=k, bi=bi, sh=SHcol(s, c): e.activation(
                            out=dst[:, c, bi * 512:(bi + 1) * 512], in_=tmpb[k][:], func=AF.Identity, bias=sh, scale=1.0),
                            **AR(reads=[("tmp", k), MODA()], writes=[hres(bi)]))
                    if bi + 1 < nblk: stat_sqrt()

            def postnorm(s, t0, ysrc):
                def square(tt):
                    S.op("act", lambda e, tt=tt: e.activation(out=sq, in_=ysrc[:, :, tt * 512:(tt + 1) * 512], func=AF.Square),
                         **AR(reads=[("y", tt)], writes=["sq"]))
                    stat_mm()

                square(0); stat_sqrt()
                for tt in range(2):
                    tb = t0 + tt * 512; xb = tb // 512
                    if tt == 0: square(1)
                    stat_recip()
                    for m in range(8):
                        k = m % 2
                        S.op("dve", lambda e, m=m, k=k, tt=tt, gc=GGcol(s, m): e.scalar_tensor_tensor(
                            out=tmpb[k][:], in0=ysrc[:, m, tt * 512:(tt + 1) * 512], scalar=gc, in1=rsb[:], op0=ALU.mult, op1=ALU.mult),
                            **AR(reads=[("y", tt), "rs", MODA()], writes=[("tmp", k)]))
                        S.op("dve", lambda e, m=m, k=k, tb=tb: e.tensor_tensor(
                            out=xT[:, m, tb:tb + 512], in0=xT[:, m, tb:tb + 512], in1=tmpb[k][:], op=ALU.add),
                            reads=[("tmp", k), ("x", xb)], writes=[("x", xb)])
                    if tt == 0: stat_sqrt()

            def ffn(i, f, s):
                state["phase_first"] = True

                def inproj(lo, hi):
                    for n in range(lo, hi):
                        wg, rg = W8.get(("win", i, f, n)); wu, ru = W8.get(("win", i, f, 22 + n))
                        wg3, wu3 = v3(wg), v3(wu)
                        for tt in range(2):
                            b = nb(); ts = slice(tt * 512, (tt + 1) * 512)
                            S.op("pe", mmg(psA[b][:], [(wg3[:, kc, :], hTh[:, kc, ts]) for kc in range(8)]),
                                 **AR(reads=[rg, ("ah", tt)], writes=[("psA", b)]))
                            S.op("pe", mmg(psB[b][:], [(wu3[:, kc, :], hTh[:, kc, ts]) for kc in range(8)]),
                                 **AR(reads=[ru, ("ah", tt)], writes=[("psB", b)]))
                            S.op("act", lambda e, b=b: e.activation(out=sgb[b][:], in_=psA[b][:], func=AF.Silu),
                                 reads=[("psA", b)], writes=[("sg", b)])
                            S.op("dve", lambda e, b=b, n=n, ts=ts: e.tensor_tensor(out=actT[:, n, ts], in0=sgb[b][:], in1=psB[b][:], op=ALU.mult),
                                 **AR(reads=[("sg", b), ("psB", b)], writes=[("act", tt)]))

                def outproj():
                    for m in range(8):
                        wo, ro = W22.get(("wout", i, f, m)); wo3 = v3(wo)
                        for tt in range(2):
                            b = nb(); ts = slice(tt * 512, (tt + 1) * 512)
                            S.op("pe", mmg(psY[b][:], [(wo3[:, kc, :], actT[:, kc, ts]) for kc in range(22)]),
                                 **AR(reads=[ro, ("act", tt)], writes=[("psY", b)]))
                            S.op("act", lambda e, b=b, m=m, ts=ts: e.activation(out=yTf[:, m, ts], in_=psY[b][:], func=AF.Copy),
                                 **AR(reads=[("psY", b)], writes=[("y", tt)]))

                ah = lambda bi: ("ah", bi)
                prenorm(s, 0, 1024, hTh, ah)
                inproj(0, 22)
                prenorm(s, 1024, 1024, hTh, ah)
                outproj()
                inproj(0, 3)
                postnorm(s, 0, yTf)
                inproj(3, 22)
                outproj()
                postnorm(s, 1024, yTf)

            def mix_out(i):
                for half in range(2):
                    for m in range(8):
                        wo, ro = W8.get(("mixout", i, m, half)); wo3 = v3(wo)
                        for tt in range(2):
                            b = nb(); tok = half * 1024 + tt * 512
                            S.op("pe", mmg(psY[b][:], [(wo3[:, kc, :], ymix[:, kc, tok:tok + 512]) for kc in range(8)]),
                                 **AR(reads=[ro, ("ymix", tok // 512)], writes=[("psY", b)]))
                            S.op("act", lambda e, b=b, m=m, tt=tt: e.activation(out=yTm[:, m, tt * 512:(tt + 1) * 512], in_=psY[b][:], func=AF.Copy),
                                 **AR(reads=[("psY", b)], writes=[("y", tt)] + H_ALL))
                    postnorm(1, half * 1024, yTm)

            def proj512(w3, tt, ps, wr, pres):
                S.op("pe", mmg(ps[:], [(w3[:, kc, :], hT[:, kc, tt * 512:(tt + 1) * 512]) for kc in range(8)]),
                     **AR(reads=[wr, ("h", tt)], writes=[pres]))

            def sconv(i, between=None):
                state["phase_first"] = True
                prenorm(1, 0, T, hT, lambda bi: ("h", bi))
                cx = af32(WK, 2064)[:, 0:2050]
                cv = af32(WK + 8256, 2048)
                S.op("dve", lambda e: e.memset(cx[:, 0:2], 0.0), **AR(writes=["cx"]))
                for m in range(8):
                    wB, rB = W8.get(("mixin", i, m)); wC, rC = W8.get(("mixin", i, 8 + m)); wX, rX = W8.get(("mixin", i, 16 + m))
                    for tt in range(4):
                        b = nb()
                        proj512(v3(wC), tt, psA[b], rC, ("psA", b))
                        proj512(v3(wX), tt, psB[b], rX, ("psB", b))
                        S.op("act", lambda e, b=b: e.activation(out=sgb[b][:], in_=psA[b][:], func=AF.Copy),
                             reads=[("psA", b)], writes=[("sg", b)])
                        S.op("dve", lambda e, b=b, tt=tt: e.tensor_tensor(out=cx[:, 2 + tt * 512: 2 + (tt + 1) * 512], in0=sgb[b][:], in1=psB[b][:], op=ALU.mult),
                             **AR(reads=[("sg", b), ("psB", b)], writes=["cx"]))
                    wc = lambda k, m=m: vecs[:, V_SCW + m * 3 + k: V_SCW + m * 3 + k + 1]
                    S.op("dve", lambda e, wc=wc: e.tensor_scalar(out=cv[:], in0=cx[:, 2:2050], scalar1=wc(2), scalar2=None, op0=ALU.mult),
                         **AR(reads=["cx", "vecs"], writes=["cv"]))
                    for k in (1, 0):
                        S.op("dve", lambda e, wc=wc, k=k: e.scalar_tensor_tensor(out=cv[:], in0=cx[:, k:k + 2048], scalar=wc(k), in1=cv[:], op0=ALU.mult, op1=ALU.add),
                             **AR(reads=["cx", "vecs"], writes=["cv"]))
                    for tt in range(4):
                        b = nb()
                        proj512(v3(wB), tt, psA[b], rB, ("psA", b))
                        S.op("dve", lambda e, b=b, tt=tt, m=m: e.tensor_tensor(out=ymix[:, m, tt * 512:(tt + 1) * 512], in0=psA[b][:], in1=cv[:, tt * 512:(tt + 1) * 512], op=ALU.mult),
                             **AR(reads=[("psA", b), "cv"], writes=[("ymix", tt)]))
                    if between is not None: between(m)
                mix_out(i)

            def lru(i, between=None):
                state["phase_first"] = True
                prenorm(1, 0, T, hT, lambda bi: ("h", bi))
                if S.dry: extra8.extend([("lrubd", 0), ("lrubd", 1)])
                bd = sqb[:, 0:1024].bitcast(BF16).rearrange("p (w m c) -> p w m c", w=2, m=8)
                bdf = sqb[:, 0:1024].bitcast(BF16)
                for w in range(2):
                    S.op("pool", lambda e, w=w: e.dma_start(out=bdf[:, w * 1024:(w + 1) * 1024], in_=dram["w8"][row8[("lrubd", w)]]),
                         writes=["sq"], dma=("bd", w))
                S.op("act", lambda e: e.activation(out=sp8[:], in_=vecs[:, V_LAM:V_LAM + 8], func=AF.Exp, scale=-1.0), reads=["vecs"], writes=["sp8"])
                S.op("act", lambda e: e.activation(out=sp8[:], in_=sp8[:], func=AF.Ln, bias=cst[:, 1:2], scale=1.0), reads=["cst"], writes=["sp8"])
                S.op("dve", lambda e: e.tensor_scalar(out=sp8[:], in0=sp8[:], scalar1=-8.0, scalar2=None, op0=ALU.mult), reads=["sp8"], writes=["sp8"])
                gl = af32(WK, 1024); xraw = af32(WK + 4096, 1028); xbb = af32(WK + 8208, 1024)
                xbf = abf(WK + 12304, 1024); ab = af32(WK + 14352, 1024); ig = af32(WK + 18448, 1024); tp = af32(WK + 22544, 1024)
                for m in range(8):
                    wG, rG = W8.get(("mixin", i, m)); wXb, rXb = W8.get(("mixin", i, 8 + m))
                    cwc = lambda k, m=m: vecs[:, V_LCW + m * 4 + k: V_LCW + m * 4 + k + 1]
                    for seg in range(2):
                        if seg == 0:
                            S.op("dve", lambda e: e.memset(xraw[:, 0:3], 0.0), **AR(writes=["xraw"]))
                        else:
                            S.op("dve", lambda e: e.tensor_copy(out=xraw[:, 0:3], in_=xraw[:, 1024:1027]), **AR(reads=["xraw"], writes=["xraw"]))
                        for tt in range(2):
                            b = nb(); blk = seg * 2 + tt; ts = slice(tt * 512, (tt + 1) * 512)
                            proj512(v3(wG), blk, psA[b], rG, ("psA", b))
                            S.op("act", lambda e, b=b: e.activation(out=sgb[b][:], in_=psA[b][:], func=AF.Square), reads=[("psA", b)], writes=[("sg", b)])
                            S.op("dve", lambda e, b=b: e.tensor_scalar(out=sgb[b][:], in0=sgb[b][:], scalar1=0.044715, scalar2=1.0, op0=ALU.mult, op1=ALU.add),
                                 reads=[("sg", b)], writes=[("sg", b)])
                            S.op("dve", lambda e, b=b: e.tensor_tensor(out=sgb[b][:], in0=sgb[b][:], in1=psA[b][:], op=ALU.mult),
                                 reads=[("sg", b), ("psA", b)], writes=[("sg", b)])
                            S.op("act", lambda e, b=b: e.activation(out=sgb[b][:], in_=sgb[b][:], func=AF.Sigmoid, scale=1.5957691216057308),
                                 reads=[("sg", b)], writes=[("sg", b)])
                            S.op("dve", lambda e, b=b, ts=ts: e.tensor_tensor(out=gl[:, ts], in0=sgb[b][:], in1=psA[b][:], op=ALU.mult),
                                 **AR(reads=[("sg", b), ("psA", b)], writes=["gl"]))
                            proj512(v3(wXb), blk, psB[b], rXb, ("psB", b))
                            S.op("act", lambda e, b=b, tt=tt: e.activation(out=xraw[:, 3 + tt * 512: 3 + (tt + 1) * 512], in_=psB[b][:], func=AF.Copy),
                                 **AR(reads=[("psB", b)], writes=["xraw"]))
                        S.op("dve", lambda e, cwc=cwc, m=m: e.tensor_scalar(out=xbb[:], in0=xraw[:, 3:1027], scalar1=cwc(3), scalar2=vecs[:, V_LCB + m: V_LCB + m + 1], op0=ALU.mult, op1=ALU.add),
                             **AR(reads=["xraw", "vecs"], writes=["xb"]))
                        for k in range(3):
                            S.op("dve", lambda e, cwc=cwc, k=k: e.scalar_tensor_tensor(out=xbb[:], in0=xraw[:, k:k + 1024], scalar=cwc(k), in1=xbb[:], op0=ALU.mult, op1=ALU.add),
                                 **AR(reads=["xraw", "vecs"], writes=["xb"]))
                        S.op("act", lambda e: e.activation(out=xbf[:], in_=xbb[:], func=AF.Copy), **AR(reads=["xb"], writes=["xbf"]))
                        for tt in range(2):
                            b = nb(); ts = slice(tt * 512, (tt + 1) * 512)
                            S.op("pe", mmg(psA[b][:], [(bd[:, 0, m, :], xbf[:, ts])]), **AR(reads=["sq", "xbf"], writes=[("psA", b)]))
                            S.op("act", lambda e, b=b, m=m: e.activation(out=sgb[b][:], in_=psA[b][:], func=AF.Sigmoid, bias=vecs[:, V_LBA + m: V_LBA + m + 1], scale=1.0),
                                 reads=[("psA", b), "vecs"], writes=[("sg", b)])
                            S.op("dve", lambda e, b=b, m=m: e.tensor_scalar(out=sgb[b][:], in0=sgb[b][:], scalar1=sp8[:, m:m + 1], scalar2=None, op0=ALU.mult),
                                 reads=[("sg", b), "sp8"], writes=[("sg", b)])
                            S.op("act", lambda e, b=b, m=m, ts=ts: e.activation(out=ab[:, ts], in_=sgb[b][:], func=AF.Exp),
                                 **AR(reads=[("sg", b)], writes=["ab"]))
                            S.op("pe", mmg(psB[b][:], [(bd[:, 1, m, :], xbf[:, ts])]), **AR(reads=["sq", "xbf"], writes=[("psB", b)]))
                            S.op("act", lambda e, b=b, m=m, ts=ts: e.activation(out=ig[:, ts], in_=psB[b][:], func=AF.Sigmoid, bias=vecs[:, V_LBX + m: V_LBX + m + 1], scale=1.0),
                                 **AR(reads=[("psB", b), "vecs"], writes=["ig"]))
                        S.op("dve", lambda e: e.tensor_tensor(out=tp[:], in0=ab[:], in1=ab[:], op=ALU.mult), **AR(reads=["ab"], writes=["tp"]))
                        S.op("dve", lambda e: e.tensor_scalar(out=tp[:], in0=tp[:], scalar1=-1.0, scalar2=1.0, op0=ALU.mult, op1=ALU.add), **AR(writes=["tp"]))
                        S.op("act", lambda e: e.activation(out=tp[:], in_=tp[:], func=AF.Sqrt), **AR(reads=["tp"], writes=["tp"]))
                        S.op("dve", lambda e: e.tensor_tensor(out=ig[:], in0=ig[:], in1=xbb[:], op=ALU.mult), **AR(reads=["ig", "xb"], writes=["ig"]))
                        S.op("dve", lambda e: e.tensor_tensor(out=ig[:], in0=ig[:], in1=tp[:], op=ALU.mult), **AR(reads=["tp"], writes=["ig"]))
                        init = 0.0 if seg == 0 else hl[:, 0:1]
                        S.op("dve", lambda e, init=init: e.tensor_tensor_scan(out=tp[:], data0=ab[:], data1=ig[:], initial=init, op0=ALU.mult, op1=ALU.add),
                             **AR(reads=["ab", "ig", "hl"], writes=["tp"]))
                        S.op("act", lambda e: e.activation(out=hl[:, 0:1], in_=tp[:, 1023:1024], func=AF.Copy), **AR(reads=["tp"], writes=["hl"]))
                        S.op("dve", lambda e, m=m, seg=seg: e.tensor_tensor(out=ymix[:, m, seg * 1024:(seg + 1) * 1024], in0=tp[:], in1=gl[:], op=ALU.mult),
                             **AR(reads=["tp", "gl"], writes=[("ymix", seg * 2), ("ymix", seg * 2 + 1)]))
                    if between is not None: between(m)
                mix_out(i)

            def fox(i, between=None):
                j = MIXJ[i]
                state["phase_first"] = True
                prenorm(1, 0, T, hT, lambda bi: ("h", bi))
                selv = sqb[0:80, 0:1024].bitcast(BF16)
                S.op("pool", lambda e: e.dma_start(out=selv, in_=sel_d), writes=["sq"], dma=("sel", 0))
                S.op("pool", lambda e: e.dma_start(out=wfb[:], in_=wf_d[j]), writes=["wfb"], dma=("wfb", 0))
                wfb3 = wfb[:, :].rearrange("p (k c) -> p k c", c=16)
                cum8 = af32(WK, 2048)[0:16, :]
                ncum = af32(WK + 8192, 256)
                QT = abf(WK + 9216, 2048); KT = abf(WK + 13312, 2048)
                Vx = abf(WK + 17408, 4096).rearrange("p (t h c) -> p t h c", t=16, h=2)
                Lb = af32(WK + 9216, 2048)[0:16, :]; Zb = af32(WK + 17408, 2048)[0:16, :]
                S.op("dve", lambda e: e.memset(Zb, 0.0), **AR(writes=["Vx"]))
                for tt in range(4):
                    ts = slice(tt * 512, (tt + 1) * 512)
                    S.op("pe", mmg(psM[0:16, :], [(wfb3[:, kc, :], hT[:, kc, ts]) for kc in range(8)]), **AR(reads=["wfb", ("h", tt)], writes=["psM"]))
                    S.op("act", lambda e, ts=ts: e.activation(out=Lb[:, ts], in_=psM[0:16, :], func=AF.Exp, bias=nbf[0:16, j:j + 1], scale=-1.0),
                         **AR(reads=["psM", "nbf"], writes=["QT", "KT"]))
                    S.op("act", lambda e, ts=ts: e.activation(out=Lb[:, ts], in_=Lb[:, ts], func=AF.Ln, bias=cst[0:16, 1:2], scale=1.0),
                         **AR(reads=["cst"], writes=["QT", "KT"]))
                S.op("dve", lambda e: e.tensor_scalar(out=Lb, in0=Lb, scalar1=-8.0, scalar2=None, op0=ALU.mult), **AR(reads=["QT", "KT"], writes=["QT", "KT"]))
                S.op("dve", lambda e: e.tensor_tensor_scan(out=cum8, data0=Lb, data1=Zb, initial=0.0, op0=ALU.add, op1=ALU.add),
                     **AR(reads=["QT", "KT", "Vx"], writes=["cum8"]))
                C80f = abf(WK + 25600, 2048)
                C80 = C80f[0:80, :]; hiT = C80f[0:16, :]; midT = C80f[32:48, :]; loT = C80f[64:80, :]
                S.op("dve", lambda e: e.memset(C80, 0.0), **AR(writes=["c80"]))
                S.op("dve", lambda e: e.tensor_copy(out=hiT, in_=cum8), **AR(reads=["cum8"], writes=["c80"]))
                S.op("dve", lambda e: e.tensor_tensor(out=Lb, in0=cum8, in1=hiT, op=ALU.subtract), **AR(reads=["cum8", "c80"], writes=["QT", "KT"]))
                S.op("dve", lambda e: e.tensor_copy(out=midT, in_=Lb), **AR(reads=["QT", "KT"], writes=["c80"]))
                S.op("dve", lambda e: e.tensor_copy(out=cum8, in_=midT), **AR(reads=["c80"], writes=["cum8"]))
                S.op("dve", lambda e: e.tensor_tensor(out=Lb, in0=Lb, in1=cum8, op=ALU.subtract), **AR(reads=["cum8"], writes=["QT", "KT"]))
                S.op("dve", lambda e: e.tensor_copy(out=loT, in_=Lb), **AR(reads=["QT", "KT"], writes=["c80"]))
                for tile in range(16):
                    S.op("pe", mmg(psM[:, tile * 16:(tile + 1) * 16], [(C80[:, tile * 128:(tile + 1) * 128], idb[0:80, :])]),
                         **AR(reads=["c80", "idb"], writes=["psM"]))
                S.op("dve", lambda e: e.tensor_scalar(out=ncum[:], in0=psM[:, 0:256], scalar1=-0.125, scalar2=None, op0=ALU.mult), **AR(reads=["psM"], writes=["ncum"]))
                S.op("dve", lambda e: e.memset(Vx[:, :, :, 64:128], 1.0), **AR(writes=["Vx"]))
                mask = vecs[:, V_MASK:V_MASK + 128]
                PTs = [sgb[k // 2][:, (k % 2) * 256:(k % 2) * 256 + 256].bitcast(BF16) for k in range(4)]
                SB = [psA[0], psA[1], psB[0], psB[1]]; SR = [("psA", 0), ("psA", 1), ("psB", 0), ("psB", 1)]
                for m in range(8):
                    wq, rq = W8.get(("mixin", i, m)); wk, rk = W8.get(("mixin", i, 8 + m)); wv, rv = W8.get(("mixin", i, 16 + m))
                    wv3 = v3(wv)
                    for tt in range(4):
                        b = nb(); ts = slice(tt * 512, (tt + 1) * 512)
                        proj512(v3(wq), tt, psA[b], rq, ("psA", b))
                        S.op("act", lambda e, b=b, ts=ts: e.activation(out=QT[:, ts], in_=psA[b][:], func=AF.Copy), **AR(reads=[("psA", b)], writes=["QT"]))
                        proj512(v3(wk), tt, psB[b], rk, ("psB", b))
                        S.op("act", lambda e, b=b, ts=ts: e.activation(out=KT[:, ts], in_=psB[b][:], func=AF.Copy), **AR(reads=[("psB", b)], writes=["KT"]))
                    for g in range(4):
                        b = nb()
                        for q in range(4):
                            tile = g * 4 + q
                            S.op("pe", mmg(psB[b][:, q * 128:(q + 1) * 128], [(hT[:, kc, tile * 128:(tile + 1) * 128], wv3[:, kc, :]) for kc in range(8)]),
                                 **AR(reads=[rv, ("h", g)], writes=[("psB", b)]))
                        pv = psB[b][:, :].rearrange("p (q c) -> p q c", c=128)
                        for hh in range(2):
                            S.op("dve", lambda e, g=g, hh=hh, pv=pv: e.tensor_copy(out=Vx[:, g * 4:(g + 1) * 4, hh, 0:64], in_=pv[:, :, hh * 64:(hh + 1) * 64]),
                                 **AR(reads=[("psB", b)], writes=["Vx"]))
                    groups = [(hh, c) for hh in range(2) for c in range(4)]
                    tiles = []
                    for gi, (hh, c) in enumerate(groups):
                        nj = 4 * (c + 1)
                        for jt in range(nj):
                            tiles.append((gi, hh, c, jt, max(0, jt * 128 - c * 512), nj))
                    LA = 3; NT = len(tiles)

                    def rec_S(t, m=m):
                        gi, hh, c, jt, n0, nj = tiles[t]; k = t % 4; h = 2 * m + hh; hs = slice(hh * 64, (hh + 1) * 64)
                        S.op("pe", mmg(SB[k][:, n0:512], [(KT[hs, jt * 128:(jt + 1) * 128], QT[hs, c * 512 + n0:(c + 1) * 512]),
                                                          (selv[:, h * 128:(h + 1) * 128], C80[:, c * 512 + n0:(c + 1) * 512])]),
                             **AR(reads=["QT", "KT", "sq", "c80"], writes=[SR[k]]))

                    def rec_rest(t, m=m):
                        gi, hh, c, jt, n0, nj = tiles[t]; k = t % 4; h = 2 * m + hh; hs = slice(hh * 64, (hh + 1) * 64)
                        kcb = gi % 2; yb = gi % 2
                        if jt >= 4 * c:
                            S.op("dve", lambda e, k=k, n0=n0: e.tensor_tensor(out=SB[k][:, n0:n0 + 128], in0=SB[k][:, n0:n0 + 128], in1=mask, op=ALU.add),
                                 reads=[SR[k], "vecs"], writes=[SR[k]])
                        S.op("act", lambda e, k=k, n0=n0, jt=jt, h=h: e.activation(out=PTs[k][:, n0:512], in_=SB[k][:, n0:512], func=AF.Exp,
                                                                                   bias=ncum[:, jt * 16 + h: jt * 16 + h + 1], scale=0.125),
                             **AR(reads=[SR[k], "ncum", ("sg", k // 2)], writes=[("pt", k)]))
                        S.op("pe", (lambda k=k, n0=n0, jt=jt, hh=hh, yb=yb, nj=nj: (lambda e: e.matmul(
                            psY[yb][:, n0:512], lhsT=Vx[:, jt, hh, :], rhs=PTs[k][:, n0:512], start=(jt == 0), stop=(jt == nj - 1))))(),
                             **AR(reads=[("pt", k), ("sg", k // 2), "Vx"], writes=[("psY", yb)]))
                        if jt == nj - 1:
                            S.op("dve", lambda e, yb=yb: e.reciprocal(out=rsb[64:128, :], in_=psY[yb][64:128, :]), reads=[("psY", yb)], writes=["rs"])
                            S.op("dve", lambda e, yb=yb, hs=hs, m=m, c=c: e.tensor_tensor(out=ymix[hs, m, c * 512:(c + 1) * 512], in0=psY[yb][0:64, :], in1=rsb[64:128, :], op=ALU.mult),
                                 **AR(reads=[("psY", yb), "rs"], writes=[("ymix", c)]))

                    for t in range(NT + LA):
                        if t < NT: rec_S(t)
                        if t >= LA: rec_rest(t - LA)
                    if between is not None: between(m)
                mix_out(i)

            nsub = 0
            PREFETCH_MOD = False
            mod_phase(0)
            for i in range(NL):
                if nsub >= n_sub: break
                state["par"] = i % 2
                if i > 0 and not PREFETCH_MOD: mod_phase(i)
                for s in range(3):
                    if nsub >= n_sub: break
                    if s == 0: ffn(i, 0, 0)
                    elif s == 2: ffn(i, 1, 2)
                    else:
                        btw = None
                        if PREFETCH_MOD and i + 1 < NL and nsub + 2 < n_sub:
                            def btw(m, i=i):
                                for j in range(m * 9, (m + 1) * 9): mod_step(i + 1, j)
                                if m == 7: mod_finish(i + 1)
                        (fox, sconv, lru)[KIND[i]](i, btw)
                    nsub += 1
            for b in range(4):
                S.op("sp", lambda e, b=b: e.dma_start(out=outT[:, :, b * 512:(b + 1) * 512], in_=xT[:, :, b * 512:(b + 1) * 512]),
                     reads=[("x", b)], dma=("out", b))
            if getattr(S, "debug", False):
                dg = af32(WK, 4096)
                allres = list(S.lastw.keys())
                S.op("dve", lambda e: e.memset(dg[:], 0.0), reads=allres, writes=["dbg"])
                S.op("dve", lambda e: e.tensor_copy(out=dg[:, 0:72], in_=modsb[:]), writes=["dbg"])
                S.op("dve", lambda e: e.tensor_copy(out=dg[:, 72:96], in_=Asb[:]), writes=["dbg"])
                S.op("dve", lambda e: e.tensor_copy(out=dg[:, 96:120], in_=GGsb[:]), writes=["dbg"])
                S.op("dve", lambda e: e.tensor_copy(out=dg[:, 120:632], in_=rsb[:]), writes=["dbg"])
                S.op("dve", lambda e: e.tensor_copy(out=dg[:, 632:1144], in_=yTf[:, 0, 0:512]), writes=["dbg"])
                S.op("dve", lambda e: e.tensor_copy(out=dg[:, 1144:1656], in_=hTh[:, 0, 0:512]), writes=["dbg"])
                S.op("dve", lambda e: e.tensor_copy(out=dg[:, 1656:2168], in_=actT[:, 0, 0:512]), writes=["dbg"])
                S.op("dve", lambda e: e.tensor_copy(out=dg[:, 2168:2176], in_=cactb[:]), writes=["dbg"])
                S.op("dve", lambda e: e.tensor_copy(out=dg[:, 2176:2688], in_=sq[:, 0, :]), writes=["dbg"])
                S.op("sp", lambda e: e.dma_start(out=dram["dbg"], in_=dg[:]), reads=["dbg"], dma=("out", 9))

        rec8, rec22, extra8 = [], [], []
        Sd = Sched(dry=True)
        gen(Sd, WRing(Sd, "w8", w8views, fetch8, None, rec8, 3), WRing(Sd, "w22", w22views, fetch22, None, rec22, 1))
        for k in [norm8(k) for k in rec8] + extra8:
            if k not in row8: row8[k] = len(row8)
        for k in rec22:
            if k not in row22: row22[k] = len(row22)
        dram["w8"] = nc.dram_tensor("w8all", [max(1, len(row8)), 128, 1024], F32, kind="ExternalInput").ap()
        dram["w22"] = nc.dram_tensor("w22all", [max(1, len(row22)), 128, DFF], F32, kind="ExternalInput").ap()
        if debug:
            dram["dbg"] = nc.dram_tensor("dbg", [128, 4096], F32, kind="ExternalOutput").ap()
        S = Sched()
        S.debug = debug
        gen(S, WRing(S, "w8", w8views, fetch8, rec8, None, 3), WRing(S, "w22", w22views, fetch22, rec22, None, 1))
        with nc.allow_low_precision("bf16 matmul operands, fp32 accumulate"):
            S.emit(nc, stack)
    nc.row8 = row8; nc.row22 = row22
    return nc


def _chunk8(w):
    n = w.shape[1] // 128
    return np.ascontiguousarray(w.reshape(8, 128, n, 128).transpose(2, 1, 0, 3)).reshape(n, 128, 1024)


def prep_inputs(inp, row8, row22):
    f32 = np.float32
    g = {k: np.asarray(v, dtype=f32) for k, v in inp.items()}
    w8 = np.zeros((max(1, len(row8)), 128, 1024), f32)
    src = {}
    for i in range(NL):
        src[("cond", i)] = _chunk8(g["w_cond"][i])
        for f in range(2):
            src[("win", i, f)] = _chunk8(g["w_ffn_in"][i, f])
        j = MIXJ[i]
        if KIND[i] == 0:
            src[("mixin", i)] = _chunk8(g["fox_w_in"][j][:, 0:3072]); wo = g["fox_w_out"][j]
        elif KIND[i] == 1:
            src[("mixin", i)] = _chunk8(g["sconv_w_in"][j]); wo = g["sconv_w_out"][j]
        else:
            src[("mixin", i)] = _chunk8(g["lru_w_in"][j]); wo = g["lru_w_out"][j]
        src[("mixout", i)] = _chunk8(wo)
    for w, nm in enumerate(("lru_w_a", "lru_w_x")):
        bd = np.zeros((128, 8, 128), f32)
        for m in range(8):
            for hh in range(2):
                bd[hh * 64:(hh + 1) * 64, m, hh * 64:(hh + 1) * 64] = g[nm][0, 2 * m + hh]
        src[("lrubd", w)] = bd.reshape(128, 1024)
    for key, r in row8.items():
        if key[0] == "lrubd": w8[r] = src[key]
        else: w8[r] = src[key[:-1]][key[-1]]
    w22all = np.ascontiguousarray(g["w_ffn_out"].reshape(NL, 2, 22, 128, 8, 128).transpose(0, 1, 4, 3, 2, 5))
    w22 = np.zeros((max(1, len(row22)), 128, DFF), f32)
    for key, r in row22.items():
        _, i, f, m = key
        w22[r] = w22all[i, f, m].reshape(128, DFF)
    wf = np.ascontiguousarray(g["fox_w_in"][:, :, 3072:3088].reshape(2, 8, 128, 16).transpose(0, 2, 1, 3)).reshape(2, 128, 128)
    sel = np.zeros((80, 16, 128), f32)
    for h in range(16):
        for r in (0, 32, 64): sel[r + h, h, :] = 1.0
    sel = sel.reshape(80, 2048)
    def pc(v):
        return v.reshape(v.shape[:-1] + (8, 128))
    base = np.zeros((128, NV), f32)
    base[:, V_BCOND:V_BCOND + 288] = g["b_cond"].reshape(NL, 72, 128).transpose(2, 0, 1).reshape(128, 288)
    base[:, V_NPRE:V_NPRE + 96] = g["norm_pre"].reshape(NL, 3, 8, 128).transpose(3, 0, 1, 2).reshape(128, 96)
    base[:, V_NPOST:V_NPOST + 96] = g["norm_post"].reshape(NL, 3, 8, 128).transpose(3, 0, 1, 2).reshape(128, 96)
    base[0:16, V_NBF:V_NBF + 2] = g["fox_b_f"].T
    base[:, V_SCW:V_SCW + 24] = g["sconv_conv_w"][0].reshape(3, 8, 128).transpose(2, 1, 0).reshape(128, 24)
    base[:, V_LCW:V_LCW + 32] = g["lru_conv_w"][0].reshape(4, 8, 128).transpose(2, 1, 0).reshape(128, 32)
    base[:, V_LCB:V_LCB + 8] = g["lru_conv_b"][0].reshape(8, 128).T
    base[:, V_LBA:V_LBA + 8] = g["lru_b_a"][0].reshape(8, 128).T
    base[:, V_LBX:V_LBX + 8] = g["lru_b_x"][0].reshape(8, 128).T
    base[:, V_LAM:V_LAM + 8] = g["lru_lambda"][0].reshape(8, 128).T
    for r in (0, 32, 64): base[r:r + 16, V_ID:V_ID + 16] = np.eye(16, dtype=f32)
    s_idx = np.arange(128)[:, None]; t_idx = np.arange(128)[None, :]
    base[:, V_MASK:V_MASK + 128] = np.where(s_idx <= t_idx, 0.0, -240000.0).astype(f32)
    in_maps = []
    for b in range(NCORES):
        v = base.copy()
        v[:, V_C:V_C + 8] = g["c"][b].reshape(8, 128).T
        xin = np.ascontiguousarray(g["x"][b].T.reshape(8, 128, T).transpose(1, 0, 2))
        in_maps.append({"xin": xin, "vecs": v, "w8all": w8, "w22all": w22, "wf": wf, "sel": sel})
    return in_maps


def kernel(**inputs):
    nc = build_program()
    in_maps = prep_inputs(inputs, nc.row8, nc.row22)
    res = run_bass_kernel_spmd(nc, in_maps, core_ids=list(range(NCORES)))
    out = np.empty((NCORES, T, D), np.float32)
    for b in range(NCORES):
        o = np.asarray(res.results[b]["outT"])
        out[b] = o.transpose(2, 1, 0).reshape(T, D)
    return out
```

```python
import numpy as np
import concourse.bass as bass
import concourse.mybir as mybir
from concourse.bass_utils import run_bass_kernel_spmd

F32, BF16 = mybir.dt.float32, mybir.dt.bfloat16
AF = mybir.ActivationFunctionType
ALU = mybir.AluOpType

D = 1024; T = 2048; DFF = 2816; NL = 4; NCORES = 8
EPS = 1e-6
KIND = [0, 1, 2, 0]
MIXJ = [0, 0, 0, 1]
NS8 = 6
NS22 = 2

def _build_index():
    idx = {}; n = 0
    for i in range(NL):
        for j in range(72):
            idx[("cond", i, j)] = n; n += 1
    for i in range(NL):
        for f in range(2):
            for j in range(44):
                idx[("win", i, f, j)] = n; n += 1
    for i in range(NL):
        nch = {0: 24, 1: 24, 2: 16}[KIND[i]]
        for j in range(nch):
            idx[("mixin", i, j)] = n; n += 1
    for i in range(NL):
        for m in range(8):
            idx[("mixout", i, m)] = n; n += 1
    idx[("lrubd", 0)] = n; n += 1
    idx[("lrubd", 1)] = n; n += 1
    return idx, n
W8IDX, N8 = _build_index()

V_BCOND = 0; V_NPRE = V_BCOND + 288; V_NPOST = V_NPRE + 96; V_NBF = V_NPOST + 96
V_SCW = V_NBF + 2; V_LCW = V_SCW + 24; V_LCB = V_LCW + 32; V_LBA = V_LCB + 8; V_LBX = V_LBA + 8
V_LAM = V_LBX + 8; V_C = V_LAM + 8; V_ID = V_C + 8; V_MASK = V_ID + 16; V_I128 = V_MASK + 128; NV = V_I128 + 128


class Sched:
    ENG = ("pe", "act", "dve", "pool", "sp")

    def __init__(self, dry=False):
        self.dry = dry
        self.ops = {e: [] for e in self.ENG}
        self.lastw = {}; self.rd = {}; self.rd_dma = {}
        self.dma_cnt = {}

    def op(self, eng, fn, reads=(), writes=(), dma=None):
        if self.dry:
            return None
        deps = set()
        for r in reads:
            w = self.lastw.get(r)
            if w is not None: deps.add(w)
        for r in writes:
            w = self.lastw.get(r)
            if w is not None: deps.add(w)
            for e, i in self.rd.get(r, {}).items(): deps.add((e, i))
            for d in self.rd_dma.get(r, ()): deps.add(d)
        idx = len(self.ops[eng])
        node = {"fn": fn, "deps": deps, "dma": dma, "sig": False, "val": None}
        if dma is not None:
            c = self.dma_cnt.get(dma, 0) + 1; self.dma_cnt[dma] = c; node["val"] = 16 * c
        self.ops[eng].append(node)
        me = (eng, idx)
        for r in reads:
            if dma is not None: self.rd_dma.setdefault(r, []).append(me)
            else: self.rd.setdefault(r, {})[eng] = idx
        for r in writes:
            self.lastw[r] = me; self.rd[r] = {}; self.rd_dma[r] = []
        return me

    def emit(self, nc, stack):
        ops = self.ops
        NEAR = 2
        def near(eng, idx, e, i):
            return e == eng and eng in ("act", "dve") and idx - i <= NEAR
        for eng in self.ENG:
            for idx, node in enumerate(ops[eng]):
                for (e, i) in node["deps"]:
                    d = ops[e][i]
                    if d["dma"] is None and (e != eng or near(eng, idx, e, i)): d["sig"] = True
        esem = {e: stack.enter_context(nc.semaphore("s_" + e)) for e in self.ENG}
        dsem = {}
        for k in self.dma_cnt:
            dsem[k] = stack.enter_context(nc.semaphore("d%d" % len(dsem)))
        for eng in self.ENG:
            c = 0
            for node in ops[eng]:
                if node["dma"] is None and node["sig"]:
                    c += 1; node["val"] = c
        final_waits = [(dsem[k], 16 * c) for k, c in self.dma_cnt.items() if isinstance(k, tuple) and k[0] == "out"]

        def run(eng, e):
            known = {}
            for idx, node in enumerate(ops[eng]):
                waits = {}
                for (de, di) in node["deps"]:
                    d = ops[de][di]
                    if d["dma"] is not None: sem, val = dsem[d["dma"]], d["val"]
                    elif de == eng and not near(eng, idx, de, di): continue
                    else: sem, val = esem[de], d["val"]
                    key = id(sem)
                    if known.get(key, 0) >= val: continue
                    if key not in waits or waits[key][1] < val: waits[key] = (sem, val)
                for key, (sem, val) in waits.items():
                    e.wait_ge(sem, val); known[key] = val
                ins = node["fn"](e)
                if node["dma"] is not None: ins.then_inc(dsem[node["dma"]], 16)
                elif node["sig"]: ins.then_inc(esem[eng], 1)
            if eng == "sp":
                for sem, val in final_waits: e.wait_ge(sem, val)

        block = stack.enter_context(nc.Block())
        block.tensor(lambda e: run("pe", e))
        block.scalar(lambda e: run("act", e))
        block.vector(lambda e: run("dve", e))
        block.gpsimd(lambda e: run("pool", e))
        block.sync(lambda e: run("sp", e))


class WRing:
    def __init__(self, S, name, views, fetch, seq, rec, hold=1):
        self.S, self.name, self.views, self.fetch, self.seq, self.rec = S, name, views, fetch, seq, rec
        self.pos = 0; self.issued = 0; self.hold = hold

    def get(self, key):
        if self.S.dry:
            self.rec.append(key)
            return self.views[0], (self.name, 0)
        assert self.seq[self.pos] == key, (self.seq[self.pos], key)
        ns = len(self.views)
        while self.issued < min(len(self.seq), self.pos + ns - (self.hold - 1)):
            k = self.issued; sl = k % ns
            src = self.fetch(self.seq[k]); dst = self.views[sl]
            self.S.op("pool", lambda e, dst=dst, src=src: e.dma_start(out=dst, in_=src),
                      writes=[(self.name, sl)], dma=(self.name, sl))
            self.issued += 1
        sl = self.pos % ns; self.pos += 1
        return self.views[sl], (self.name, sl)


def mmg(out, pairs):
    def fn(e):
        n = len(pairs); ins = None
        for k, (l, r) in enumerate(pairs):
            ins = e.matmul(out, lhsT=l, rhs=r, start=(k == 0), stop=(k == n - 1))
        return ins
    return fn


def build_program(n_sub=12, debug=False):
    import contextlib
    dram = {}
    row8 = {}; row22 = {}
    nc = bass.Bass("TRN2", target_bir_lowering=False)
    xin = nc.dram_tensor("xin", [128, 8, T], F32, kind="ExternalInput").ap()
    vecs_d = nc.dram_tensor("vecs", [128, NV], F32, kind="ExternalInput").ap()
    wf_d = nc.dram_tensor("wf", [2, 128, 128], F32, kind="ExternalInput").ap()
    sel_d = nc.dram_tensor("sel", [80, 2048], F32, kind="ExternalInput").ap()
    outT = nc.dram_tensor("outT", [128, 8, T], F32, kind="ExternalOutput").ap()

    stack = contextlib.ExitStack()
    with stack:
        def sb(name, shape, dt):
            return stack.enter_context(nc.sbuf_tensor(name, shape, dt))
        xT = sb("xT", [128, 8, T], F32)
        arena = sb("arena", [128, 24064], F32)
        w8buf = sb("w8buf", [128, NS8 * 1024], BF16)
        w22buf = sb("w22buf", [128, NS22 * DFF], BF16)
        sqb = sb("sqb", [128, 2048], F32)
        rsb = sb("rsb", [128, 512], F32)
        tmpb = [sb("tmp%d" % k, [128, 512], F32) for k in range(2)]
        sgb = [sb("sg%d" % k, [128, 512], F32) for k in range(2)]
        vecs = sb("vecs_sb", [128, NV], F32)
        modsb2 = [sb("modsb%d" % k, [128, 72], F32) for k in range(2)]
        Asb2 = [sb("Asb%d" % k, [128, 24], F32) for k in range(2)]
        GGsb2 = [sb("GGsb%d" % k, [128, 24], F32) for k in range(2)]
        cactb = sb("cactb", [128, 8], BF16)
        onesd = sb("onesd", [128, 128], BF16)
        cst = sb("cst", [128, 4], F32)
        wfb = sb("wfb", [128, 128], BF16)
        sp8 = sb("sp8", [128, 8], F32)
        hl = sb("hl", [128, 1], F32)
        nbf = sb("nbf", [128, 2], F32)
        idb = sb("idb", [128, 16], BF16)
        identb = sb("identb", [128, 128], BF16)
        psA = [stack.enter_context(nc.psum_tensor("psA%d" % k, [128, 512], F32)) for k in range(2)]
        psB = [stack.enter_context(nc.psum_tensor("psB%d" % k, [128, 512], F32)) for k in range(2)]
        psY = [stack.enter_context(nc.psum_tensor("psY%d" % k, [128, 512], F32)) for k in range(2)]
        psS = stack.enter_context(nc.psum_tensor("psS", [128, 512], F32))
        psM = stack.enter_context(nc.psum_tensor("psM", [128, 512], F32))

        def abf(off_b, n_el):
            return arena[:, off_b // 4: off_b // 4 + n_el // 2].bitcast(BF16)
        def af32(off_b, n_el):
            return arena[:, off_b // 4: off_b // 4 + n_el]
        hTh = abf(0, 8 * 1024).rearrange("p (c t) -> p c t", t=1024)
        actT = abf(16384, 22 * 1024).rearrange("p (c t) -> p c t", t=1024)
        yTf = af32(61440, 8 * 1024).rearrange("p (c t) -> p c t", t=1024)
        hT = abf(0, 8 * T).rearrange("p (c t) -> p c t", t=T)
        yTm = af32(0, 8 * 1024).rearrange("p (c t) -> p c t", t=1024)
        ymix = abf(32768, 8 * T).rearrange("p (c t) -> p c t", t=T)
        WK = 65536
        sq = sqb[:, :].bitcast(BF16).rearrange("p (c t) -> p c t", t=512)

        w8views = [w8buf[:, k * 1024:(k + 1) * 1024] for k in range(NS8)]
        w22views = [w22buf[:, k * DFF:(k + 1) * DFF] for k in range(NS22)]

        def norm8(key):
            return key[:3] if key[0] == "mixout" else key
        def fetch8(key):
            return dram["w8"][row8[norm8(key)]]
        def fetch22(key):
            return dram["w22"][row22[key]]

        def v3(view, kc=8):
            return view.rearrange("p (k c) -> p k c", c=128)

        H_ALL = [("h", b) for b in range(4)]

        def gen(S, W8, W22):
            state = {"phase_first": False, "bctr": 0, "par": 0}

            def AR(reads=(), writes=()):
                reads = list(reads); writes = list(writes)
                if state["phase_first"]:
                    writes.append("arena"); state["phase_first"] = False
                else:
                    reads.append("arena")
                return dict(reads=reads, writes=writes)

            def nb():
                state["bctr"] += 1
                return state["bctr"] % 2

            S.op("sp", lambda e: e.dma_start(out=vecs[:], in_=vecs_d), writes=["vecs"], dma=("vl", 0))
            for b in range(4):
                S.op("sp", lambda e, b=b: e.dma_start(out=xT[:, :, b * 512:(b + 1) * 512], in_=xin[:, :, b * 512:(b + 1) * 512]),
                     writes=[("x", b)], dma=("xl", b))
            S.op("dve", lambda e: e.memset(onesd[:], 1.0 / 1024.0), writes=["onesd"])
            S.op("dve", lambda e: e.memset(cst[:, 0:1], EPS), writes=["cst"])
            S.op("dve", lambda e: e.memset(cst[:, 1:2], 1.0), writes=["cst"])
            S.op("dve", lambda e: e.memset(cst[:, 2:3], 0.0), writes=["cst"])
            S.op("act", lambda e: e.activation(out=cactb[:], in_=vecs[:, V_C:V_C + 8], func=AF.Silu),
                 reads=["vecs"], writes=["cact"])
            S.op("dve", lambda e: e.tensor_copy(out=identb[:], in_=vecs[:, V_I128:V_I128 + 128]), reads=["vecs"], writes=["identb"])
            S.op("dve", lambda e: e.tensor_copy(out=idb[0:80, :], in_=vecs[0:80, V_ID:V_ID + 16]), reads=["vecs"], writes=["idb"])
            S.op("dve", lambda e: e.tensor_scalar(out=nbf[0:16, :], in0=vecs[0:16, V_NBF:V_NBF + 2], scalar1=-1.0, scalar2=None, op0=ALU.mult),
                 reads=["vecs"], writes=["nbf"])

            def Acol(s, c): return Asb2[state["par"]][:, s * 8 + c: s * 8 + c + 1]
            def SHcol(s, c): return modsb2[state["par"]][:, s * 24 + c: s * 24 + c + 1]
            def GGcol(s, c): return GGsb2[state["par"]][:, s * 8 + c: s * 8 + c + 1]
            def MODA(): return ("modA", state["par"])

            def mod_step(i, j):
                wv, wr = W8.get(("cond", i, j))
                w3 = v3(wv)
                S.op("pe", mmg(psM[:, j:j + 1], [(w3[:, kc, :], cactb[:, kc:kc + 1]) for kc in range(8)]),
                     reads=[wr, "cact"], writes=["psM"])

            def mod_finish(i):
                pp = i % 2; modsb = modsb2[pp]; Asb = Asb2[pp]; GGsb = GGsb2[pp]; res = ("modA", pp)
                S.op("dve", lambda e: e.tensor_tensor(out=modsb[:], in0=psM[:, 0:72], in1=vecs[:, V_BCOND + i * 72: V_BCOND + (i + 1) * 72], op=ALU.add),
                     reads=["psM", "vecs"], writes=[res])
                for s in range(3):
                    wsub = 1.0 if s == 1 else 0.5
                    o = (i * 3 + s) * 8
                    S.op("dve", lambda e, s=s, o=o: e.scalar_tensor_tensor(
                        out=Asb[:, s * 8:(s + 1) * 8], in0=modsb[:, s * 24 + 8: s * 24 + 16], scalar=1.0,
                        in1=vecs[:, V_NPRE + o: V_NPRE + o + 8], op0=ALU.add, op1=ALU.mult),
                        reads=["vecs"], writes=[res])
                    S.op("dve", lambda e, s=s, o=o, wsub=wsub: e.scalar_tensor_tensor(
                        out=GGsb[:, s * 8:(s + 1) * 8], in0=modsb[:, s * 24 + 16: s * 24 + 24], scalar=wsub,
                        in1=vecs[:, V_NPOST + o: V_NPOST + o + 8], op0=ALU.mult, op1=ALU.mult),
                        reads=["vecs"], writes=[res])

            def mod_phase(i):
                for j in range(72): mod_step(i, j)
                mod_finish(i)

            def stat_mm():
                S.op("pe", mmg(psS[:], [(onesd[:], sq[:, c, :]) for c in range(8)]), reads=["sq", "onesd"], writes=["psS"])

            def stat_sqrt():
                S.op("act", lambda e: e.activation(out=rsb[:], in_=psS[:], func=AF.Sqrt, bias=cst[:, 0:1], scale=1.0),
                     reads=["psS", "cst"], writes=["rs"])

            def stat_recip():
                S.op("dve", lambda e: e.reciprocal(out=rsb[:], in_=rsb[:]), reads=["rs"], writes=["rs"])

            def prenorm(s, t0, ntok, dst, hres):
                nblk = ntok // 512

                def square(bi):
                    tb = t0 + bi * 512
                    S.op("act", lambda e, tb=tb: e.activation(out=sq, in_=xT[:, :, tb:tb + 512], func=AF.Square),
                         reads=[("x", tb // 512)], writes=["sq"])
                    stat_mm()

                square(0); stat_sqrt()
                for bi in range(nblk):
                    tb = t0 + bi * 512; xb = tb // 512
                    if bi + 1 < nblk: square(bi + 1)
                    stat_recip()
                    for c in range(8):
                        k = c % 2
                        S.op("dve", lambda e, c=c, k=k, tb=tb, ac=Acol(s, c): e.scalar_tensor_tensor(
                            out=tmpb[k][:], in0=xT[:, c, tb:tb + 512], scalar=ac, in1=rsb[:], op0=ALU.mult, op1=ALU.mult),
                            reads=[("x", xb), "rs", MODA()], writes=[("tmp", k)])
                        S.op("act", lambda e, c=c, k=k, bi=bi, sh=SHcol(s, c): e.activation(
                            out=dst[:, c, bi * 512:(bi + 1) * 512], in_=tmpb[k][:], func=AF.Identity, bias=sh, scale=1.0),
                            **AR(reads=[("tmp", k), MODA()], writes=[hres(bi)]))
                    if bi + 1 < nblk: stat_sqrt()

            def postnorm(s, t0, ysrc):
                def square(tt):
                    S.op("act", lambda e, tt=tt: e.activation(out=sq, in_=ysrc[:, :, tt * 512:(tt + 1) * 512], func=AF.Square),
                         **AR(reads=[("y", tt)], writes=["sq"]))
                    stat_mm()

                square(0); stat_sqrt()
                for tt in range(2):
                    tb = t0 + tt * 512; xb = tb // 512
                    if tt == 0: square(1)
                    stat_recip()
                    for m in range(8):
                        k = m % 2
                        S.op("dve", lambda e, m=m, k=k, tt=tt, gc=GGcol(s, m): e.scalar_tensor_tensor(
                            out=tmpb[k][:], in0=ysrc[:, m, tt * 512:(tt + 1) * 512], scalar=gc, in1=rsb[:], op0=ALU.mult, op1=ALU.mult),
                            **AR(reads=[("y", tt), "rs", MODA()], writes=[("tmp", k)]))
                        S.op("dve", lambda e, m=m, k=k, tb=tb: e.tensor_tensor(
                            out=xT[:, m, tb:tb + 512], in0=xT[:, m, tb:tb + 512], in1=tmpb[k][:], op=ALU.add),
                            reads=[("tmp", k), ("x", xb)], writes=[("x", xb)])
                    if tt == 0: stat_sqrt()

            def ffn(i, f, s):
                state["phase_first"] = True

                def inproj(lo, hi):
                    for n in range(lo, hi):
                        wg, rg = W8.get(("win", i, f, n)); wu, ru = W8.get(("win", i, f, 22 + n))
                        wg3, wu3 = v3(wg), v3(wu)
                        for tt in range(2):
                            b = nb(); ts = slice(tt * 512, (tt + 1) * 512)
                            S.op("pe", mmg(psA[b][:], [(wg3[:, kc, :], hTh[:, kc, ts]) for kc in range(8)]),
                                 **AR(reads=[rg, ("ah", tt)], writes=[("psA", b)]))
                            S.op("pe", mmg(psB[b][:], [(wu3[:, kc, :], hTh[:, kc, ts]) for kc in range(8)]),
                                 **AR(reads=[ru, ("ah", tt)], writes=[("psB", b)]))
                            S.op("act", lambda e, b=b: e.activation(out=sgb[b][:], in_=psA[b][:], func=AF.Silu),
                                 reads=[("psA", b)], writes=[("sg", b)])
                            S.op("dve", lambda e, b=b, n=n, ts=ts: e.tensor_tensor(out=actT[:, n, ts], in0=sgb[b][:], in1=psB[b][:], op=ALU.mult),
                                 **AR(reads=[("sg", b), ("psB", b)], writes=[("act", tt)]))

                def outproj():
                    for m in range(8):
                        wo, ro = W22.get(("wout", i, f, m)); wo3 = v3(wo)
                        for tt in range(2):
                            b = nb(); ts = slice(tt * 512, (tt + 1) * 512)
                            S.op("pe", mmg(psY[b][:], [(wo3[:, kc, :], actT[:, kc, ts]) for kc in range(22)]),
                                 **AR(reads=[ro, ("act", tt)], writes=[("psY", b)]))
                            S.op("act", lambda e, b=b, m=m, ts=ts: e.activation(out=yTf[:, m, ts], in_=psY[b][:], func=AF.Copy),
                                 **AR(reads=[("psY", b)], writes=[("y", tt)]))

                ah = lambda bi: ("ah", bi)
                prenorm(s, 0, 1024, hTh, ah)
                inproj(0, 22)
                prenorm(s, 1024, 1024, hTh, ah)
                outproj()
                inproj(0, 3)
                postnorm(s, 0, yTf)
                inproj(3, 22)
                outproj()
                postnorm(s, 1024, yTf)

            def mix_out(i):
                for half in range(2):
                    for m in range(8):
                        wo, ro = W8.get(("mixout", i, m, half)); wo3 = v3(wo)
                        for tt in range(2):
                            b = nb(); tok = half * 1024 + tt * 512
                            S.op("pe", mmg(psY[b][:], [(wo3[:, kc, :], ymix[:, kc, tok:tok + 512]) for kc in range(8)]),
                                 **AR(reads=[ro, ("ymix", tok // 512)], writes=[("psY", b)]))
                            S.op("act", lambda e, b=b, m=m, tt=tt: e.activation(out=yTm[:, m, tt * 512:(tt + 1) * 512], in_=psY[b][:], func=AF.Copy),
                                 **AR(reads=[("psY", b)], writes=[("y", tt)] + H_ALL))
                    postnorm(1, half * 1024, yTm)

            def proj512(w3, tt, ps, wr, pres):
                S.op("pe", mmg(ps[:], [(w3[:, kc, :], hT[:, kc, tt * 512:(tt + 1) * 512]) for kc in range(8)]),
                     **AR(reads=[wr, ("h", tt)], writes=[pres]))

            def sconv(i, between=None):
                state["phase_first"] = True
                prenorm(1, 0, T, hT, lambda bi: ("h", bi))
                cx = af32(WK, 2064)[:, 0:2050]
                cv = af32(WK + 8256, 2048)
                S.op("dve", lambda e: e.memset(cx[:, 0:2], 0.0), **AR(writes=["cx"]))
                for m in range(8):
                    wB, rB = W8.get(("mixin", i, m)); wC, rC = W8.get(("mixin", i, 8 + m)); wX, rX = W8.get(("mixin", i, 16 + m))
                    for tt in range(4):
                        b = nb()
                        proj512(v3(wC), tt, psA[b], rC, ("psA", b))
                        proj512(v3(wX), tt, psB[b], rX, ("psB", b))
                        S.op("act", lambda e, b=b: e.activation(out=sgb[b][:], in_=psA[b][:], func=AF.Copy),
                             reads=[("psA", b)], writes=[("sg", b)])
                        S.op("dve", lambda e, b=b, tt=tt: e.tensor_tensor(out=cx[:, 2 + tt * 512: 2 + (tt + 1) * 512], in0=sgb[b][:], in1=psB[b][:], op=ALU.mult),
                             **AR(reads=[("sg", b), ("psB", b)], writes=["cx"]))
                    wc = lambda k, m=m: vecs[:, V_SCW + m * 3 + k: V_SCW + m * 3 + k + 1]
                    S.op("dve", lambda e, wc=wc: e.tensor_scalar(out=cv[:], in0=cx[:, 2:2050], scalar1=wc(2), scalar2=None, op0=ALU.mult),
                         **AR(reads=["cx", "vecs"], writes=["cv"]))
                    for k in (1, 0):
                        S.op("dve", lambda e, wc=wc, k=k: e.scalar_tensor_tensor(out=cv[:], in0=cx[:, k:k + 2048], scalar=wc(k), in1=cv[:], op0=ALU.mult, op1=ALU.add),
                             **AR(reads=["cx", "vecs"], writes=["cv"]))
                    for tt in range(4):
                        b = nb()
                        proj512(v3(wB), tt, psA[b], rB, ("psA", b))
                        S.op("dve", lambda e, b=b, tt=tt, m=m: e.tensor_tensor(out=ymix[:, m, tt * 512:(tt + 1) * 512], in0=psA[b][:], in1=cv[:, tt * 512:(tt + 1) * 512], op=ALU.mult),
                             **AR(reads=[("psA", b), "cv"], writes=[("ymix", tt)]))
                    if between is not None: between(m)
                mix_out(i)

            def lru(i, between=None):
                state["phase_first"] = True
                prenorm(1, 0, T, hT, lambda bi: ("h", bi))
                if S.dry: extra8.extend([("lrubd", 0), ("lrubd", 1)])
                bd = sqb[:, 0:1024].bitcast(BF16).rearrange("p (w m c) -> p w m c", w=2, m=8)
                bdf = sqb[:, 0:1024].bitcast(BF16)
                for w in range(2):
                    S.op("pool", lambda e, w=w: e.dma_start(out=bdf[:, w * 1024:(w + 1) * 1024], in_=dram["w8"][row8[("lrubd", w)]]),
                         writes=["sq"], dma=("bd", w))
                S.op("act", lambda e: e.activation(out=sp8[:], in_=vecs[:, V_LAM:V_LAM + 8], func=AF.Exp, scale=-1.0), reads=["vecs"], writes=["sp8"])
                S.op("act", lambda e: e.activation(out=sp8[:], in_=sp8[:], func=AF.Ln, bias=cst[:, 1:2], scale=1.0), reads=["cst"], writes=["sp8"])
                S.op("dve", lambda e: e.tensor_scalar(out=sp8[:], in0=sp8[:], scalar1=-8.0, scalar2=None, op0=ALU.mult), reads=["sp8"], writes=["sp8"])
                gl = af32(WK, 1024); xraw = af32(WK + 4096, 1028); xbb = af32(WK + 8208, 1024)
                xbf = abf(WK + 12304, 1024); ab = af32(WK + 14352, 1024); ig = af32(WK + 18448, 1024); tp = af32(WK + 22544, 1024)
                for m in range(8):
                    wG, rG = W8.get(("mixin", i, m)); wXb, rXb = W8.get(("mixin", i, 8 + m))
                    cwc = lambda k, m=m: vecs[:, V_LCW + m * 4 + k: V_LCW + m * 4 + k + 1]
                    for seg in range(2):
                        if seg == 0:
                            S.op("dve", lambda e: e.memset(xraw[:, 0:3], 0.0), **AR(writes=["xraw"]))
                        else:
                            S.op("dve", lambda e: e.tensor_copy(out=xraw[:, 0:3], in_=xraw[:, 1024:1027]), **AR(reads=["xraw"], writes=["xraw"]))
                        for tt in range(2):
                            b = nb(); blk = seg * 2 + tt; ts = slice(tt * 512, (tt + 1) * 512)
                            proj512(v3(wG), blk, psA[b], rG, ("psA", b))
                            S.op("act", lambda e, b=b: e.activation(out=sgb[b][:], in_=psA[b][:], func=AF.Square), reads=[("psA", b)], writes=[("sg", b)])
                            S.op("dve", lambda e, b=b: e.tensor_scalar(out=sgb[b][:], in0=sgb[b][:], scalar1=0.044715, scalar2=1.0, op0=ALU.mult, op1=ALU.add),
                                 reads=[("sg", b)], writes=[("sg", b)])
                            S.op("dve", lambda e, b=b: e.tensor_tensor(out=sgb[b][:], in0=sgb[b][:], in1=psA[b][:], op=ALU.mult),
                                 reads=[("sg", b), ("psA", b)], writes=[("sg", b)])
                            S.op("act", lambda e, b=b: e.activation(out=sgb[b][:], in_=sgb[b][:], func=AF.Sigmoid, scale=1.5957691216057308),
                                 reads=[("sg", b)], writes=[("sg", b)])
                            S.op("dve", lambda e, b=b, ts=ts: e.tensor_tensor(out=gl[:, ts], in0=sgb[b][:], in1=psA[b][:], op=ALU.mult),
                                 **AR(reads=[("sg", b), ("psA", b)], writes=["gl"]))
                            proj512(v3(wXb), blk, psB[b], rXb, ("psB", b))
                            S.op("act", lambda e, b=b, tt=tt: e.activation(out=xraw[:, 3 + tt * 512: 3 + (tt + 1) * 512], in_=psB[b][:], func=AF.Copy),
                                 **AR(reads=[("psB", b)], writes=["xraw"]))
                        S.op("dve", lambda e, cwc=cwc, m=m: e.tensor_scalar(out=xbb[:], in0=xraw[:, 3:1027], scalar1=cwc(3), scalar2=vecs[:, V_LCB + m: V_LCB + m + 1], op0=ALU.mult, op1=ALU.add),
                             **AR(reads=["xraw", "vecs"], writes=["xb"]))
                        for k in range(3):
                            S.op("dve", lambda e, cwc=cwc, k=k: e.scalar_tensor_tensor(out=xbb[:], in0=xraw[:, k:k + 1024], scalar=cwc(k), in1=xbb[:], op0=ALU.mult, op1=ALU.add),
                                 **AR(reads=["xraw", "vecs"], writes=["xb"]))
                        S.op("act", lambda e: e.activation(out=xbf[:], in_=xbb[:], func=AF.Copy), **AR(reads=["xb"], writes=["xbf"]))
                        for tt in range(2):
                            b = nb(); ts = slice(tt * 512, (tt + 1) * 512)
                            S.op("pe", mmg(psA[b][:], [(bd[:, 0, m, :], xbf[:, ts])]), **AR(reads=["sq", "xbf"], writes=[("psA", b)]))
                            S.op("act", lambda e, b=b, m=m: e.activation(out=sgb[b][:], in_=psA[b][:], func=AF.Sigmoid, bias=vecs[:, V_LBA + m: V_LBA + m + 1], scale=1.0),
                                 reads=[("psA", b), "vecs"], writes=[("sg", b)])
                            S.op("dve", lambda e, b=b, m=m: e.tensor_scalar(out=sgb[b][:], in0=sgb[b][:], scalar1=sp8[:, m:m + 1], scalar2=None, op0=ALU.mult),
                                 reads=[("sg", b), "sp8"], writes=[("sg", b)])
                            S.op("act", lambda e, b=b, m=m, ts=ts: e.activation(out=ab[:, ts], in_=sgb[b][:], func=AF.Exp),
                                 **AR(reads=[("sg", b)], writes=["ab"]))
                            S.op("pe", mmg(psB[b][:], [(bd[:, 1, m, :], xbf[:, ts])]), **AR(reads=["sq", "xbf"], writes=[("psB", b)]))
                            S.op("act", lambda e, b=b, m=m, ts=ts: e.activation(out=ig[:, ts], in_=psB[b][:], func=AF.Sigmoid, bias=vecs[:, V_LBX + m: V_LBX + m + 1], scale=1.0),
                                 **AR(reads=[("psB", b), "vecs"], writes=["ig"]))
                        S.op("dve", lambda e: e.tensor_tensor(out=tp[:], in0=ab[:], in1=ab[:], op=ALU.mult), **AR(reads=["ab"], writes=["tp"]))
                        S.op("dve", lambda e: e.tensor_scalar(out=tp[:], in0=tp[:], scalar1=-1.0, scalar2=1.0, op0=ALU.mult, op1=ALU.add), **AR(writes=["tp"]))
                        S.op("act", lambda e: e.activation(out=tp[:], in_=tp[:], func=AF.Sqrt), **AR(reads=["tp"], writes=["tp"]))
                        S.op("dve", lambda e: e.tensor_tensor(out=ig[:], in0=ig[:], in1=xbb[:], op=ALU.mult), **AR(reads=["ig", "xb"], writes=["ig"]))
                        S.op("dve", lambda e: e.tensor_tensor(out=ig[:], in0=ig[:], in1=tp[:], op=ALU.mult), **AR(reads=["tp"], writes=["ig"]))
                        init = 0.0 if seg == 0 else hl[:, 0:1]
                        S.op("dve", lambda e, init=init: e.tensor_tensor_scan(out=tp[:], data0=ab[:], data1=ig[:], initial=init, op0=ALU.mult, op1=ALU.add),
                             **AR(reads=["ab", "ig", "hl"], writes=["tp"]))
                        S.op("act", lambda e: e.activation(out=hl[:, 0:1], in_=tp[:, 1023:1024], func=AF.Copy), **AR(reads=["tp"], writes=["hl"]))
                        S.op("dve", lambda e, m=m, seg=seg: e.tensor_tensor(out=ymix[:, m, seg * 1024:(seg + 1) * 1024], in0=tp[:], in1=gl[:], op=ALU.mult),
                             **AR(reads=["tp", "gl"], writes=[("ymix", seg * 2), ("ymix", seg * 2 + 1)]))
                    if between is not None: between(m)
                mix_out(i)

            def fox(i, between=None):
                j = MIXJ[i]
                state["phase_first"] = True
                prenorm(1, 0, T, hT, lambda bi: ("h", bi))
                selv = sqb[0:80, 0:1024].bitcast(BF16)
                S.op("pool", lambda e: e.dma_start(out=selv, in_=sel_d), writes=["sq"], dma=("sel", 0))
                S.op("pool", lambda e: e.dma_start(out=wfb[:], in_=wf_d[j]), writes=["wfb"], dma=("wfb", 0))
                wfb3 = wfb[:, :].rearrange("p (k c) -> p k c", c=16)
                cum8 = af32(WK, 2048)[0:16, :]
                ncum = af32(WK + 8192, 256)
                QT = abf(WK + 9216, 2048); KT = abf(WK + 13312, 2048)
                Vx = abf(WK + 17408, 4096).rearrange("p (t h c) -> p t h c", t=16, h=2)
                Lb = af32(WK + 9216, 2048)[0:16, :]; Zb = af32(WK + 17408, 2048)[0:16, :]
                S.op("dve", lambda e: e.memset(Zb, 0.0), **AR(writes=["Vx"]))
                for tt in range(4):
                    ts = slice(tt * 512, (tt + 1) * 512)
                    S.op("pe", mmg(psM[0:16, :], [(wfb3[:, kc, :], hT[:, kc, ts]) for kc in range(8)]), **AR(reads=["wfb", ("h", tt)], writes=["psM"]))
                    S.op("act", lambda e, ts=ts: e.activation(out=Lb[:, ts], in_=psM[0:16, :], func=AF.Exp, bias=nbf[0:16, j:j + 1], scale=-1.0),
                         **AR(reads=["psM", "nbf"], writes=["QT", "KT"]))
                    S.op("act", lambda e, ts=ts: e.activation(out=Lb[:, ts], in_=Lb[:, ts], func=AF.Ln, bias=cst[0:16, 1:2], scale=1.0),
                         **AR(reads=["cst"], writes=["QT", "KT"]))
                S.op("dve", lambda e: e.tensor_scalar(out=Lb, in0=Lb, scalar1=-8.0, scalar2=None, op0=ALU.mult), **AR(reads=["QT", "KT"], writes=["QT", "KT"]))
                S.op("dve", lambda e: e.tensor_tensor_scan(out=cum8, data0=Lb, data1=Zb, initial=0.0, op0=ALU.add, op1=ALU.add),
                     **AR(reads=["QT", "KT", "Vx"], writes=["cum8"]))
                C80f = abf(WK + 25600, 2048)
                C80 = C80f[0:80, :]; hiT = C80f[0:16, :]; midT = C80f[32:48, :]; loT = C80f[64:80, :]
                S.op("dve", lambda e: e.memset(C80, 0.0), **AR(writes=["c80"]))
                S.op("dve", lambda e: e.tensor_copy(out=hiT, in_=cum8), **AR(reads=["cum8"], writes=["c80"]))
                S.op("dve", lambda e: e.tensor_tensor(out=Lb, in0=cum8, in1=hiT, op=ALU.subtract), **AR(reads=["cum8", "c80"], writes=["QT", "KT"]))
                S.op("dve", lambda e: e.tensor_copy(out=midT, in_=Lb), **AR(reads=["QT", "KT"], writes=["c80"]))
                S.op("dve", lambda e: e.tensor_copy(out=cum8, in_=midT), **AR(reads=["c80"], writes=["cum8"]))
                S.op("dve", lambda e: e.tensor_tensor(out=Lb, in0=Lb, in1=cum8, op=ALU.subtract), **AR(reads=["cum8"], writes=["QT", "KT"]))
                S.op("dve", lambda e: e.tensor_copy(out=loT, in_=Lb), **AR(reads=["QT", "KT"], writes=["c80"]))
                for tile in range(16):
                    S.op("pe", mmg(psM[:, tile * 16:(tile + 1) * 16], [(C80[:, tile * 128:(tile + 1) * 128], idb[0:80, :])]),
                         **AR(reads=["c80", "idb"], writes=["psM"]))
                S.op("dve", lambda e: e.tensor_scalar(out=ncum[:], in0=psM[:, 0:256], scalar1=-0.125, scalar2=None, op0=ALU.mult), **AR(reads=["psM"], writes=["ncum"]))
                S.op("dve", lambda e: e.memset(Vx[:, :, :, 64:128], 1.0), **AR(writes=["Vx"]))
                mask = vecs[:, V_MASK:V_MASK + 128]
                PTs = [sgb[k // 2][:, (k % 2) * 256:(k % 2) * 256 + 256].bitcast(BF16) for k in range(4)]
                SB = [psA[0], psA[1], psB[0], psB[1]]; SR = [("psA", 0), ("psA", 1), ("psB", 0), ("psB", 1)]
                for m in range(8):
                    wq, rq = W8.get(("mixin", i, m)); wk, rk = W8.get(("mixin", i, 8 + m)); wv, rv = W8.get(("mixin", i, 16 + m))
                    wv3 = v3(wv)
                    VTb = [tmpb[k][:, 0:256].bitcast(BF16) for k in range(2)]
                    for tt in range(4):
                        b = nb(); ts = slice(tt * 512, (tt + 1) * 512); vk = tt % 2
                        proj512(v3(wq), tt, psA[b], rq, ("psA", b))
                        S.op("act", lambda e, b=b, ts=ts: e.activation(out=QT[:, ts], in_=psA[b][:], func=AF.Copy), **AR(reads=[("psA", b)], writes=["QT"]))
                        proj512(v3(wk), tt, psB[b], rk, ("psB", b))
                        S.op("act", lambda e, b=b, ts=ts: e.activation(out=KT[:, ts], in_=psB[b][:], func=AF.Copy), **AR(reads=[("psB", b)], writes=["KT"]))
                        proj512(wv3, tt, psY[b], rv, ("psY", b))
                        S.op("act", lambda e, b=b, vk=vk: e.activation(out=VTb[vk], in_=psY[b][:], func=AF.Copy), reads=[("psY", b)], writes=[("tmp", vk)])
                        for q in range(4):
                            S.op("pe", mmg(psS[:, q * 128:(q + 1) * 128], [(VTb[vk][:, q * 128:(q + 1) * 128], identb[:])]),
                                 reads=[("tmp", vk), "identb"], writes=["psS"])
                        pv = psS[:, :].rearrange("p (q c) -> p q c", c=128)
                        for hh in range(2):
                            S.op("dve", lambda e, tt=tt, hh=hh, pv=pv: e.tensor_copy(out=Vx[:, tt * 4:(tt + 1) * 4, hh, 0:64], in_=pv[:, :, hh * 64:(hh + 1) * 64]),
                                 **AR(reads=["psS"], writes=["Vx"]))
                    groups = [(hh, c) for hh in range(2) for c in range(4)]
                    tiles = []
                    for gi, (hh, c) in enumerate(groups):
                        nj = 4 * (c + 1)
                        for jt in range(nj):
                            tiles.append((gi, hh, c, jt, max(0, jt * 128 - c * 512), nj))
                    LA = 3; NT = len(tiles)

                    def rec_S(t, m=m):
                        gi, hh, c, jt, n0, nj = tiles[t]; k = t % 4; h = 2 * m + hh; hs = slice(hh * 64, (hh + 1) * 64)
                        S.op("pe", mmg(SB[k][:, n0:512], [(KT[hs, jt * 128:(jt + 1) * 128], QT[hs, c * 512 + n0:(c + 1) * 512]),
                                                          (selv[:, h * 128:(h + 1) * 128], C80[:, c * 512 + n0:(c + 1) * 512])]),
                             **AR(reads=["QT", "KT", "sq", "c80"], writes=[SR[k]]))

                    def rec_rest(t, m=m):
                        gi, hh, c, jt, n0, nj = tiles[t]; k = t % 4; h = 2 * m + hh; hs = slice(hh * 64, (hh + 1) * 64)
                        kcb = gi % 2; yb = gi % 2
                        if jt >= 4 * c:
                            S.op("dve", lambda e, k=k, n0=n0: e.tensor_tensor(out=SB[k][:, n0:n0 + 128], in0=SB[k][:, n0:n0 + 128], in1=mask, op=ALU.add),
                                 reads=[SR[k], "vecs"], writes=[SR[k]])
                        S.op("act", lambda e, k=k, n0=n0, jt=jt, h=h: e.activation(out=PTs[k][:, n0:512], in_=SB[k][:, n0:512], func=AF.Exp,
                                                                                   bias=ncum[:, jt * 16 + h: jt * 16 + h + 1], scale=0.125),
                             **AR(reads=[SR[k], "ncum", ("sg", k // 2)], writes=[("pt", k)]))
                        S.op("pe", (lambda k=k, n0=n0, jt=jt, hh=hh, yb=yb, nj=nj: (lambda e: e.matmul(
                            psY[yb][:, n0:512], lhsT=Vx[:, jt, hh, :], rhs=PTs[k][:, n0:512], start=(jt == 0), stop=(jt == nj - 1))))(),
                             **AR(reads=[("pt", k), ("sg", k // 2), "Vx"], writes=[("psY", yb)]))
                        if jt == nj - 1:
                            S.op("dve", lambda e, yb=yb: e.reciprocal(out=rsb[64:128, :], in_=psY[yb][64:128, :]), reads=[("psY", yb)], writes=["rs"])
                            S.op("dve", lambda e, yb=yb, hs=hs, m=m, c=c: e.tensor_tensor(out=ymix[hs, m, c * 512:(c + 1) * 512], in0=psY[yb][0:64, :], in1=rsb[64:128, :], op=ALU.mult),
                                 **AR(reads=[("psY", yb), "rs"], writes=[("ymix", c)]))

                    for t in range(NT + LA):
                        if t < NT: rec_S(t)
                        if t >= LA: rec_rest(t - LA)
                    if between is not None: between(m)
                mix_out(i)

            nsub = 0
            PREFETCH_MOD = False
            mod_phase(0)
            for i in range(NL):
                if nsub >= n_sub: break
                state["par"] = i % 2
                if i > 0 and not PREFETCH_MOD: mod_phase(i)
                for s in range(3):
                    if nsub >= n_sub: break
                    if s == 0: ffn(i, 0, 0)
                    elif s == 2: ffn(i, 1, 2)
                    else:
                        btw = None
                        if PREFETCH_MOD and i + 1 < NL and nsub + 2 < n_sub:
                            def btw(m, i=i):
                                for j in range(m * 9, (m + 1) * 9): mod_step(i + 1, j)
                                if m == 7: mod_finish(i + 1)
                        (fox, sconv, lru)[KIND[i]](i, btw)
                    nsub += 1
            for b in range(4):
                S.op("sp", lambda e, b=b: e.dma_start(out=outT[:, :, b * 512:(b + 1) * 512], in_=xT[:, :, b * 512:(b + 1) * 512]),
                     reads=[("x", b)], dma=("out", b))
            if getattr(S, "debug", False):
                dg = af32(WK, 4096)
                allres = list(S.lastw.keys())
                S.op("dve", lambda e: e.memset(dg[:], 0.0), reads=allres, writes=["dbg"])
                S.op("dve", lambda e: e.tensor_copy(out=dg[:, 0:72], in_=modsb[:]), writes=["dbg"])
                S.op("dve", lambda e: e.tensor_copy(out=dg[:, 72:96], in_=Asb[:]), writes=["dbg"])
                S.op("dve", lambda e: e.tensor_copy(out=dg[:, 96:120], in_=GGsb[:]), writes=["dbg"])
                S.op("dve", lambda e: e.tensor_copy(out=dg[:, 120:632], in_=rsb[:]), writes=["dbg"])
                S.op("dve", lambda e: e.tensor_copy(out=dg[:, 632:1144], in_=yTf[:, 0, 0:512]), writes=["dbg"])
                S.op("dve", lambda e: e.tensor_copy(out=dg[:, 1144:1656], in_=hTh[:, 0, 0:512]), writes=["dbg"])
                S.op("dve", lambda e: e.tensor_copy(out=dg[:, 1656:2168], in_=actT[:, 0, 0:512]), writes=["dbg"])
                S.op("dve", lambda e: e.tensor_copy(out=dg[:, 2168:2176], in_=cactb[:]), writes=["dbg"])
                S.op("dve", lambda e: e.tensor_copy(out=dg[:, 2176:2688], in_=sq[:, 0, :]), writes=["dbg"])
                S.op("sp", lambda e: e.dma_start(out=dram["dbg"], in_=dg[:]), reads=["dbg"], dma=("out", 9))

        rec8, rec22, extra8 = [], [], []
        Sd = Sched(dry=True)
        gen(Sd, WRing(Sd, "w8", w8views, fetch8, None, rec8, 3), WRing(Sd, "w22", w22views, fetch22, None, rec22, 1))
        for k in [norm8(k) for k in rec8] + extra8:
            if k not in row8: row8[k] = len(row8)
        for k in rec22:
            if k not in row22: row22[k] = len(row22)
        dram["w8"] = nc.dram_tensor("w8all", [max(1, len(row8)), 128, 1024], F32, kind="ExternalInput").ap()
        dram["w22"] = nc.dram_tensor("w22all", [max(1, len(row22)), 128, DFF], F32, kind="ExternalInput").ap()
        if debug:
            dram["dbg"] = nc.dram_tensor("dbg", [128, 4096], F32, kind="ExternalOutput").ap()
        S = Sched()
        S.debug = debug
        gen(S, WRing(S, "w8", w8views, fetch8, rec8, None, 3), WRing(S, "w22", w22views, fetch22, rec22, None, 1))
        with nc.allow_low_precision("bf16 matmul operands, fp32 accumulate"):
            S.emit(nc, stack)
    nc.row8 = row8; nc.row22 = row22
    return nc


def _chunk8(w):
    n = w.shape[1] // 128
    return np.ascontiguousarray(w.reshape(8, 128, n, 128).transpose(2, 1, 0, 3)).reshape(n, 128, 1024)


def prep_inputs(inp, row8, row22):
    f32 = np.float32
    g = {k: np.asarray(v, dtype=f32) for k, v in inp.items()}
    w8 = np.zeros((max(1, len(row8)), 128, 1024), f32)
    src = {}
    for i in range(NL):
        src[("cond", i)] = _chunk8(g["w_cond"][i])
        for f in range(2):
            src[("win", i, f)] = _chunk8(g["w_ffn_in"][i, f])
        j = MIXJ[i]
        if KIND[i] == 0:
            src[("mixin", i)] = _chunk8(g["fox_w_in"][j][:, 0:3072]); wo = g["fox_w_out"][j]
        elif KIND[i] == 1:
            src[("mixin", i)] = _chunk8(g["sconv_w_in"][j]); wo = g["sconv_w_out"][j]
        else:
            src[("mixin", i)] = _chunk8(g["lru_w_in"][j]); wo = g["lru_w_out"][j]
        src[("mixout", i)] = _chunk8(wo)
    for w, nm in enumerate(("lru_w_a", "lru_w_x")):
        bd = np.zeros((128, 8, 128), f32)
        for m in range(8):
            for hh in range(2):
                bd[hh * 64:(hh + 1) * 64, m, hh * 64:(hh + 1) * 64] = g[nm][0, 2 * m + hh]
        src[("lrubd", w)] = bd.reshape(128, 1024)
    for key, r in row8.items():
        if key[0] == "lrubd": w8[r] = src[key]
        else: w8[r] = src[key[:-1]][key[-1]]
    w22all = np.ascontiguousarray(g["w_ffn_out"].reshape(NL, 2, 22, 128, 8, 128).transpose(0, 1, 4, 3, 2, 5))
    w22 = np.zeros((max(1, len(row22)), 128, DFF), f32)
    for key, r in row22.items():
        _, i, f, m = key
        w22[r] = w22all[i, f, m].reshape(128, DFF)
    wf = np.ascontiguousarray(g["fox_w_in"][:, :, 3072:3088].reshape(2, 8, 128, 16).transpose(0, 2, 1, 3)).reshape(2, 128, 128)
    sel = np.zeros((80, 16, 128), f32)
    for h in range(16):
        for r in (0, 32, 64): sel[r + h, h, :] = 1.0
    sel = sel.reshape(80, 2048)
    def pc(v):
        return v.reshape(v.shape[:-1] + (8, 128))
    base = np.zeros((128, NV), f32)
    base[:, V_BCOND:V_BCOND + 288] = g["b_cond"].reshape(NL, 72, 128).transpose(2, 0, 1).reshape(128, 288)
    base[:, V_NPRE:V_NPRE + 96] = g["norm_pre"].reshape(NL, 3, 8, 128).transpose(3, 0, 1, 2).reshape(128, 96)
    base[:, V_NPOST:V_NPOST + 96] = g["norm_post"].reshape(NL, 3, 8, 128).transpose(3, 0, 1, 2).reshape(128, 96)
    base[0:16, V_NBF:V_NBF + 2] = g["fox_b_f"].T
    base[:, V_SCW:V_SCW + 24] = g["sconv_conv_w"][0].reshape(3, 8, 128).transpose(2, 1, 0).reshape(128, 24)
    base[:, V_LCW:V_LCW + 32] = g["lru_conv_w"][0].reshape(4, 8, 128).transpose(2, 1, 0).reshape(128, 32)
    base[:, V_LCB:V_LCB + 8] = g["lru_conv_b"][0].reshape(8, 128).T
    base[:, V_LBA:V_LBA + 8] = g["lru_b_a"][0].reshape(8, 128).T
    base[:, V_LBX:V_LBX + 8] = g["lru_b_x"][0].reshape(8, 128).T
    base[:, V_LAM:V_LAM + 8] = g["lru_lambda"][0].reshape(8, 128).T
    for r in (0, 32, 64): base[r:r + 16, V_ID:V_ID + 16] = np.eye(16, dtype=f32)
    base[:, V_I128:V_I128 + 128] = np.eye(128, dtype=f32)
    s_idx = np.arange(128)[:, None]; t_idx = np.arange(128)[None, :]
    base[:, V_MASK:V_MASK + 128] = np.where(s_idx <= t_idx, 0.0, -240000.0).astype(f32)
    in_maps = []
    for b in range(NCORES):
        v = base.copy()
        v[:, V_C:V_C + 8] = g["c"][b].reshape(8, 128).T
        xin = np.ascontiguousarray(g["x"][b].T.reshape(8, 128, T).transpose(1, 0, 2))
        in_maps.append({"xin": xin, "vecs": v, "w8all": w8, "w22all": w22, "wf": wf, "sel": sel})
    return in_maps


def kernel(**inputs):
    nc = build_program()
    in_maps = prep_inputs(inputs, nc.row8, nc.row22)
    res = run_bass_kernel_spmd(nc, in_maps, core_ids=list(range(NCORES)))
    out = np.empty((NCORES, T, D), np.float32)
    for b in range(NCORES):
        o = np.asarray(res.results[b]["outT"])
        out[b] = o.transpose(2, 1, 0).reshape(T, D)
    return out
```
